# Optimizing a Trainium2 kernel written in Bass

```python
import jax
import jax.numpy as jnp
from jax import lax
import numpy as np

D_MODEL = 1024
BATCH = 2
SEQ = 8192
DEPTH = 2

N_META = 16
NORM_EPS = 1e-6
SSM_D_INNER = 2 * D_MODEL
SSM_HEAD_DIM = 64
SSM_HEADS = SSM_D_INNER // SSM_HEAD_DIM
SSM_GROUPS = 4
SSM_HEADS_PER_GROUP = SSM_HEADS // SSM_GROUPS
SSM_STATE = 128
SSM_CONV = 4
SSM_CHUNK = 256
SSM_CONV_DIM = SSM_D_INNER + 2 * SSM_GROUPS * SSM_STATE
SSM_IN_DIM = SSM_D_INNER + SSM_CONV_DIM + SSM_HEADS
SB_HEAD_DIM = 64
SB_HEADS = D_MODEL // SB_HEAD_DIM
SB_WIDTH = SB_HEADS * SB_HEAD_DIM
SB_Q_BLOCK = 128
D_FF = 256 * ((8 * D_MODEL // 3 + 255) // 256)
FFN_CONV = 3

kernel_name = 'hybrid_ssd_stickbreaking_yoco'


def _rmsnorm(x, g):
    x32 = x.astype(jnp.float32)
    y = x32 * lax.rsqrt(jnp.mean(x32 * x32, axis=-1, keepdims=True) + NORM_EPS)
    return (y * g.astype(jnp.float32)).astype(x.dtype)


def _causal_dwconv(x, w, bias):
    width = w.shape[0]
    L = x.shape[1]
    xp = jnp.pad(x, ((0, 0), (width - 1, 0), (0, 0)))
    y = xp[:, 0:L] * w[0] + bias
    for k in range(1, width):
        y = y + xp[:, k:k + L] * w[k]
    return y


def _ssd_mixer(u, w_in, conv_w, conv_b, dt_bias, a_log, d_skip, gate_g, w_out):
    b, L, _ = u.shape
    G, E, P, N, Q = SSM_GROUPS, SSM_HEADS_PER_GROUP, SSM_HEAD_DIM, SSM_STATE, SSM_CHUNK
    f32 = jnp.float32
    z, xbc, dt_raw = jnp.split(u @ w_in, [SSM_D_INNER, SSM_D_INNER + SSM_CONV_DIM], axis=-1)
    xbc = jax.nn.silu(_causal_dwconv(xbc, conv_w, conv_b))
    xs, b_in, c_in = jnp.split(xbc, [SSM_D_INNER, SSM_D_INNER + G * N], axis=-1)
    dt = jax.nn.softplus(dt_raw.astype(f32) + dt_bias.astype(f32))
    a = -jnp.exp(a_log.astype(f32))
    pf = (-N_META) % Q
    pe = (-(pf + L)) % Q
    nc = (pf + L + pe) // Q

    def to_chunks(t, tail):
        return jnp.pad(t, ((0, 0), (pf, pe), (0, 0))).reshape((b, nc, Q) + tail)

    x_c = to_chunks(xs, (G, E, P)).astype(f32)
    b_c = to_chunks(b_in, (G, N)).astype(f32)
    c_c = to_chunks(c_in, (G, N)).astype(f32)
    dt_c = to_chunks(dt, (G, E))
    xdt = x_c * dt_c[..., None]
    a_cs = jnp.cumsum(jnp.transpose(dt_c * a.reshape(G, E), (0, 3, 4, 1, 2)), axis=-1)
    causal = jnp.tril(jnp.ones((Q, Q), dtype=bool))
    decay_in = jnp.exp(jnp.where(causal, a_cs[..., :, None] - a_cs[..., None, :], -jnp.inf))
    cb = jnp.einsum('bclgn,bcsgn->bgcls', c_c, b_c)
    y_diag = jnp.einsum('bgcls,bgecls,bcsgep->bclgep', cb, decay_in, xdt)
    decay_to_end = jnp.exp(a_cs[..., -1:] - a_cs)
    chunk_states = jnp.einsum('bclgn,bgecl,bclgep->cbgepn', b_c, decay_to_end, xdt)
    chunk_decay = jnp.moveaxis(jnp.exp(a_cs[..., -1]), -1, 0)

    def step(state, inp):
        s_new, d = inp
        return state * d[..., None, None] + s_new, state

    _, prev_states = lax.scan(step, jnp.zeros((b, G, E, P, N), f32), (chunk_states, chunk_decay))
    y_off = jnp.einsum('bclgn,cbgepn,bgecl->bclgep', c_c, prev_states, jnp.exp(a_cs))
    y = (y_diag + y_off).reshape(b, nc * Q, SSM_D_INNER)[:, pf:pf + L]
    y = y + (xs.reshape(b, L, SSM_HEADS, P).astype(f32) * d_skip.astype(f32)[:, None]).reshape(b, L, SSM_D_INNER)
    hg = (y * jax.nn.silu(z.astype(f32))).reshape(b, L, G, SSM_D_INNER // G)
    hg = hg * lax.rsqrt(jnp.mean(hg * hg, axis=-1, keepdims=True) + NORM_EPS)
    hg = hg.reshape(b, L, SSM_D_INNER) * gate_g.astype(f32)
    return hg.astype(u.dtype) @ w_out


def _stick_breaking_attention(q, k, v):
    b, L, H, Dh = q.shape
    lp = -(-L // SB_Q_BLOCK) * SB_Q_BLOCK
    pad = ((0, 0), (0, lp - L), (0, 0), (0, 0))
    q, k, v = jnp.pad(q, pad), jnp.pad(k, pad), jnp.pad(v, pad)
    scale = Dh ** -0.5
    outs = []
    for i in range(lp // SB_Q_BLOCK):
        t0, t1 = i * SB_Q_BLOCK, (i + 1) * SB_Q_BLOCK
        logits = jnp.einsum('bthd,bshd->bhts', q[:, t0:t1], k[:, :t1]).astype(jnp.float32) * scale
        t_idx = t0 + jnp.arange(SB_Q_BLOCK)[:, None]
        s_idx = jnp.arange(t1)[None, :]
        visible = s_idx < t_idx
        log_keep = jnp.where(visible, jax.nn.log_sigmoid(-logits), 0.0)
        later = lax.cumsum(log_keep, axis=3, reverse=True) - log_keep
        log_w = jnp.where(visible, jax.nn.log_sigmoid(logits) + later, -jnp.inf)
        w = jnp.exp(log_w).astype(v.dtype)
        outs.append(jnp.einsum('bhts,bshd->bthd', w, v[:, :t1]))
    return jnp.concatenate(outs, axis=1)[:, :L]


def _conv_ffn(u, w_up, conv_w, conv_b, w_down):
    h = _causal_dwconv(u @ w_up, conv_w, conv_b)
    g, val = jnp.split(h, 2, axis=-1)
    return (jax.nn.silu(g) * val) @ w_down


def setup_inputs(seed: int = 0) -> dict:
    key = jax.random.key(seed)
    ks = jax.random.split(key, 32)
    n_a = DEPTH // 2
    n_b = DEPTH - n_a
    f32 = jnp.float32

    def nrm(k, shape, scale):
        return jax.random.normal(k, shape, f32) * scale

    def gain(k, shape):
        return 1.0 + 0.02 * jax.random.normal(k, shape, f32)

    dt0 = jnp.exp(jax.random.uniform(ks[6], (n_a, SSM_HEADS), f32, np.log(1e-3), np.log(1e-1)))
    dt_bias = dt0 + jnp.log(-jnp.expm1(-dt0))
    return {
        'x': jax.random.normal(ks[0], (BATCH, SEQ, D_MODEL), f32),
        'meta_tokens': nrm(ks[1], (N_META, D_MODEL), 1.0),
        'ssd_norm': gain(ks[2], (n_a, D_MODEL)),
        'ssd_w_in': nrm(ks[3], (n_a, D_MODEL, SSM_IN_DIM), D_MODEL ** -0.5),
        'ssd_conv_w': nrm(ks[4], (n_a, SSM_CONV, SSM_CONV_DIM), SSM_CONV ** -0.5),
        'ssd_conv_b': nrm(ks[5], (n_a, SSM_CONV_DIM), 0.02),
        'ssd_dt_bias': dt_bias,
        'ssd_a_log': jnp.log(jax.random.uniform(ks[7], (n_a, SSM_HEADS), f32, 1.0, 16.0)),
        'ssd_d_skip': jax.random.uniform(ks[8], (n_a, SSM_HEADS), f32, 0.5, 1.5),
        'ssd_gate_norm': gain(ks[9], (n_a, SSM_D_INNER)),
        'ssd_w_out': nrm(ks[10], (n_a, SSM_D_INNER, D_MODEL), SSM_D_INNER ** -0.5),
        'kv_norm': gain(ks[11], (D_MODEL,)),
        'w_kv': nrm(ks[12], (D_MODEL, 2 * SB_WIDTH), D_MODEL ** -0.5),
        'sb_norm': gain(ks[13], (n_b, D_MODEL)),
        'sb_w_q': nrm(ks[14], (n_b, D_MODEL, SB_WIDTH), D_MODEL ** -0.5),
        'sb_w_o': nrm(ks[15], (n_b, SB_WIDTH, D_MODEL), SB_WIDTH ** -0.5),
        'ffn_norm': gain(ks[16], (DEPTH, D_MODEL)),
        'ffn_w_up': nrm(ks[17], (DEPTH, D_MODEL, 2 * D_FF), D_MODEL ** -0.5),
        'ffn_conv_w': nrm(ks[18], (DEPTH, FFN_CONV, 2 * D_FF), FFN_CONV ** -0.5),
        'ffn_conv_b': nrm(ks[19], (DEPTH, 2 * D_FF), 0.02),
        'ffn_w_down': nrm(ks[20], (DEPTH, D_FF, D_MODEL), D_FF ** -0.5),
        'final_norm': gain(ks[21], (D_MODEL,)),
    }


def reference(x, meta_tokens, ssd_norm, ssd_w_in, ssd_conv_w, ssd_conv_b, ssd_dt_bias, ssd_a_log,
              ssd_d_skip, ssd_gate_norm, ssd_w_out, kv_norm, w_kv, sb_norm, sb_w_q, sb_w_o,
              ffn_norm, ffn_w_up, ffn_conv_w, ffn_conv_b, ffn_w_down, final_norm):
    b = x.shape[0]
    n_a = DEPTH // 2
    h = jnp.concatenate([jnp.broadcast_to(meta_tokens[None], (b, N_META, D_MODEL)).astype(x.dtype), x], axis=1)
    L = h.shape[1]
    k_shared = None
    v_shared = None
    for layer in range(DEPTH):
        if layer < n_a:
            h = h + _ssd_mixer(_rmsnorm(h, ssd_norm[layer]), ssd_w_in[layer], ssd_conv_w[layer],
                               ssd_conv_b[layer], ssd_dt_bias[layer], ssd_a_log[layer],
                               ssd_d_skip[layer], ssd_gate_norm[layer], ssd_w_out[layer])
        else:
            if layer == n_a:
                kv = _rmsnorm(h, kv_norm) @ w_kv
                k_shared, v_shared = jnp.split(kv.reshape(b, L, 2, SB_HEADS, SB_HEAD_DIM), 2, axis=2)
                k_shared, v_shared = k_shared[:, :, 0], v_shared[:, :, 0]
            j = layer - n_a
            q = (_rmsnorm(h, sb_norm[j]) @ sb_w_q[j]).reshape(b, L, SB_HEADS, SB_HEAD_DIM)
            o = _stick_breaking_attention(q, k_shared, v_shared).reshape(b, L, SB_WIDTH)
            h = h + o @ sb_w_o[j]
        h = h + _conv_ffn(_rmsnorm(h, ffn_norm[layer]), ffn_w_up[layer], ffn_conv_w[layer],
                          ffn_conv_b[layer], ffn_w_down[layer])
    return _rmsnorm(h, final_norm)[:, N_META:]
```

```python
import numpy as np
import ml_dtypes
from contextlib import ExitStack
import concourse.bass as bass
import concourse.mybir as mybir
from concourse.bass_utils import run_bass_kernel_spmd

F32 = mybir.dt.float32
BF16 = mybir.dt.bfloat16
AF = mybir.ActivationFunctionType
ALU = mybir.AluOpType
AX = mybir.AxisListType
NPBF = ml_dtypes.bfloat16

D = 1024
LB = 8208
TQ = 2052
DI = 2048
DFF = 2816
EPS = 1e-6
EPOCH = 30000


I32 = mybir.dt.int32
ARENA = 106400
ISZ = {F32: 4, BF16: 2, I32: 4}


class Buf:
    __slots__ = ("name", "w", "r", "dsem", "dcnt")

    def __init__(self, name):
        self.name = name
        self.w = None
        self.r = {}
        self.dsem = None
        self.dcnt = 0


class T:
    __slots__ = ("t", "b")

    def __init__(self, t, b):
        self.t = t
        self.b = b


class Prog:
    ENGS = ("pe", "act", "dve", "pool", "sp")
    EMAP = {"pe": "tensor", "act": "scalar", "dve": "vector", "pool": "gpsimd", "sp": "sync"}

    def __init__(self, nc):
        self.nc = nc
        self.q = {e: [] for e in self.ENGS}
        self.cnt = {e: 0 for e in self.ENGS}
        self.seen = {e: {} for e in self.ENGS}
        self.sems = {}
        self.latest = {}
        self.stack = ExitStack()
        self.nbuf = 0
        self.arena = self.stack.enter_context(nc.sbuf_tensor("arena", [128, ARENA], BF16))
        self.banks = [self.stack.enter_context(nc.psum_tensor(f"bank{i}", [128, 512], F32)) for i in range(8)]
        self.off = 0
        self.persist = 0
        self.nbank = 0
        self.ncc = 0
        self.ccpend = []
        self.dfree = {"sw": [], "hw": []}
        self.dlive = []

    def _sem(self, key):
        if key not in self.sems:
            self.sems[key] = self.stack.enter_context(self.nc.semaphore("s_" + key.replace("#", "_")))
        return self.sems[key]

    def buf(self, name=None):
        self.nbuf += 1
        return Buf(f"{name or 'b'}{self.nbuf}")

    def sb(self, name, shape, dtype, stack=None):
        shape = list(shape)
        n = 1
        for d in shape[1:]:
            n *= d
        nel = (n * ISZ[dtype] + 1) // 2
        nel = (nel + 15) // 16 * 16
        assert self.off + nel <= ARENA, f"SBUF arena overflow at {name}: {self.off}+{nel}"
        v = self.arena[0:shape[0], self.off:self.off + n * ISZ[dtype] // 2]
        self.off += nel
        if dtype != BF16:
            v = v.bitcast(dtype)
        if len(shape) == 3:
            v = v.rearrange("p (a b) -> p a b", a=shape[1])
        return T(v, self.buf(name))

    def psum(self, name, shape, dtype=F32, stack=None):
        assert self.nbank < 8, "out of PSUM banks"
        bk = self.banks[self.nbank]
        self.nbank += 1
        v = bk[:, :]
        if dtype == BF16:
            v = v.bitcast(BF16)
        return T(v, self.buf(name))

    def _waits(self, eng, reads, writes):
        need = {}

        def add(k, v):
            if need.get(k, 0) < v:
                need[k] = v
        for b in reads:
            if b.w:
                add(*b.w)
        for b in writes:
            if b.w:
                add(*b.w)
            for k, v in b.r.items():
                add(k, v)
        if eng == "pe":
            for k in [k for k in need if k.startswith("pe#")]:
                del need[k]
        out = []
        seen = self.seen[eng]
        for k, v in need.items():
            if seen.get(k, 0) < v:
                seen[k] = v
                out.append((k, v))
        return out

    def _mark(self, ev, reads, writes):
        k, v = ev
        if self.latest.get(k, 0) < v:
            self.latest[k] = v
        for b in reads:
            if b.r.get(k, 0) < v:
                b.r[k] = v
        for b in writes:
            b.w = ev
            b.r = {}

    def op(self, eng, fn, reads=(), writes=(), sig=True):
        waits = self._waits(eng, reads, writes)
        c = self.cnt[eng]
        key = f"{eng}#{c // EPOCH}"
        self._sem(key)
        ev = (key, c % EPOCH + 1)
        if sig:
            self.cnt[eng] = c + 1
            self.q[eng].append((waits, fn, (key, 1)))
        else:
            assert eng == "pe"
            self.q[eng].append((waits, fn, None))
        self._mark(ev, reads, writes)
        return ev

    def dma(self, eng, fn, reads=(), writes=(), sembuf=None):
        waits = self._waits(eng, reads, writes)
        sb = sembuf or (writes[0] if writes else reads[0])
        if sb.dsem is None or sb.dcnt >= EPOCH:
            cls = "sw" if eng == "pool" else "hw"
            fl = self.dfree[cls]
            while fl and self.latest.get(fl[-1], 0) >= EPOCH - 4096:
                fl.pop()
            if fl:
                sb.dsem = fl.pop()
                sb.dcnt = self.latest.get(sb.dsem, 0)
            else:
                sb.dsem = f"d{cls}{len(self.sems)}"
                sb.dcnt = 0
                self._sem(sb.dsem)
            self.dlive.append((cls, sb.dsem))
        sb.dcnt += 16
        ev = (sb.dsem, sb.dcnt)
        self.q[eng].append((waits, fn, (sb.dsem, 16)))
        self._mark(ev, reads, writes)
        return ev

    def wait_all(self, eng, bufs):
        waits = self._waits(eng, bufs, bufs)
        self.q[eng].append((waits, None, None))

    def barrier(self):
        for e in self.ENGS:
            seen = self.seen[e]
            waits = []
            for k, v in self.latest.items():
                if seen.get(k, 0) < v:
                    seen[k] = v
                    waits.append((k, v))
            self.q[e].append((waits, None, None))

    def phase_start(self):
        self.barrier()
        self.off = self.persist
        self.nbank = 0
        for cls, k in self.dlive:
            self.dfree[cls].append(k)
        self.dlive = []

    def collective(self, kind, in_ap, out_ap, groups, wait_bufs):
        waits = self._waits("pool", wait_bufs, wait_bufs)
        key = f"cc{self.ncc}"
        self.ncc += 1
        self._sem(key)
        self.q["pool"].append((waits, lambda e: e.collective_compute(kind, ALU.bypass, replica_groups=groups,
                                                                    ins=[in_ap], outs=[out_ap]), (key, 1)))
        self.latest[key] = 1
        self.ccpend.append(key)

    def collective_wait(self):
        waits = [(k, 1) for k in self.ccpend if self.seen["pool"].get(k, 0) < 1]
        for k, _ in waits:
            self.seen["pool"][k] = 1
        self.ccpend = []
        self.q["pool"].append((waits, None, None))

    def emit(self):
        nc = self.nc
        sems = self.sems
        with nc.Block() as block:
            for e in self.ENGS:
                items = self.q[e]

                def body(engine, items=items):
                    for waits, fn, inc in items:
                        for k, v in waits:
                            engine.wait_ge(sems[k], v)
                        if fn is not None:
                            ins = fn(engine)
                            if inc is not None:
                                ins.then_inc(sems[inc[0]], inc[1])
                getattr(block, self.EMAP[e])(body)

    def close(self):
        self.stack.close()


def tiles(width, maxw=512):
    n = -(-width // maxw)
    base, rem = divmod(width, n)
    out, o = [], 0
    for i in range(n):
        w = base + (1 if i < rem else 0)
        out.append((o, w))
        o += w
    return out


def chunks128(width):
    out, o = [], 0
    while o < width:
        w = min(128, width - o)
        out.append((o, w))
        o += w
    return out


class Ctx:
    def __init__(self, P, wslot_elems=4096, nwslots=3, nps=8):
        self.nc = P.nc
        self.P = P
        self.ps = [P.psum(f"ps{i}", [128, 512]) for i in range(nps)]
        self.psi = 0
        self.wslots = [P.sb(f"wsl{i}", [128, wslot_elems], BF16) for i in range(nwslots)]
        self.wsi = 0
        self.wslot_elems = wslot_elems
        self.ones = P.sb("ones_f", [128, 128], F32)
        P.op("pool", lambda e: e.memset(self.ones.t[:], 1.0), writes=[self.ones.b])
        self.epst = P.sb("epst", [128, 1], F32)
        P.op("pool", lambda e: e.memset(self.epst.t[:], EPS), writes=[self.epst.b])
        self.onesb = P.sb("ones_b", [128, 128], BF16)
        P.op("pool", lambda e: e.memset(self.onesb.t[:], 1.0), writes=[self.onesb.b])
        self.sq = [P.sb(f"sq{i}", [128, 512], BF16) for i in range(4)]
        self.rs = [P.sb(f"rs{i}", [128, 512], F32) for i in range(2)]
        self.sqi = 0
        self.rsi = 0

    def psum(self):
        p = self.ps[self.psi % len(self.ps)]
        self.psi += 1
        return p

    def wslot(self):
        w = self.wslots[self.wsi % len(self.wslots)]
        self.wsi += 1
        return w

    def load_w(self, w_ap, k0, kc, n0, ncols):
        assert kc * ncols <= self.wslot_elems, (kc, ncols)
        sl = self.wslot()
        view = sl.t[:, 0:kc * ncols].rearrange("p (k n) -> p k n", k=kc)
        src = w_ap[k0:k0 + kc * 128, n0:n0 + ncols].rearrange("(k p) n -> p k n", p=128)
        self.P.dma("pool", lambda e: e.dma_start(out=view, in_=src), writes=[sl.b])
        return view, sl.b


def load_small(cx, name, dram_ap, shape, dtype=F32):
    t = cx.P.sb(name, shape, dtype)
    cx.P.dma("sp", lambda e: e.dma_start(out=t.t[:], in_=dram_ap), writes=[t.b])
    return t


def rmsnorm_fm(cx, src, g, dst, c0, width, dst_c0=0, kc=8):
    P = cx.P
    for (t0, tw) in tiles(width):
        ps = cx.psum()
        for k in range(kc):
            sq = cx.sq[cx.sqi % 4]
            cx.sqi += 1
            sl = src.t[:, k, c0 + t0:c0 + t0 + tw]
            P.op("pool", lambda e, sq=sq, sl=sl, tw=tw: e.tensor_tensor(out=sq.t[:, :tw], in0=sl, in1=sl, op=ALU.mult),
                 reads=[src.b], writes=[sq.b])
            P.op("pe", lambda e, ps=ps, sq=sq, tw=tw, k=k: e.matmul(ps.t[:, :tw], lhsT=cx.onesb.t[:], rhs=sq.t[:, :tw],
                                                              start=(k == 0), stop=(k == kc - 1)),
                 reads=[cx.onesb.b, sq.b], writes=[ps.b])
        rs = cx.rs[cx.rsi % 2]
        cx.rsi += 1
        P.op("act", lambda e, rs=rs, ps=ps, tw=tw: e.activation(out=rs.t[:, :tw], in_=ps.t[:, :tw], func=AF.Ln,
                                                             bias=cx.epst.t[:], scale=1.0 / (128 * kc)),
             reads=[ps.b, cx.epst.b], writes=[rs.b])
        P.op("act", lambda e, rs=rs, tw=tw: e.activation(out=rs.t[:, :tw], in_=rs.t[:, :tw], func=AF.Exp, scale=-0.5),
             reads=[rs.b], writes=[rs.b])
        for k in range(kc):
            P.op("dve", lambda e, k=k, rs=rs, t0=t0, tw=tw: e.scalar_tensor_tensor(
                out=dst.t[:, k, dst_c0 + t0:dst_c0 + t0 + tw], in0=src.t[:, k, c0 + t0:c0 + t0 + tw],
                scalar=g.t[:, k:k + 1], in1=rs.t[:, :tw], op0=ALU.mult, op1=ALU.mult),
                reads=[src.b, g.b, rs.b], writes=[dst.b])


def proj_fm(cx, w_ap, kc, n0, ncols, uT, c0, width, evac, ngroup=None):
    P = cx.P
    gcols = ngroup or max(128, (cx.wslot_elems // kc) // 128 * 128)
    gcols = min(gcols, 512)
    tl = tiles(width)
    for g0 in range(0, ncols, gcols):
        gc = min(gcols, ncols - g0)
        wv, wb = cx.load_w(w_ap, 0, kc, n0 + g0, gc)
        for jj, (j0, nsz) in enumerate(chunks128(gc)):
            j = (g0 + j0) // 128
            for (t0, tw) in tl:
                ps = cx.psum()
                for k in range(kc):
                    P.op("pe", lambda e, ps=ps, k=k, j0=j0, nsz=nsz, t0=t0, tw=tw, wv=wv: e.matmul(
                        ps.t[:nsz, :tw], lhsT=wv[:, k, j0:j0 + nsz], rhs=uT.t[:, k, c0 + t0:c0 + t0 + tw],
                        start=(k == 0), stop=(k == kc - 1)), reads=[wb, uT.b], writes=[ps.b], sig=(k == kc - 1))
                evac(ps, j, nsz, t0, tw)


def proj_tm(cx, w_ap, kc, n0, ncols, uT, c0, width, evac):
    P = cx.P
    gcols = min(512, max(1, (cx.wslot_elems // kc)))
    ch = chunks128(width)
    for g0 in range(0, ncols, gcols):
        gc = min(gcols, ncols - g0)
        wv, wb = cx.load_w(w_ap, 0, kc, n0 + g0, gc)
        for ci, (t0, csz) in enumerate(ch):
            ps = cx.psum()
            for k in range(kc):
                P.op("pe", lambda e, ps=ps, k=k, t0=t0, csz=csz, gc=gc, wv=wv: e.matmul(
                    ps.t[:csz, :gc], lhsT=uT.t[:, k, c0 + t0:c0 + t0 + csz], rhs=wv[:, k, 0:gc],
                    start=(k == 0), stop=(k == kc - 1)), reads=[wb, uT.b], writes=[ps.b], sig=(k == kc - 1))
            evac(ps, ci, t0, csz, g0, gc)


def conv_fm(cx, pre, w_t, b_t, j, taps, wo, acc):
    P = cx.P
    kl = taps - 1
    P.op("dve", lambda e: e.tensor_scalar(out=acc.t[:, :wo], in0=pre.t[:, kl:kl + wo], scalar1=w_t.t[:, j, kl:kl + 1],
                                          scalar2=b_t.t[:, j:j + 1], op0=ALU.mult, op1=ALU.add),
         reads=[pre.b, w_t.b, b_t.b], writes=[acc.b])
    for k in range(taps - 1):
        P.op("dve", lambda e, k=k: e.scalar_tensor_tensor(out=acc.t[:, :wo], in0=pre.t[:, k:k + wo],
                                                        scalar=w_t.t[:, j, k:k + 1], in1=acc.t[:, :wo],
                                                        op0=ALU.mult, op1=ALU.add),
             reads=[pre.b, w_t.b, acc.b], writes=[acc.b])


def ffn_fm(cx, hm, uT, w_up, w_down, cw, cb, wh, halo, actT, pre, acc):
    P = cx.P
    for j in range(22):
        pg, pv = pre[(2 * j) % 4], pre[(2 * j + 1) % 4]
        ag, av = acc[(2 * j) % 4], acc[(2 * j + 1) % 4]

        def ev_g(ps, jj, nsz, t0, tw, pg=pg):
            P.op("act", lambda e: e.activation(out=pg.t[:, t0:t0 + tw], in_=ps.t[:, :tw], func=AF.Identity),
                 reads=[ps.b], writes=[pg.b])

        def ev_v(ps, jj, nsz, t0, tw, pv=pv):
            P.op("act", lambda e: e.activation(out=pv.t[:, t0:t0 + tw], in_=ps.t[:, :tw], func=AF.Identity),
                 reads=[ps.b], writes=[pv.b])
        proj_fm(cx, w_up, 8, j * 128, 128, uT, 0, wh + 2, ev_g)
        proj_fm(cx, w_up, 8, DFF + j * 128, 128, uT, 0, wh + 2, ev_v)
        conv_fm(cx, pg, cw, cb, j, 3, wh, ag)
        conv_fm(cx, pv, cw, cb, 22 + j, 3, wh, av)
        P.op("act", lambda e, ag=ag: e.activation(out=ag.t[:, :wh], in_=ag.t[:, :wh], func=AF.Silu),
             reads=[ag.b], writes=[ag.b])
        P.op("dve", lambda e, ag=ag, av=av, j=j: e.tensor_tensor(out=actT.t[:, j, :wh], in0=ag.t[:, :wh],
                                                                in1=av.t[:, :wh], op=ALU.mult),
             reads=[ag.b, av.b], writes=[actT.b])

    def ev_d(ps, j, nsz, t0, tw):
        sl = hm.t[:, j, halo + t0:halo + t0 + tw]
        P.op("dve", lambda e: e.tensor_tensor(out=sl, in0=ps.t[:, :tw], in1=sl, op=ALU.add),
             reads=[ps.b, hm.b], writes=[hm.b])
    proj_fm(cx, w_down, 22, 0, D, actT, 0, wh, ev_d, ngroup=128)


def phase_token(P, kind, io):
    kcm = 16 if kind == "B" else 8
    HIN = 4 if kind == "B" else 2
    WIN = TQ + HIN
    WOUT = WIN - 2
    HW = WOUT // 2
    HWH = HW + 2
    cx = Ctx(P)
    outbufs = []
    hm = P.sb("hm", [128, 8, HWH], F32)
    uT = P.sb("uT", [128, 8, HWH], BF16)
    arena = P.sb("tkar", [128, max(kcm * HWH, 22 * HW)], BF16)
    opT = T(arena.t[:, 0:kcm * HWH].rearrange("p (k w) -> p k w", k=kcm), arena.b)
    actT = T(arena.t[:, 0:22 * HW].rearrange("p (k w) -> p k w", k=22), arena.b)
    pre = [P.sb(f"pre{i}", [128, HWH], F32) for i in range(4)]
    acc = [P.sb(f"acc{i}", [128, HW], F32) for i in range(4)]
    gf = load_small(cx, "gf", io["g_ffn"], [128, 8])
    cw = P.sb("cw", [128, 44, 3], F32)
    P.dma("sp", lambda e: e.dma_start(out=cw.t[:].rearrange("p a b -> p (a b)"), in_=io["cw"]), writes=[cw.b])
    cb = load_small(cx, "cb", io["cb"], [128, 44])
    stg_i = [0]
    resid, w_mix, w_up, w_down = io["resid"], io["w_mix"], io["w_up"], io["w_down"]
    sc, scb = io["sc"], io["scb"]
    if kind == "B":
        gkv = load_small(cx, "gkv", io["g_kv"], [128, 8])
        gq = load_small(cx, "gq", io["g_q"], [128, 8])
        hmask = load_small(cx, "hmask", io["hmask"], [128, 1])
        stg = [P.sb(f"stg{i}", [128, 512], BF16) for i in range(4)]
        outbufs += [s.b for s in stg] + [hm.b]
        w_kv, w_q, h1scr, b2, b2h = io["w_kv"], io["w_q"], io["h1scr"], io["b2"], io["b2h"]
    else:
        gfin = load_small(cx, "gfin", io["g_fin"], [128, 8])
        fo = P.sb("fo", [128, 8, 512], F32)
        outbufs.append(fo.b)
        outo = io["outo"]

    def do_half(a, first):
        for k in range(8):
            P.dma("sp", lambda e, k=k: e.dma_start(out=hm.t[:, k, :], in_=resid[k * 128:(k + 1) * 128, a:a + HWH]),
                  writes=[hm.b])

        for pt in range(kcm // 4):
            P.dma("sp", lambda e, pt=pt: e.dma_start(
                out=opT.t[:, pt * 4:(pt + 1) * 4, :], in_=sc[pt][:, :, a:a + HWH].rearrange("r p c -> p r c")),
                reads=[scb], writes=[opT.b])

        def tap(n):
            import os
            if kind == "D" and first and os.environ.get("FUSE_DTAP", "") == str(n):
                for k in range(8):
                    P.dma("sp", lambda e, k=k: e.dma_start(out=io["dtap"][k * 128:(k + 1) * 128, 0:HWH], in_=hm.t[:, k, :]),
                          reads=[hm.b], sembuf=hm.b)
        tap(1)

        def ev_mix(ps, j, nsz, t0, tw):
            sl = hm.t[:, j, t0:t0 + tw]
            P.op("dve", lambda e: e.tensor_tensor(out=sl, in0=ps.t[:, :tw], in1=sl, op=ALU.add),
                 reads=[ps.b, hm.b], writes=[hm.b])
        proj_fm(cx, w_mix, kcm, 0, D, opT, 0, HWH, ev_mix, ngroup=256 if kcm == 16 else 512)
        tap(2)
        rmsnorm_fm(cx, hm, gf, uT, 0, HWH)
        ffn_fm(cx, hm, uT, w_up, w_down, cw, cb, HW, 2, actT, pre, acc)
        tap(3)

        if kind == "B":
            if first:
                P.op("dve", lambda e: e.tensor_scalar(out=hm.t[:, :, 2:4], in0=hm.t[:, :, 2:4],
                                                      scalar1=hmask.t[:, 0:1], scalar2=None, op0=ALU.mult),
                     reads=[hm.b, hmask.b], writes=[hm.b])
            for k in range(8):
                P.dma("sp", lambda e, k=k: e.dma_start(out=h1scr[k * 128:(k + 1) * 128, a:a + HW],
                                                       in_=hm.t[:, k, 2:2 + HW]), reads=[hm.b], sembuf=hm.b)
            skip = 2 if first else 0
            c0 = 2 + skip
            wk = HW - skip
            rel0 = a + c0 - 4
            rmsnorm_fm(cx, hm, gkv, uT, c0, wk)

            def mk_ev(row0, scale):
                def ev(ps, j, nsz, t0, tw):
                    s = stg[stg_i[0] % 4]
                    stg_i[0] += 1
                    P.op("act", lambda e: e.activation(out=s.t[:, :tw], in_=ps.t[:, :tw], func=AF.Copy, scale=scale),
                         reads=[ps.b], writes=[s.b])
                    P.dma("sp", lambda e: e.dma_start(
                        out=b2[j // 2, row0 + j % 2][:, rel0 + t0:rel0 + t0 + tw], in_=s.t[:, :tw]),
                        reads=[s.b], sembuf=s.b)
                return ev
            proj_fm(cx, w_kv, 8, 0, D, uT, 0, wk, mk_ev(2, 1.0))

            def ev_v(ps, ci, t0, csz, n_off, nw):
                s = stg[stg_i[0] % 4]
                stg_i[0] += 1
                P.op("dve", lambda e: e.tensor_copy(out=s.t[:csz, :nw], in_=ps.t[:csz, :nw]),
                     reads=[ps.b], writes=[s.b])
                for u in range(nw // 128):
                    f0 = n_off + u * 128
                    off = ((f0 // 256) * 6 + 4 + (f0 % 256) // 128) * 128 * TQ + (rel0 + t0) * 128
                    P.dma("sp", lambda e, u=u, off=off: e.dma_start(
                        out=bass.AP(b2h, off, [[128, csz], [1, 128]]), in_=s.t[:csz, u * 128:(u + 1) * 128]),
                        reads=[s.b], sembuf=s.b)
            proj_tm(cx, w_kv, 8, D, D, uT, 0, wk, ev_v)
            rmsnorm_fm(cx, hm, gq, uT, c0, wk)
            proj_fm(cx, w_q, 8, 0, D, uT, 0, wk, mk_ev(0, 0.125))
        else:
            for (t0, tw) in tiles(HW):
                fin_tile(a, t0, tw)

    def fin_tile(a, t0, tw):
        rmsnorm_fm(cx, hm, gfin, fo, 2 + t0, tw)
        for k in range(8):
            P.dma("sp", lambda e, k=k: e.dma_start(out=outo[k * 128:(k + 1) * 128, a + t0:a + t0 + tw],
                                                   in_=fo.t[:, k, :tw]), reads=[fo.b], sembuf=fo.b)

    do_half(0, True)
    do_half(HW, False)
    return outbufs


def phase_attn(P, io):
    oTd = io["b3"]
    sc, sch, scb = io["sc"], io["sch"], io["scb"]
    NB = 65
    kT = P.sb("kT", [128, 4, LB], BF16)
    vS = P.sb("vS", [128, NB, 320], BF16)
    qt = [P.sb(f"qt{i}", [128, 4, 512], BF16) for i in range(2)]
    spA = P.sb("spA", [128, NB, 512], BF16)
    wt = [P.sb(f"wt{i}", [128, 1024], BF16) for i in range(2)]
    tmp = [P.sb(f"tmp{i}", [128, 1024], F32) for i in range(2)]
    carry = P.sb("carry", [128, 512], F32)
    ost = [P.sb(f"ost{i}", [64, 512], BF16) for i in range(2)]
    negU = P.sb("negU", [128, 128], BF16)
    onesb = P.sb("onesb", [128, 128], BF16)
    masks = [P.sb(f"mask{i}", [128, 512], BF16) for i in range(4)]
    ps1 = [P.psum(f"ps1_{i}", [128, 512]) for i in range(2)]
    ps2 = [P.psum(f"ps2_{i}", [128, 512]) for i in range(2)]
    pcb = [P.psum(f"pcb{i}", [128, 512]) for i in range(2)]
    po = [P.psum(f"po{i}", [128, 512]) for i in range(2)]

    P.op("pool", lambda e: e.memset(onesb.t[:], 1.0), writes=[onesb.b])
    P.op("pool", lambda e: e.memset(negU.t[:], -1.0), writes=[negU.b])
    P.op("pool", lambda e: e.memset(kT.t[64:128, :, :], 0.0), writes=[kT.b])
    P.op("pool", lambda e: e.memset(vS.t[:, :, 256:320], 0.0), writes=[vS.b])
    for qq in qt:
        P.op("pool", lambda e, qq=qq: e.memset(qq.t[64:128, :, :], 0.0), writes=[qq.b])
    P.op("pool", lambda e: e.affine_select(out=negU.t[:], in_=negU.t[:], pattern=[[-1, 128]], compare_op=ALU.is_ge,
                                           fill=0.0, base=0, channel_multiplier=1), reads=[negU.b], writes=[negU.b])
    for i in range(4):
        P.op("pool", lambda e, i=i: e.memset(masks[i].t[:], 1.0), writes=[masks[i].b])
        P.op("pool", lambda e, i=i: e.affine_select(out=masks[i].t[:], in_=masks[i].t[:], pattern=[[1, 512]],
                                                    compare_op=ALU.is_gt, fill=0.0, base=-128 * i,
                                                    channel_multiplier=-1), reads=[masks[i].b], writes=[masks[i].b])
    CH = 128 * TQ
    for X in range(2):
        for r in range(4):
            P.dma("sp", lambda e, X=X, r=r: e.dma_start(
                out=kT.t[0:64, 2 * X:2 * X + 2, r * TQ:(r + 1) * TQ],
                in_=sc[2 + X, r].rearrange("(two p) c -> p two c", p=64)), reads=[scb], writes=[kT.b])
    zt = P.sb("zt", [128, 2, 2], BF16)
    P.op("pool", lambda e: e.memset(zt.t[:], 0.0), writes=[zt.b])
    P.dma("sp", lambda e: e.dma_start(out=oTd[0][:, :, 0:2].rearrange("j p c -> p j c"), in_=zt.t[:]),
          reads=[zt.b], sembuf=zt.b)
    def ldv_piece(X, r, lrow, nrow, blk, p0, nblk):
        off = ((4 + X) * 4 + r) * CH + lrow * 128
        if nblk:
            P.dma("sp", lambda e: e.dma_start(out=vS.t[:, blk:blk + nblk, X * 128:(X + 1) * 128],
                                              in_=bass.AP(sch, off, [[128, 128], [128 * 128, nblk], [1, 128]])),
                  reads=[scb], writes=[vS.b])
        else:
            P.dma("sp", lambda e: e.dma_start(out=vS.t[p0:p0 + nrow, blk, X * 128:(X + 1) * 128],
                                              in_=bass.AP(sch, off, [[128, nrow], [1, 128]])),
                  reads=[scb], writes=[vS.b])
    for X in range(2):
        for r in range(4):
            lo, hi = TQ * r, TQ * r + TQ
            pos = lo
            if pos % 128:
                n = 128 - pos % 128
                ldv_piece(X, r, pos - lo, n, pos // 128, pos % 128, 0)
                pos += n
            nfull = (hi - pos) // 128
            if nfull:
                ldv_piece(X, r, pos - lo, 128 * nfull, pos // 128, 0, nfull)
                pos += 128 * nfull
            if pos < hi:
                ldv_piece(X, r, pos - lo, hi - pos, pos // 128, 0, 0)

    spb = [P.buf(f"sp{b_}") for b_ in range(NB)]
    pex = [P.sb(f"pex{i}", [128, 1024], F32) for i in range(2)]
    onef = P.sb("onef", [128, 1], F32)
    P.op("pool", lambda e: e.memset(onef.t[:], 1.0), writes=[onef.b])
    cnt = {"a": 0, "b": 0, "o": 0, "ap": 0, "bp": 0}

    def tile_geom(j):
        t0 = 512 * j
        tw = 512 if j < 16 else 16
        nblk = 4 * j + 4 if j < 16 else 65
        return t0, tw, nblk

    def load_q(j):
        t0, tw, nblk = tile_geom(j)
        q = qt[j % 2]
        for X in range(2):
            for r in range(4):
                lo, hi = max(t0, TQ * r), min(t0 + tw, TQ * r + TQ)
                if lo < hi:
                    P.dma("sp", lambda e, X=X, r=r, lo=lo, hi=hi: e.dma_start(
                        out=q.t[0:64, 2 * X:2 * X + 2, lo - t0:hi - t0],
                        in_=sc[X, r].rearrange("(two p) c -> p two c", p=64)[:, :, lo - TQ * r:hi - TQ * r]),
                        reads=[scb], writes=[q.b])

    def blkinfo(j, blk):
        ksz = 128 if blk < 64 else 16
        diag = None
        if j < 16 and blk >= 4 * j:
            diag = blk - 4 * j
        if j == 16 and blk == 64:
            diag = 0
        return ksz, diag

    def a_step(j, h, blk):
        t0, tw, nblk = tile_geom(j)
        q = qt[j % 2]
        ksz, diag = blkinfo(j, blk)
        i1 = cnt["a"] % 2
        cnt["a"] += 1
        p1, px = ps1[i1], pex[cnt["ap"] % 2]
        cnt["ap"] += 1
        P.op("pe", lambda e: e.matmul(p1.t[:ksz, :tw], lhsT=kT.t[:, h, blk * 128:blk * 128 + ksz], rhs=q.t[:, h, :tw],
                                      start=True, stop=True), reads=[kT.b, q.b], writes=[p1.b])
        P.op("act", lambda e: e.activation(out=px.t[:ksz, :tw], in_=p1.t[:ksz, :tw], func=AF.Exp),
             reads=[p1.b], writes=[px.b])
        P.op("act", lambda e: e.activation(out=spA.t[:ksz, blk, :tw], in_=px.t[:ksz, :tw], func=AF.Ln,
                                           bias=onef.t[:ksz, :]), reads=[px.b, onef.b], writes=[spb[blk]])
        if diag is not None:
            P.op("dve", lambda e: e.tensor_tensor(out=spA.t[:ksz, blk, :tw], in0=spA.t[:ksz, blk, :tw],
                                                   in1=masks[diag].t[:ksz, :tw], op=ALU.mult),
                 reads=[spb[blk], masks[diag].b], writes=[spb[blk]])

    class BState:
        pass

    def b_begin(j, h):
        st = BState()
        st.j, st.h = j, h
        st.t0, st.tw, st.nblk = tile_geom(j)
        st.pacc = po[cnt["o"] % 2]
        st.first = True
        st.pend = None
        return st

    def b_pv(st, blk, ksz, w, start):
        h, tw, pacc = st.h, st.tw, st.pacc
        P.op("pe", lambda e: e.matmul(pacc.t[:128, :tw], lhsT=vS.t[:ksz, blk, h * 64:h * 64 + 128],
                                      rhs=w.t[:ksz, :tw], start=start, stop=(blk == 0)),
             reads=[vS.b, w.b], writes=[pacc.b])

    def b_step(st, blk):
        j, h, tw = st.j, st.h, st.tw
        q = qt[j % 2]
        ksz, diag = blkinfo(j, blk)
        i2 = cnt["b"] % 2
        cnt["b"] += 1
        ip = cnt["bp"] % 2
        cnt["bp"] += 1
        p2, pc, w, tm = ps2[i2], pcb[i2], wt[ip], tmp[ip]
        first = st.first
        P.op("pe", lambda e: e.matmul(p2.t[:ksz, :tw], lhsT=negU.t[:ksz, :ksz], rhs=spA.t[:ksz, blk, :tw],
                                      start=True, stop=False), reads=[negU.b, spb[blk]], writes=[p2.b], sig=False)
        P.op("pe", lambda e: e.matmul(p2.t[:ksz, :tw], lhsT=kT.t[:, h, blk * 128:blk * 128 + ksz], rhs=q.t[:, h, :tw],
                                      start=False, stop=True), reads=[kT.b, q.b], writes=[p2.b])
        if blk > 0:
            P.op("pe", lambda e: e.matmul(pc.t[:, :tw], lhsT=onesb.t[:ksz, :], rhs=spA.t[:ksz, blk, :tw],
                                          start=True, stop=True), reads=[onesb.b, spb[blk]], writes=[pc.b])
        if st.pend is not None:
            for pe_ in st.pend:
                b_pv(st, *pe_)
        if first:
            P.op("act", lambda e: e.activation(out=w.t[:ksz, :tw], in_=p2.t[:ksz, :tw], func=AF.Exp),
                 reads=[p2.b], writes=[w.b])
        else:
            P.op("dve", lambda e: e.tensor_tensor(out=tm.t[:ksz, :tw], in0=p2.t[:ksz, :tw], in1=carry.t[:ksz, :tw],
                                                  op=ALU.subtract), reads=[p2.b, carry.b], writes=[tm.b])
            P.op("act", lambda e: e.activation(out=w.t[:ksz, :tw], in_=tm.t[:ksz, :tw], func=AF.Exp),
                 reads=[tm.b], writes=[w.b])
        if diag is not None:
            P.op("dve", lambda e: e.tensor_tensor(out=w.t[:ksz, :tw], in0=w.t[:ksz, :tw],
                                                   in1=masks[diag].t[:ksz, :tw], op=ALU.mult),
                 reads=[w.b, masks[diag].b], writes=[w.b])
        if blk > 0:
            if first:
                P.op("dve", lambda e: e.tensor_copy(out=carry.t[:, :tw], in_=pc.t[:, :tw]),
                     reads=[pc.b], writes=[carry.b])
            else:
                P.op("dve", lambda e: e.tensor_tensor(out=carry.t[:, :tw], in0=pc.t[:, :tw], in1=carry.t[:, :tw],
                                                      op=ALU.add), reads=[pc.b, carry.b], writes=[carry.b])
        st.pend = [(blk, ksz, w, first)]
        st.first = False

    def b_end(st):
        h, t0, tw, pacc = st.h, st.t0, st.tw, st.pacc
        for pe_ in st.pend:
            b_pv(st, *pe_)
        cnt["o"] += 1
        o = ost[cnt["o"] % 2]
        P.op("dve", lambda e: e.tensor_copy(out=o.t[:, :tw], in_=pacc.t[:64, :tw]), reads=[pacc.b], writes=[o.b])
        for d in range(4):
            lo, hi = max(t0, TQ * d - 2), min(t0 + tw, TQ * d + TQ)
            if lo < hi:
                c0d = TQ * d - 2
                P.dma("sp", lambda e, d=d, lo=lo, hi=hi, c0d=c0d: e.dma_start(
                    out=oTd[d, h // 2][(h % 2) * 64:(h % 2) * 64 + 64, lo - c0d:hi - c0d], in_=o.t[:, lo - t0:hi - t0]),
                    reads=[o.b], sembuf=o.b)

    def a_pair(j, h, blk):
        t0, tw, nblk = tile_geom(j)
        q = qt[j % 2]
        px = pex[cnt["ap"] % 2]
        cnt["ap"] += 1
        for u, bb in ((1, blk), (0, blk - 1)):
            p1 = ps1[cnt["a"] % 2]
            cnt["a"] += 1
            P.op("pe", lambda e, p1=p1, bb=bb: e.matmul(p1.t[:, :tw], lhsT=kT.t[:, h, bb * 128:bb * 128 + 128],
                                                       rhs=q.t[:, h, :tw], start=True, stop=True),
                 reads=[kT.b, q.b], writes=[p1.b])
            P.op("act", lambda e, p1=p1, u=u: e.activation(out=px.t[:, u * 512:u * 512 + tw], in_=p1.t[:, :tw],
                                                         func=AF.Exp), reads=[p1.b], writes=[px.b])
        P.op("act", lambda e: e.activation(out=spA.t[:, blk - 1:blk + 1, :tw],
                                           in_=px.t[:, :].rearrange("p (u c) -> p u c", u=2)[:, :, :tw],
                                           func=AF.Ln, bias=onef.t[:, :]),
             reads=[px.b, onef.b], writes=[spb[blk - 1], spb[blk]])
        for bb in (blk, blk - 1):
            ksz, diag = blkinfo(j, bb)
            if diag is not None:
                P.op("dve", lambda e, bb=bb, diag=diag: e.tensor_tensor(
                    out=spA.t[:, bb, :tw], in0=spA.t[:, bb, :tw], in1=masks[diag].t[:, :tw], op=ALU.mult),
                    reads=[spb[bb], masks[diag].b], writes=[spb[bb]])

    def b_pair(st, blk):
        j, h, tw = st.j, st.h, st.tw
        q = qt[j % 2]
        ip = cnt["bp"] % 2
        cnt["bp"] += 1
        w, tm = wt[ip], tmp[ip]
        first = st.first
        for u, bb in ((1, blk), (0, blk - 1)):
            i2 = cnt["b"] % 2
            cnt["b"] += 1
            p2, pc = ps2[i2], pcb[i2]
            P.op("pe", lambda e, p2=p2, bb=bb: e.matmul(p2.t[:, :tw], lhsT=negU.t[:, :], rhs=spA.t[:, bb, :tw],
                                                       start=True, stop=False),
                 reads=[negU.b, spb[bb]], writes=[p2.b], sig=False)
            P.op("pe", lambda e, p2=p2, bb=bb: e.matmul(p2.t[:, :tw], lhsT=kT.t[:, h, bb * 128:bb * 128 + 128],
                                                       rhs=q.t[:, h, :tw], start=False, stop=True),
                 reads=[kT.b, q.b], writes=[p2.b])
            if bb > 0:
                P.op("pe", lambda e, pc=pc, bb=bb: e.matmul(pc.t[:, :tw], lhsT=onesb.t[:, :], rhs=spA.t[:, bb, :tw],
                                                           start=True, stop=True),
                     reads=[onesb.b, spb[bb]], writes=[pc.b])
            if u == 1 and st.pend is not None:
                for pe_ in st.pend:
                    b_pv(st, *pe_)
                st.pend = None
            if first and u == 1:
                P.op("dve", lambda e, p2=p2, u=u: e.tensor_copy(out=tm.t[:, u * 512:u * 512 + tw], in_=p2.t[:, :tw]),
                     reads=[p2.b], writes=[tm.b])
            else:
                P.op("dve", lambda e, p2=p2, u=u: e.tensor_tensor(out=tm.t[:, u * 512:u * 512 + tw], in0=p2.t[:, :tw],
                                                                 in1=carry.t[:, :tw], op=ALU.subtract),
                     reads=[p2.b, carry.b], writes=[tm.b])
            if bb > 0:
                if first and u == 1:
                    P.op("dve", lambda e, pc=pc: e.tensor_copy(out=carry.t[:, :tw], in_=pc.t[:, :tw]),
                         reads=[pc.b], writes=[carry.b])
                else:
                    P.op("dve", lambda e, pc=pc: e.tensor_tensor(out=carry.t[:, :tw], in0=pc.t[:, :tw],
                                                                in1=carry.t[:, :tw], op=ALU.add),
                         reads=[pc.b, carry.b], writes=[carry.b])
        P.op("act", lambda e: e.activation(out=w.t[:, :].rearrange("p (u c) -> p u c", u=2)[:, :, :tw],
                                           in_=tm.t[:, :].rearrange("p (u c) -> p u c", u=2)[:, :, :tw], func=AF.Exp),
             reads=[tm.b], writes=[w.b])
        pend = []
        for u, bb in ((1, blk), (0, blk - 1)):
            ksz, diag = blkinfo(j, bb)
            wv = T(w.t[:, u * 512:(u + 1) * 512], w.b)
            if diag is not None:
                P.op("dve", lambda e, wv=wv, diag=diag: e.tensor_tensor(
                    out=wv.t[:, :tw], in0=wv.t[:, :tw], in1=masks[diag].t[:, :tw], op=ALU.mult),
                    reads=[w.b, masks[diag].b], writes=[w.b])
            pend.append((bb, 128, wv, first and u == 1))
        st.pend = pend
        st.first = False

    streams = [(j, h) for j in range(17) for h in range(4)]
    def a_range(j, h, hi, lo):
        blk = hi - 1
        while blk >= lo:
            if j < 16 and blk - 1 >= lo and blk < 64:
                a_pair(j, h, blk)
                blk -= 2
            else:
                a_step(j, h, blk)
                blk -= 1

    load_q(0)
    a_range(0, 0, tile_geom(0)[2], 0)
    for k, (j, h) in enumerate(streams):
        nb_cur = tile_geom(j)[2]
        nxt = streams[k + 1] if k + 1 < len(streams) else None
        if nxt is not None:
            if nxt[1] == 0:
                load_q(nxt[0])
            nb_nxt = tile_geom(nxt[0])[2]
            a_range(nxt[0], nxt[1], nb_nxt, nb_cur)
        st = b_begin(j, h)
        blk = nb_cur - 1
        while blk >= 0:
            if j < 16 and blk >= 1:
                b_pair(st, blk)
                if nxt is not None:
                    a_range(nxt[0], nxt[1], blk + 1, blk - 1)
                blk -= 2
            else:
                b_step(st, blk)
                if nxt is not None:
                    a_range(nxt[0], nxt[1], blk + 1, blk)
                blk -= 1
        b_end(st)
    return [o.b for o in ost] + [zt.b]


def phase_ssd(P, io):
    xTd, wsd, g_in, cwd, cbd = io["xT"], io["wsel"], io["g_in"], io["cw"], io["cb"]
    dtbd, alogd, dskd, ggd, hgo = io["dtb"], io["alog"], io["dsk"], io["gg"], io["b1"]
    cx = Ctx(P, wslot_elems=8 * 1288, nwslots=1, nps=4)
    W = TQ
    CH = chunks128(W)
    NCH = len(CH)
    ptr = [P.psum(f"ptr{i}", [128, 1024], BF16) for i in range(2)]
    pacc = [P.psum(f"pacc{i}", [128, 512]) for i in range(2)]
    gin = load_small(cx, "gin", g_in, [128, 8])
    cw = P.sb("cw", [128, 6, 4], F32)
    P.dma("sp", lambda e: e.dma_start(out=cw.t[:].rearrange("p a b -> p (a b)"), in_=cwd), writes=[cw.b])
    cb = load_small(cx, "cb", cbd, [128, 6])
    dtb = load_small(cx, "dtb", dtbd, [128, 8])
    aneg = load_small(cx, "aneg", alogd, [128, 8])
    dsk = load_small(cx, "dsk", dskd, [128, 8])
    gg = load_small(cx, "gg", ggd, [128, 512])
    P.op("act", lambda e: e.activation(out=aneg.t[:], in_=aneg.t[:], func=AF.Exp), reads=[aneg.b], writes=[aneg.b])
    P.op("dve", lambda e: e.tensor_scalar(out=aneg.t[:], in0=aneg.t[:], scalar1=-1.0, scalar2=None, op0=ALU.mult),
         reads=[aneg.b], writes=[aneg.b])
    wv, wb = cx.load_w(wsd, 0, 8, 0, 1288)

    ident = P.sb("ident", [128, 128], BF16)
    P.op("pool", lambda e: e.memset(ident.t[:], 1.0), writes=[ident.b])
    P.op("pool", lambda e: e.affine_select(out=ident.t[:], in_=ident.t[:], pattern=[[-1, 128]], compare_op=ALU.is_equal,
                                           fill=0.0, base=0, channel_multiplier=1), reads=[ident.b], writes=[ident.b])
    triI = P.sb("triI", [128, 128], F32)
    P.op("pool", lambda e: e.memset(triI.t[:], 1.0), writes=[triI.b])
    P.op("pool", lambda e: e.affine_select(out=triI.t[:], in_=triI.t[:], pattern=[[1, 128]], compare_op=ALU.is_ge,
                                           fill=0.0, base=0, channel_multiplier=-1), reads=[triI.b], writes=[triI.b])
    mstr = P.sb("mstr", [128, 128], F32)
    P.op("pool", lambda e: e.memset(mstr.t[:], 1.0), writes=[mstr.b])
    P.op("pool", lambda e: e.affine_select(out=mstr.t[:], in_=mstr.t[:], pattern=[[-1, 128]], compare_op=ALU.is_gt,
                                           fill=0.0, base=0, channel_multiplier=1), reads=[mstr.b], writes=[mstr.b])

    xins = [P.sb(f"xin{i}", [128, 8, 416], F32) for i in range(2)]
    xcnt = [0]
    uT = P.sb("uT", [128, 8, W + 3], BF16)
    pre = P.sb("pre", [128, W + 3], F32)
    acc = P.sb("acc", [128, W], F32)
    xsT = P.sb("xsT", [128, 6, W], BF16)
    xs_tm = P.sb("xs_tm", [128, NCH, 512], BF16)
    B_tm = P.sb("B_tm", [128, NCH, 128], BF16)
    zs_tm = P.sb("zs_tm", [128, NCH, 512], BF16)
    dt = P.sb("dt", [128, NCH, 8], F32)
    dta = P.sb("dta", [128, NCH, 8], F32)
    cs = P.sb("cs", [128, NCH, 8], F32)
    ecs = P.sb("ecs", [128, NCH, 8], F32)
    wst = P.sb("wst", [128, NCH, 8], F32)
    cdec = P.sb("cdec", [128, NCH, 8], F32)
    for t_ in (dt, dta, cs, ecs, wst, cdec):
        P.op("pool", lambda e, t_=t_: e.memset(t_.t[:], 0.0), writes=[t_.b])
    S = P.sb("S", [128, 512], F32)
    Sb = P.sb("Sb", [128, 512], BF16)
    P.op("pool", lambda e: e.memset(S.t[:], 0.0), writes=[S.b])
    P.op("pool", lambda e: e.memset(Sb.t[:], 0.0), writes=[Sb.b])
    cbT = P.sb("cbT", [128, 128], F32)
    lh8 = [P.sb(f"lh8_{i}", [128, 8, 128], F32) for i in range(2)]
    dec8 = [P.sb(f"dec8_{i}", [128, 8, 128], F32) for i in range(2)]
    MT8 = [P.sb(f"MT8_{i}", [128, 8, 128], BF16) for i in range(2)]
    xdt = P.sb("xdt", [128, 512], BF16)
    xdte = P.sb("xdte", [128, 512], BF16)
    y1 = P.sb("y1", [128, 512], F32)
    y2 = P.sb("y2", [128, 512], F32)
    hgn = P.sb("hgn", [128, 512], BF16)
    ss = P.sb("ss", [128, 2], F32)
    hst = [P.sb(f"hst{i}", [128, 4, 128], BF16) for i in range(2)]
    cnt = {"h": 0, "l": 0}
    v3 = lambda ap: ap.rearrange("p (h d) -> p h d", h=8)
    bc = lambda ap: ap.unsqueeze(2).to_broadcast([ap.shape[0], 8, 64])

    def do_seg(si):
        s0 = si * W

        def ld(t0, tw):
            xin = xins[xcnt[0] % 2]
            xcnt[0] += 1
            ti = t0 // tw
            for k in range(8):
                P.dma(("sp", "act")[k % 2], lambda e, k=k: e.dma_start(
                    out=xin.t[:, k, :tw], in_=xTd[si, ti, k]), writes=[xin.b])
            rmsnorm_fm(cx, xin, gin, uT, 0, tw, dst_c0=t0)
        for (t0, tw) in tiles(W + 3):
            ld(t0, tw)

        def inproj(jc):
            for (t0, tw) in tiles(W + 3):
                ps = cx.psum()
                for k in range(8):
                    P.op("pe", lambda e, ps=ps, k=k, t0=t0, tw=tw: e.matmul(
                        ps.t[:, :tw], lhsT=wv[:, k, 512 + jc * 128:512 + (jc + 1) * 128], rhs=uT.t[:, k, t0:t0 + tw],
                        start=(k == 0), stop=(k == 7)), reads=[wb, uT.b], writes=[ps.b])
                P.op("act", lambda e, ps=ps, t0=t0, tw=tw: e.activation(out=pre.t[:, t0:t0 + tw], in_=ps.t[:, :tw],
                                                                       func=AF.Identity), reads=[ps.b], writes=[pre.b])
            conv_fm(cx, pre, cw, cb, jc, 4, W, acc)
            P.op("act", lambda e: e.activation(out=xsT.t[:, jc, :], in_=acc.t[:, :], func=AF.Silu),
                 reads=[acc.b], writes=[xsT.b])
        for jc in range(6):
            inproj(jc)

        def tr(ci, c0, csz):
            pt = ptr[ci % 2]
            for jc in range(5):
                P.op("pe", lambda e, jc=jc: e.transpose(pt.t[:csz, jc * 128:(jc + 1) * 128],
                                                        xsT.t[:, jc, c0:c0 + csz], ident.t[:]),
                     reads=[xsT.b, ident.b], writes=[pt.b])
            P.op("dve", lambda e: e.tensor_copy(out=xs_tm.t[:csz, ci, :], in_=pt.t[:csz, 0:512]),
                 reads=[pt.b], writes=[xs_tm.b])
            P.op("dve", lambda e: e.tensor_copy(out=B_tm.t[:csz, ci, :], in_=pt.t[:csz, 512:640]),
                 reads=[pt.b], writes=[B_tm.b])

        def zz(ci, c0, csz):
            ps = cx.psum()
            for k in range(8):
                P.op("pe", lambda e, k=k: e.matmul(ps.t[:csz, :512], lhsT=uT.t[:, k, 3 + c0:3 + c0 + csz],
                                                  rhs=wv[:, k, 0:512], start=(k == 0), stop=(k == 7)),
                     reads=[wb, uT.b], writes=[ps.b])
            P.op("act", lambda e: e.activation(out=zs_tm.t[:csz, ci, :], in_=ps.t[:csz, :512], func=AF.Silu),
                 reads=[ps.b], writes=[zs_tm.b])

        def dd(ci, c0, csz):
            ps = cx.psum()
            for k in range(8):
                P.op("pe", lambda e, k=k: e.matmul(ps.t[:csz, :8], lhsT=uT.t[:, k, 3 + c0:3 + c0 + csz],
                                                  rhs=wv[:, k, 1280:1288], start=(k == 0), stop=(k == 7)),
                     reads=[wb, uT.b], writes=[ps.b])
            P.op("dve", lambda e: e.tensor_tensor(out=dt.t[:csz, ci, :], in0=ps.t[:csz, :8], in1=dtb.t[:csz, :],
                                                  op=ALU.add), reads=[ps.b, dtb.b], writes=[dt.b])

        def da(ci, c0, csz):
            P.op("dve", lambda e: e.tensor_tensor(out=dta.t[:csz, ci, :], in0=dt.t[:csz, ci, :], in1=aneg.t[:csz, :],
                                                  op=ALU.mult), reads=[dt.b, aneg.b], writes=[dta.b])

        def cc(ci, c0, csz):
            ps = cx.psum()
            P.op("pe", lambda e: e.matmul(ps.t[:csz, 0:8], lhsT=triI.t[:csz, :csz], rhs=dta.t[:csz, ci, :],
                                          start=True, stop=True), reads=[triI.b, dta.b], writes=[ps.b])
            P.op("pe", lambda e: e.matmul(ps.t[:, 8:16], lhsT=cx.ones.t[:csz, :], rhs=dta.t[:csz, ci, :],
                                          start=True, stop=True), reads=[cx.ones.b, dta.b], writes=[ps.b])
            P.op("dve", lambda e: e.tensor_copy(out=cs.t[:csz, ci, :], in_=ps.t[:csz, 0:8]),
                 reads=[ps.b], writes=[cs.b])
            P.op("dve", lambda e: e.tensor_tensor(out=wst.t[:csz, ci, :], in0=ps.t[:csz, 8:16],
                                                  in1=cs.t[:csz, ci, :], op=ALU.subtract),
                 reads=[ps.b, cs.b], writes=[wst.b])
            P.op("act", lambda e: e.activation(out=cdec.t[:, ci, :], in_=ps.t[:, 8:16], func=AF.Exp),
                 reads=[ps.b], writes=[cdec.b])
        for ci, (c0, csz) in enumerate(CH):
            tr(ci, c0, csz)
        for ci, (c0, csz) in enumerate(CH):
            zz(ci, c0, csz)
        for ci, (c0, csz) in enumerate(CH):
            dd(ci, c0, csz)
        P.op("act", lambda e: e.activation(out=dt.t[:], in_=dt.t[:], func=AF.Softplus), reads=[dt.b], writes=[dt.b])
        for ci, (c0, csz) in enumerate(CH):
            da(ci, c0, csz)
        for ci, (c0, csz) in enumerate(CH):
            cc(ci, c0, csz)
        P.op("act", lambda e: e.activation(out=ecs.t[:], in_=cs.t[:], func=AF.Exp), reads=[cs.b], writes=[ecs.b])
        P.op("act", lambda e: e.activation(out=wst.t[:], in_=wst.t[:], func=AF.Exp), reads=[wst.b], writes=[wst.b])
        P.op("dve", lambda e: e.tensor_tensor(out=wst.t[:], in0=wst.t[:], in1=dt.t[:], op=ALU.mult),
             reads=[wst.b, dt.b], writes=[wst.b])
        for ci, (c0, csz) in enumerate(CH):
            do_chunk(s0, ci, c0, csz)

    def do_chunk(s0, ci, c0, csz):
        P.op("dve", lambda e: e.tensor_tensor(out=v3(xdt.t[:csz, :]), in0=v3(xs_tm.t[:csz, ci, :]),
                                              in1=bc(dt.t[:csz, ci, :]), op=ALU.mult),
             reads=[xs_tm.b, dt.b], writes=[xdt.b])
        P.op("pool", lambda e: e.tensor_tensor(out=v3(xdte.t[:csz, :]), in0=v3(xs_tm.t[:csz, ci, :]),
                                               in1=bc(wst.t[:csz, ci, :]), op=ALU.mult),
             reads=[xs_tm.b, wst.b], writes=[xdte.b])
        ps = cx.psum()
        P.op("pe", lambda e: e.matmul(ps.t[:csz, :csz], lhsT=xsT.t[:, 4, c0:c0 + csz], rhs=xsT.t[:, 5, c0:c0 + csz],
                                      start=True, stop=True), reads=[xsT.b], writes=[ps.b])
        P.op("dve", lambda e: e.tensor_tensor(out=cbT.t[:csz, :csz], in0=ps.t[:csz, :csz], in1=triI.t[:csz, :csz],
                                              op=ALU.mult), reads=[ps.b, triI.b], writes=[cbT.b])
        yp = pacc[ci % 2]
        l8, d8, m8 = lh8[ci % 2], dec8[ci % 2], MT8[ci % 2]
        P.op("dve", lambda e: e.tensor_tensor(
            out=l8.t[:csz, :, :csz], in0=mstr.t[:csz, :csz].unsqueeze(1).to_broadcast([csz, 8, csz]),
            in1=dta.t[:csz, ci, :].unsqueeze(2).to_broadcast([csz, 8, csz]), op=ALU.mult),
            reads=[mstr.b, dta.b], writes=[l8.b])
        pgs = [cx.psum(), cx.psum()]
        for hh in range(8):
            pg = pgs[hh // 4]
            P.op("pe", lambda e, pg=pg, hh=hh: e.matmul(pg.t[:csz, (hh % 4) * 128:(hh % 4) * 128 + csz],
                                                       lhsT=l8.t[:csz, hh, :csz], rhs=triI.t[:csz, :csz],
                                                       start=True, stop=True), reads=[l8.b, triI.b], writes=[pg.b])
        for g4 in range(2):
            pg = pgs[g4]
            P.op("act", lambda e, pg=pg, g4=g4: e.activation(
                out=d8.t[:csz, 4 * g4:4 * g4 + 4, :csz],
                in_=pg.t[:csz, :].rearrange("p (h s) -> p h s", h=4)[:, :, :csz], func=AF.Exp),
                reads=[pg.b], writes=[d8.b])
        P.op("dve", lambda e: e.tensor_tensor(
            out=m8.t[:csz, :, :csz], in0=d8.t[:csz, :, :csz],
            in1=cbT.t[:csz, :csz].unsqueeze(1).to_broadcast([csz, 8, csz]), op=ALU.mult),
            reads=[d8.b, cbT.b], writes=[m8.b])
        for hh in range(8):
            P.op("pe", lambda e, hh=hh: e.matmul(yp.t[:csz, hh * 64:(hh + 1) * 64], lhsT=m8.t[:csz, hh, :csz],
                                                rhs=xdt.t[:csz, hh * 64:(hh + 1) * 64], start=True, stop=True),
                 reads=[m8.b, xdt.b], writes=[yp.b])
        po_ = cx.psum()
        P.op("pe", lambda e: e.matmul(po_.t[:csz, :512], lhsT=xsT.t[:, 5, c0:c0 + csz], rhs=Sb.t[:, :],
                                      start=True, stop=True), reads=[xsT.b, Sb.b], writes=[po_.b])
        P.op("dve", lambda e: e.tensor_tensor(out=v3(y1.t[:csz, :]), in0=v3(po_.t[:csz, :512]),
                                              in1=bc(ecs.t[:csz, ci, :]), op=ALU.mult),
             reads=[po_.b, ecs.b], writes=[y1.b])
        P.op("dve", lambda e: e.tensor_tensor(out=y1.t[:csz, :], in0=yp.t[:csz, :512], in1=y1.t[:csz, :], op=ALU.add),
             reads=[yp.b, y1.b], writes=[y1.b])
        P.op("pool", lambda e: e.tensor_tensor(out=v3(y2.t[:csz, :]), in0=v3(xs_tm.t[:csz, ci, :]),
                                               in1=bc(dsk.t[:csz, :]), op=ALU.mult),
             reads=[xs_tm.b, dsk.b], writes=[y2.b])
        P.op("dve", lambda e: e.tensor_tensor(out=y1.t[:csz, :], in0=y1.t[:csz, :], in1=y2.t[:csz, :], op=ALU.add),
             reads=[y1.b, y2.b], writes=[y1.b])
        P.op("dve", lambda e: e.tensor_tensor(out=y1.t[:csz, :], in0=y1.t[:csz, :], in1=zs_tm.t[:csz, ci, :],
                                              op=ALU.mult), reads=[y1.b, zs_tm.b], writes=[y1.b])
        P.op("act", lambda e: e.activation(out=y2.t[:csz, :], in_=y1.t[:csz, :], func=AF.Square,
                                           accum_out=ss.t[:csz, 0:1]), reads=[y1.b], writes=[y2.b, ss.b])
        P.op("act", lambda e: e.activation(out=ss.t[:csz, 1:2], in_=ss.t[:csz, 0:1], func=AF.Ln,
                                           bias=cx.epst.t[:csz, :], scale=1.0 / 512), reads=[ss.b, cx.epst.b],
             writes=[ss.b])
        P.op("act", lambda e: e.activation(out=ss.t[:csz, 1:2], in_=ss.t[:csz, 1:2], func=AF.Exp, scale=-0.5),
             reads=[ss.b], writes=[ss.b])
        P.op("dve", lambda e: e.scalar_tensor_tensor(out=hgn.t[:csz, :], in0=y1.t[:csz, :], scalar=ss.t[:csz, 1:2],
                                                     in1=gg.t[:csz, :], op0=ALU.mult, op1=ALU.mult),
             reads=[y1.b, ss.b, gg.b], writes=[hgn.b])
        pt = ptr[ci % 2]
        hs = hst[cnt["h"] % 2]
        cnt["h"] += 1
        for jc in range(4):
            P.op("pe", lambda e, jc=jc: e.transpose(pt.t[:, jc * 128:jc * 128 + csz], hgn.t[:csz, jc * 128:(jc + 1) * 128],
                                                    ident.t[:csz, :csz]), reads=[hgn.b, ident.b], writes=[pt.b])
        P.op("act", lambda e: e.activation(out=hs.t[:, :, :csz],
                                           in_=pt.t[:, 0:512].rearrange("p (j c) -> p j c", j=4)[:, :, :csz],
                                           func=AF.Identity), reads=[pt.b], writes=[hs.b])
        si = s0 // W
        P.dma("sp", lambda e: e.dma_start(
            out=hgo[si][:, :, 4 + c0:4 + c0 + csz].rearrange("j p c -> p j c"), in_=hs.t[:, :, :csz]),
            reads=[hs.b], sembuf=hs.b)
        if c0 + csz == W and si < 3:
            P.dma("sp", lambda e: e.dma_start(
                out=hgo[si + 1][:, :, 0:4].rearrange("j p c -> p j c"), in_=hs.t[:, :, csz - 4:csz]),
                reads=[hs.b], sembuf=hs.b)
        pn = cx.psum()
        P.op("pe", lambda e: e.matmul(pn.t[:, :512], lhsT=B_tm.t[:csz, ci, :], rhs=xdte.t[:csz, :], start=True,
                                      stop=True), reads=[B_tm.b, xdte.b], writes=[pn.b])
        P.op("dve", lambda e: e.tensor_tensor(out=v3(S.t[:, :]), in0=v3(S.t[:, :]), in1=bc(cdec.t[:, ci, :]),
                                              op=ALU.mult), reads=[S.b, cdec.b], writes=[S.b])
        P.op("dve", lambda e: e.tensor_tensor(out=S.t[:, :], in0=pn.t[:, :512], in1=S.t[:, :], op=ALU.add),
             reads=[pn.b, S.b], writes=[S.b])
        P.op("pool", lambda e: e.tensor_copy(out=Sb.t[:, :], in_=S.t[:, :]), reads=[S.b], writes=[Sb.b])

    zt = P.sb("zt", [128, 4, 4], BF16)
    P.op("pool", lambda e: e.memset(zt.t[:], 0.0), writes=[zt.b])
    P.dma("sp", lambda e: e.dma_start(out=hgo[0][:, :, 0:4].rearrange("j p c -> p j c"), in_=zt.t[:]),
          reads=[zt.b], sembuf=zt.b)
    for si in range(4):
        do_seg(si)
        io["after_seg"](si, [h.b for h in hst] + [zt.b])
    return [h.b for h in hst] + [zt.b]


GROUPS = [[0, 1, 2, 3], [4, 5, 6, 7]]


def build_fused():
    nc = bass.Bass("TRN2", target_bir_lowering=False)

    def dr(n, s, dt=F32, k="ExternalInput"):
        return nc.dram_tensor(n, list(s), dt, kind=k)
    ioA = {"xT": dr("A_xT", [4, 5, 8, 128, 411]).ap(), "wsel": dr("A_wsel", [D, 1288]).ap(), "g_in": dr("A_g_in", [128, 8]).ap(),
           "cw": dr("A_cw", [128, 24]).ap(), "cb": dr("A_cb", [128, 6]).ap(), "dtb": dr("A_dtb", [128, 8]).ap(),
           "alog": dr("A_alog", [128, 8]).ap(), "dsk": dr("A_dsk", [128, 8]).ap(), "gg": dr("A_gg", [128, 512]).ap()}

    def tok_io(pfx, kcm):
        return {"w_mix": dr(pfx + "w_mix", [kcm * 128, D]).ap(), "w_up": dr(pfx + "w_up", [D, 2 * DFF]).ap(),
                "w_down": dr(pfx + "w_down", [DFF, D]).ap(), "g_ffn": dr(pfx + "g_ffn", [128, 8]).ap(),
                "cw": dr(pfx + "cw", [128, 132]).ap(), "cb": dr(pfx + "cb", [128, 44]).ap()}
    ioB = tok_io("B_", 16)
    ioB.update({"resid": dr("B_resid", [D, TQ + 4]).ap(), "w_kv": dr("B_w_kv", [D, 2 * D]).ap(),
                "w_q": dr("B_w_q", [D, D]).ap(), "g_kv": dr("B_g_kv", [128, 8]).ap(), "g_q": dr("B_g_q", [128, 8]).ap(),
                "hmask": dr("B_hmask", [128, 1]).ap()})
    ioD = tok_io("D_", 8)
    ioD.update({"g_fin": dr("D_g_fin", [128, 8]).ap(), "outo": dr("out", [D, TQ], F32, "ExternalOutput").ap()})
    idxd = dr("idx", [1, 1], I32).ap()

    C1, C3, CH2 = TQ + 4, TQ + 2, 128 * TQ
    b1 = nc.dram_tensor("b1", [4, 4, 128, C1], BF16)
    g1 = nc.dram_tensor("g1", [4, 4, 4, 128, C1], BF16)
    b2 = nc.dram_tensor("b2", [4, 6, 128, TQ], BF16)
    g2 = nc.dram_tensor("g2", [4, 6, 4, 128, TQ], BF16)
    b3 = nc.dram_tensor("b3", [4, 2, 128, C3], BF16)
    g3 = nc.dram_tensor("g3", [4, 2, 4, 128, C3], BF16)
    h1scr = nc.dram_tensor("h1scr", [D, TQ + 2], F32)
    dtap = nc.dram_tensor("dtap", [D, 1028], F32)
    sc1 = nc.dram_tensor("sc1", [4, 4, 128, C1], BF16)
    sc2 = nc.dram_tensor("sc2", [6, 4, 128, TQ], BF16)
    sc3 = nc.dram_tensor("sc3", [2, 4, 128, C3], BF16)

    P = Prog(nc)
    regs = {n: P.stack.enter_context(nc.gpsimd.register(n)) for n in ("ridx", "r1", "r2q", "r3", "rtmp")}
    it = P.sb("idxt", [1, 2], I32)
    scr = P.sb("scr", [1, 16], BF16)
    P.persist = P.off
    P.dma("pool", lambda e: e.dma_start(out=it.t[0:1, 0:1], in_=idxd), writes=[it.b])

    def setup(e):
        e.reg_load(regs["ridx"], it.t[0:1, 0:1])
        e.reg_mul(regs["r1"], regs["ridx"], 16 * 128 * C1)
        e.reg_mul(regs["r2q"], regs["ridx"], 24 * CH2)
        e.reg_mul(regs["r3"], regs["ridx"], 8 * 128 * C3)
        return e.memset(scr.t[:], 0.0)
    P.op("pool", setup, reads=[it.b], writes=[scr.b])

    def pull(sct, gt, reg, nrows, ncols):
        b = P.buf("sc")
        P.dma("pool", lambda e: e.dma_start(out=sct.ap().rearrange("a b c d -> (a b c) d"),
                                            in_=bass.AP(gt, reg, [[ncols, nrows], [1, ncols]])), writes=[b])
        return b

    def gather_dest(bt, gt, d, n1, ob):
        for c in range(n1):
            P.collective("AllGather", bt.ap()[d, c].opt(), gt.ap()[d, c].opt(), GROUPS, ob if c == 0 else [])

    def gather_all(bt, gt, n0, n1, ob):
        for d in range(n0):
            gather_dest(bt, gt, d, n1, ob if d == 0 else [])
        P.collective_wait()

    import os
    upto = int(os.environ.get("FUSE_UPTO", "4"))
    nocc = os.environ.get("FUSE_NOCC", "0") == "1"
    if nocc:
        P.collective = lambda *a, **k: None

    def finish():
        dbg = os.environ.get("FUSE_DEBUG", "")
        if dbg:
            src = {"h1scr": h1scr, "sc1": sc1, "sc2": sc2, "sc3": sc3, "b1": b1, "b2": b2, "b3": b3, "dtap": dtap}[dbg]
            shp = list(src.ap().shape)
            n = 1
            for d_ in shp[:-1]:
                n *= d_
            dt_ = F32 if dbg in ("h1scr", "dtap") else BF16
            dbo = nc.dram_tensor("dbg", [n, shp[-1]], dt_, kind="ExternalOutput")
            P.barrier()
            bb = P.buf("dbg")
            names = "abcdefg"[:len(shp) - 1]
            view = src.ap() if len(shp) == 2 else src.ap().rearrange(" ".join(names) + " z -> (" + " ".join(names) + ") z")
            P.dma("sp", lambda e: e.dma_start(out=dbo.ap(), in_=view), writes=[bb])
            P.wait_all("sp", [bb])
        P.barrier()
        print("sems", len(P.sems), "instr", {e: len(q) for e, q in P.q.items()})
        P.emit()
        P.close()
        return nc
    P.phase_start()
    ioA["b1"] = b1.ap()
    ioA["after_seg"] = lambda si, ob: gather_dest(b1, g1, si, 4, ob)
    phase_ssd(P, ioA)
    P.collective_wait()
    if upto == 1:
        return finish()
    P.phase_start()
    ioB.update({"sc": sc1.ap(), "scb": pull(sc1, g1, regs["r1"], 16 * 128, C1), "h1scr": h1scr.ap(),
                "b2": b2.ap(), "b2h": b2})
    ob = phase_token(P, "B", ioB)
    gather_all(b2, g2, 4, 6, ob)
    if upto == 2:
        return finish()
    P.phase_start()
    ob = phase_attn(P, {"b3": b3.ap(), "sc": sc2.ap(), "sch": sc2, "scb": pull(sc2, g2, regs["r2q"], 24 * 128, TQ)})
    gather_all(b3, g3, 4, 2, ob)
    if upto == 3:
        return finish()
    P.phase_start()
    ioD.update({"dtap": dtap.ap(), "sc": sc3.ap(), "scb": pull(sc3, g3, regs["r3"], 8 * 128, C3), "resid": h1scr.ap()})
    ob = phase_token(P, "D", ioD)
    P.wait_all("sp", ob)
    return finish()


_NC_CACHE = {}


def get_nc(key, fn, *a):
    if key not in _NC_CACHE:
        _NC_CACHE[key] = fn(*a)
    return _NC_CACHE[key]


def fm(v, n):
    return np.ascontiguousarray(np.asarray(v, np.float32).reshape(n, 128).T)


def _halo_cols(full, s, halo, W):
    out = np.zeros((full.shape[0], halo + W), full.dtype)
    lo = max(0, s - halo)
    out[:, lo - (s - halo):] = full[:, lo:s + W]
    return out


def _ffn_params(inp, layer, pfx):
    cwT = np.ascontiguousarray(np.asarray(inp["ffn_conv_w"][layer], np.float32).T.reshape(44, 128, 3)
                               .transpose(1, 0, 2).reshape(128, 132))
    return {pfx + "w_up": np.asarray(inp["ffn_w_up"][layer], np.float32),
            pfx + "w_down": np.asarray(inp["ffn_w_down"][layer], np.float32),
            pfx + "g_ffn": fm(inp["ffn_norm"][layer], 8), pfx + "cw": cwT, pfx + "cb": fm(inp["ffn_conv_b"][layer], 44)}


def kernel(**inp):
    inp = {k: np.asarray(v) for k, v in inp.items()}
    x = inp["x"].astype(np.float32)
    nb = x.shape[0]
    h0 = np.concatenate([np.broadcast_to(inp["meta_tokens"][None].astype(np.float32), (nb, 16, D)), x], axis=1)
    h0T = [np.ascontiguousarray(h0[b].T) for b in range(nb)]
    cores = list(range(8))
    nc = get_nc("F", build_fused)
    mA = ssd_maps(inp, h0)
    fB = _ffn_params(inp, 0, "B_")
    fD = _ffn_params(inp, 1, "D_")
    maps = []
    for c in cores:
        b, i = divmod(c, 4)
        m = {"A_" + k: v for k, v in mA[c].items()}
        m.update(fB)
        m.update(fD)
        m.update({"B_resid": _halo_cols(h0T[b], i * TQ, 4, TQ), "B_w_mix": np.ascontiguousarray(np.asarray(inp["ssd_w_out"][0], np.float32)
                                                   .reshape(4, 4, 128, D).transpose(1, 0, 2, 3).reshape(DI, D)),
                  "B_w_kv": np.asarray(inp["w_kv"], np.float32), "B_w_q": np.asarray(inp["sb_w_q"][0], np.float32),
                  "B_g_kv": fm(inp["kv_norm"], 8), "B_g_q": fm(inp["sb_norm"][0], 8),
                  "B_hmask": np.full((128, 1), 0.0 if i == 0 else 1.0, np.float32),
                  "D_w_mix": np.ascontiguousarray(np.asarray(inp["sb_w_o"][0], np.float32)
                                                   .reshape(4, 2, 128, D).transpose(1, 0, 2, 3).reshape(D, D)), "D_g_fin": fm(inp["final_norm"], 8),
                  "idx": np.array([[i]], np.int32)})
        maps.append(m)
    res = run_bass_kernel_spmd(nc, maps, core_ids=cores).results
    out = np.empty((nb, LB - 16, D), np.float32)
    for b in range(nb):
        full = np.concatenate([res[b * 4 + t]["out"] for t in range(4)], axis=1)
        out[b] = full[:, 16:].T
    return out


def ssd_maps(inp, h0):
    w_in = inp["ssd_w_in"][0]
    cwf = inp["ssd_conv_w"][0]
    cbf = inp["ssd_conv_b"][0]
    maps = []
    for c in range(8):
        b, g = divmod(c, 4)
        cols = np.concatenate([np.arange(512 * g, 512 * g + 512), 2048 + np.arange(512 * g, 512 * g + 512),
                               4096 + np.arange(128 * g, 128 * g + 128), 4608 + np.arange(128 * g, 128 * g + 128),
                               5120 + np.arange(8 * g, 8 * g + 8)])
        cch = np.concatenate([np.arange(512 * g, 512 * g + 512), 2048 + np.arange(128 * g, 128 * g + 128),
                              2560 + np.arange(128 * g, 128 * g + 128)])
        xpad = np.zeros((D, 3 + LB), np.float32)
        xpad[:, 3:] = h0[b].T
        xT = np.empty((4, 5, 8, 128, 411), np.float32)
        for si_ in range(4):
            for ti_ in range(5):
                c0_ = si_ * TQ + ti_ * 411
                xT[si_, ti_] = xpad[:, c0_:c0_ + 411].reshape(8, 128, 411)
        rep = lambda v: np.ascontiguousarray(np.broadcast_to(np.asarray(v, np.float32)[None, :], (128, len(v))))
        maps.append({
            "xT": xT, "wsel": np.ascontiguousarray(w_in[:, cols]), "g_in": fm(inp["ssd_norm"][0], 8),
            "cw": np.ascontiguousarray(cwf[:, cch].T.reshape(6, 128, 4).transpose(1, 0, 2).reshape(128, 24)),
            "cb": fm(cbf[cch], 6), "dtb": rep(inp["ssd_dt_bias"][0][8 * g:8 * g + 8]),
            "alog": rep(inp["ssd_a_log"][0][8 * g:8 * g + 8]), "dsk": rep(inp["ssd_d_skip"][0][8 * g:8 * g + 8]),
            "gg": rep(inp["ssd_gate_norm"][0][512 * g:512 * g + 512]),
        })
    return maps
```

```python
import numpy as np
import ml_dtypes
from contextlib import ExitStack
import concourse.bass as bass
import concourse.mybir as mybir
from concourse.bass_utils import run_bass_kernel_spmd

F32 = mybir.dt.float32
BF16 = mybir.dt.bfloat16
AF = mybir.ActivationFunctionType
ALU = mybir.AluOpType
AX = mybir.AxisListType
NPBF = ml_dtypes.bfloat16

D = 1024
LB = 8208
TQ = 2052
DI = 2048
DFF = 2816
EPS = 1e-6
EPOCH = 30000


I32 = mybir.dt.int32
ARENA = 106400
ISZ = {F32: 4, BF16: 2, I32: 4}


class Buf:
    __slots__ = ("name", "w", "r", "dsem", "dcnt")

    def __init__(self, name):
        self.name = name
        self.w = None
        self.r = {}
        self.dsem = None
        self.dcnt = 0


class T:
    __slots__ = ("t", "b")

    def __init__(self, t, b):
        self.t = t
        self.b = b


class Prog:
    ENGS = ("pe", "act", "dve", "pool", "sp")
    EMAP = {"pe": "tensor", "act": "scalar", "dve": "vector", "pool": "gpsimd", "sp": "sync"}

    def __init__(self, nc):
        self.nc = nc
        self.q = {e: [] for e in self.ENGS}
        self.cnt = {e: 0 for e in self.ENGS}
        self.seen = {e: {} for e in self.ENGS}
        self.sems = {}
        self.latest = {}
        self.stack = ExitStack()
        self.nbuf = 0
        self.arena = self.stack.enter_context(nc.sbuf_tensor("arena", [128, ARENA], BF16))
        self.banks = [self.stack.enter_context(nc.psum_tensor(f"bank{i}", [128, 512], F32)) for i in range(8)]
        self.off = 0
        self.persist = 0
        self.nbank = 0
        self.ncc = 0
        self.ccpend = []
        self.dfree = {"sw": [], "hw": []}
        self.dlive = []

    def _sem(self, key):
        if key not in self.sems:
            self.sems[key] = self.stack.enter_context(self.nc.semaphore("s_" + key.replace("#", "_")))
        return self.sems[key]

    def buf(self, name=None):
        self.nbuf += 1
        return Buf(f"{name or 'b'}{self.nbuf}")

    def sb(self, name, shape, dtype, stack=None):
        shape = list(shape)
        n = 1
        for d in shape[1:]:
            n *= d
        nel = (n * ISZ[dtype] + 1) // 2
        nel = (nel + 15) // 16 * 16
        assert self.off + nel <= ARENA, f"SBUF arena overflow at {name}: {self.off}+{nel}"
        v = self.arena[0:shape[0], self.off:self.off + n * ISZ[dtype] // 2]
        self.off += nel
        if dtype != BF16:
            v = v.bitcast(dtype)
        if len(shape) == 3:
            v = v.rearrange("p (a b) -> p a b", a=shape[1])
        return T(v, self.buf(name))

    def psum(self, name, shape, dtype=F32, stack=None):
        assert self.nbank < 8, "out of PSUM banks"
        bk = self.banks[self.nbank]
        self.nbank += 1
        v = bk[:, :]
        if dtype == BF16:
            v = v.bitcast(BF16)
        return T(v, self.buf(name))

    def _waits(self, eng, reads, writes):
        need = {}

        def add(k, v):
            if need.get(k, 0) < v:
                need[k] = v
        for b in reads:
            if b.w:
                add(*b.w)
        for b in writes:
            if b.w:
                add(*b.w)
            for k, v in b.r.items():
                add(k, v)
        if eng == "pe":
            for k in [k for k in need if k.startswith("pe#")]:
                del need[k]
        out = []
        seen = self.seen[eng]
        for k, v in need.items():
            if seen.get(k, 0) < v:
                seen[k] = v
                out.append((k, v))
        return out

    def _mark(self, ev, reads, writes):
        k, v = ev
        if self.latest.get(k, 0) < v:
            self.latest[k] = v
        for b in reads:
            if b.r.get(k, 0) < v:
                b.r[k] = v
        for b in writes:
            b.w = ev
            b.r = {}

    def op(self, eng, fn, reads=(), writes=(), sig=True):
        waits = self._waits(eng, reads, writes)
        c = self.cnt[eng]
        key = f"{eng}#{c // EPOCH}"
        self._sem(key)
        ev = (key, c % EPOCH + 1)
        if sig:
            self.cnt[eng] = c + 1
            self.q[eng].append((waits, fn, (key, 1)))
        else:
            assert eng == "pe"
            self.q[eng].append((waits, fn, None))
        self._mark(ev, reads, writes)
        return ev

    def dma(self, eng, fn, reads=(), writes=(), sembuf=None):
        waits = self._waits(eng, reads, writes)
        sb = sembuf or (writes[0] if writes else reads[0])
        if sb.dsem is None or sb.dcnt >= EPOCH:
            cls = "sw" if eng == "pool" else "hw"
            fl = self.dfree[cls]
            while fl and self.latest.get(fl[-1], 0) >= EPOCH - 4096:
                fl.pop()
            if fl:
                sb.dsem = fl.pop()
                sb.dcnt = self.latest.get(sb.dsem, 0)
            else:
                sb.dsem = f"d{cls}{len(self.sems)}"
                sb.dcnt = 0
                self._sem(sb.dsem)
            self.dlive.append((cls, sb.dsem))
        sb.dcnt += 16
        ev = (sb.dsem, sb.dcnt)
        self.q[eng].append((waits, fn, (sb.dsem, 16)))
        self._mark(ev, reads, writes)
        return ev

    def wait_all(self, eng, bufs):
        waits = self._waits(eng, bufs, bufs)
        self.q[eng].append((waits, None, None))

    def barrier(self):
        for e in self.ENGS:
            seen = self.seen[e]
            waits = []
            for k, v in self.latest.items():
                if seen.get(k, 0) < v:
                    seen[k] = v
                    waits.append((k, v))
            self.q[e].append((waits, None, None))

    def phase_start(self):
        self.barrier()
        self.off = self.persist
        self.nbank = 0
        for cls, k in self.dlive:
            self.dfree[cls].append(k)
        self.dlive = []

    def collective(self, kind, in_ap, out_ap, groups, wait_bufs):
        waits = self._waits("pool", wait_bufs, wait_bufs)
        key = f"cc{self.ncc}"
        self.ncc += 1
        self._sem(key)
        self.q["pool"].append((waits, lambda e: e.collective_compute(kind, ALU.bypass, replica_groups=groups,
                                                                    ins=[in_ap], outs=[out_ap]), (key, 1)))
        self.latest[key] = 1
        self.ccpend.append(key)

    def collective_wait(self):
        waits = [(k, 1) for k in self.ccpend if self.seen["pool"].get(k, 0) < 1]
        for k, _ in waits:
            self.seen["pool"][k] = 1
        self.ccpend = []
        self.q["pool"].append((waits, None, None))

    def emit(self):
        nc = self.nc
        sems = self.sems
        with nc.Block() as block:
            for e in self.ENGS:
                items = self.q[e]

                def body(engine, items=items):
                    for waits, fn, inc in items:
                        for k, v in waits:
                            engine.wait_ge(sems[k], v)
                        if fn is not None:
                            ins = fn(engine)
                            if inc is not None:
                                ins.then_inc(sems[inc[0]], inc[1])
                getattr(block, self.EMAP[e])(body)

    def close(self):
        self.stack.close()


def tiles(width, maxw=512):
    n = -(-width // maxw)
    base, rem = divmod(width, n)
    out, o = [], 0
    for i in range(n):
        w = base + (1 if i < rem else 0)
        out.append((o, w))
        o += w
    return out


def chunks128(width):
    out, o = [], 0
    while o < width:
        w = min(128, width - o)
        out.append((o, w))
        o += w
    return out


class Ctx:
    def __init__(self, P, wslot_elems=4096, nwslots=3, nps=8):
        self.nc = P.nc
        self.P = P
        self.ps = [P.psum(f"ps{i}", [128, 512]) for i in range(nps)]
        self.psi = 0
        self.wslots = [P.sb(f"wsl{i}", [128, wslot_elems], BF16) for i in range(nwslots)]
        self.wsi = 0
        self.wslot_elems = wslot_elems
        self.ones = P.sb("ones_f", [128, 128], F32)
        P.op("pool", lambda e: e.memset(self.ones.t[:], 1.0), writes=[self.ones.b])
        self.epst = P.sb("epst", [128, 1], F32)
        P.op("pool", lambda e: e.memset(self.epst.t[:], EPS), writes=[self.epst.b])
        self.onesb = P.sb("ones_b", [128, 128], BF16)
        P.op("pool", lambda e: e.memset(self.onesb.t[:], 1.0), writes=[self.onesb.b])
        self.sq = [P.sb(f"sq{i}", [128, 512], BF16) for i in range(4)]
        self.rs = [P.sb(f"rs{i}", [128, 512], F32) for i in range(2)]
        self.sqi = 0
        self.rsi = 0

    def psum(self):
        p = self.ps[self.psi % len(self.ps)]
        self.psi += 1
        return p

    def wslot(self):
        w = self.wslots[self.wsi % len(self.wslots)]
        self.wsi += 1
        return w

    def load_w(self, w_ap, k0, kc, n0, ncols):
        assert kc * ncols <= self.wslot_elems, (kc, ncols)
        sl = self.wslot()
        view = sl.t[:, 0:kc * ncols].rearrange("p (k n) -> p k n", k=kc)
        src = w_ap[k0:k0 + kc * 128, n0:n0 + ncols].rearrange("(k p) n -> p k n", p=128)
        self.P.dma("pool", lambda e: e.dma_start(out=view, in_=src), writes=[sl.b])
        return view, sl.b


def load_small(cx, name, dram_ap, shape, dtype=F32):
    t = cx.P.sb(name, shape, dtype)
    cx.P.dma("sp", lambda e: e.dma_start(out=t.t[:], in_=dram_ap), writes=[t.b])
    return t


def rmsnorm_fm(cx, src, g, dst, c0, width, dst_c0=0, kc=8):
    P = cx.P
    for (t0, tw) in tiles(width):
        ps = cx.psum()
        for k in range(kc):
            sq = cx.sq[cx.sqi % 4]
            cx.sqi += 1
            sl = src.t[:, k, c0 + t0:c0 + t0 + tw]
            P.op("pool", lambda e, sq=sq, sl=sl, tw=tw: e.tensor_tensor(out=sq.t[:, :tw], in0=sl, in1=sl, op=ALU.mult),
                 reads=[src.b], writes=[sq.b])
            P.op("pe", lambda e, ps=ps, sq=sq, tw=tw, k=k: e.matmul(ps.t[:, :tw], lhsT=cx.onesb.t[:], rhs=sq.t[:, :tw],
                                                              start=(k == 0), stop=(k == kc - 1)),
                 reads=[cx.onesb.b, sq.b], writes=[ps.b])
        rs = cx.rs[cx.rsi % 2]
        cx.rsi += 1
        P.op("act", lambda e, rs=rs, ps=ps, tw=tw: e.activation(out=rs.t[:, :tw], in_=ps.t[:, :tw], func=AF.Ln,
                                                             bias=cx.epst.t[:], scale=1.0 / (128 * kc)),
             reads=[ps.b, cx.epst.b], writes=[rs.b])
        P.op("act", lambda e, rs=rs, tw=tw: e.activation(out=rs.t[:, :tw], in_=rs.t[:, :tw], func=AF.Exp, scale=-0.5),
             reads=[rs.b], writes=[rs.b])
        for k in range(kc):
            P.op("dve", lambda e, k=k, rs=rs, t0=t0, tw=tw: e.scalar_tensor_tensor(
                out=dst.t[:, k, dst_c0 + t0:dst_c0 + t0 + tw], in0=src.t[:, k, c0 + t0:c0 + t0 + tw],
                scalar=g.t[:, k:k + 1], in1=rs.t[:, :tw], op0=ALU.mult, op1=ALU.mult),
                reads=[src.b, g.b, rs.b], writes=[dst.b])


def proj_fm(cx, w_ap, kc, n0, ncols, uT, c0, width, evac, ngroup=None):
    P = cx.P
    gcols = ngroup or max(128, (cx.wslot_elems // kc) // 128 * 128)
    gcols = min(gcols, 512)
    tl = tiles(width)
    for g0 in range(0, ncols, gcols):
        gc = min(gcols, ncols - g0)
        wv, wb = cx.load_w(w_ap, 0, kc, n0 + g0, gc)
        for jj, (j0, nsz) in enumerate(chunks128(gc)):
            j = (g0 + j0) // 128
            for (t0, tw) in tl:
                ps = cx.psum()
                for k in range(kc):
                    P.op("pe", lambda e, ps=ps, k=k, j0=j0, nsz=nsz, t0=t0, tw=tw, wv=wv: e.matmul(
                        ps.t[:nsz, :tw], lhsT=wv[:, k, j0:j0 + nsz], rhs=uT.t[:, k, c0 + t0:c0 + t0 + tw],
                        start=(k == 0), stop=(k == kc - 1)), reads=[wb, uT.b], writes=[ps.b], sig=(k == kc - 1))
                evac(ps, j, nsz, t0, tw)


def proj_tm(cx, w_ap, kc, n0, ncols, uT, c0, width, evac):
    P = cx.P
    gcols = min(512, max(1, (cx.wslot_elems // kc)))
    ch = chunks128(width)
    for g0 in range(0, ncols, gcols):
        gc = min(gcols, ncols - g0)
        wv, wb = cx.load_w(w_ap, 0, kc, n0 + g0, gc)
        for ci, (t0, csz) in enumerate(ch):
            ps = cx.psum()
            for k in range(kc):
                P.op("pe", lambda e, ps=ps, k=k, t0=t0, csz=csz, gc=gc, wv=wv: e.matmul(
                    ps.t[:csz, :gc], lhsT=uT.t[:, k, c0 + t0:c0 + t0 + csz], rhs=wv[:, k, 0:gc],
                    start=(k == 0), stop=(k == kc - 1)), reads=[wb, uT.b], writes=[ps.b], sig=(k == kc - 1))
            evac(ps, ci, t0, csz, g0, gc)


def conv_fm(cx, pre, w_t, b_t, j, taps, wo, acc):
    P = cx.P
    kl = taps - 1
    P.op("dve", lambda e: e.tensor_scalar(out=acc.t[:, :wo], in0=pre.t[:, kl:kl + wo], scalar1=w_t.t[:, j, kl:kl + 1],
                                          scalar2=b_t.t[:, j:j + 1], op0=ALU.mult, op1=ALU.add),
         reads=[pre.b, w_t.b, b_t.b], writes=[acc.b])
    for k in range(taps - 1):
        P.op("dve", lambda e, k=k: e.scalar_tensor_tensor(out=acc.t[:, :wo], in0=pre.t[:, k:k + wo],
                                                        scalar=w_t.t[:, j, k:k + 1], in1=acc.t[:, :wo],
                                                        op0=ALU.mult, op1=ALU.add),
             reads=[pre.b, w_t.b, acc.b], writes=[acc.b])


def ffn_fm(cx, hm, uT, w_up, w_down, cw, cb, wh, halo, actT, pre, acc):
    P = cx.P
    for j in range(22):
        pg, pv = pre[(2 * j) % 4], pre[(2 * j + 1) % 4]
        ag, av = acc[(2 * j) % 4], acc[(2 * j + 1) % 4]

        def ev_g(ps, jj, nsz, t0, tw, pg=pg):
            P.op("act", lambda e: e.activation(out=pg.t[:, t0:t0 + tw], in_=ps.t[:, :tw], func=AF.Identity),
                 reads=[ps.b], writes=[pg.b])

        def ev_v(ps, jj, nsz, t0, tw, pv=pv):
            P.op("act", lambda e: e.activation(out=pv.t[:, t0:t0 + tw], in_=ps.t[:, :tw], func=AF.Identity),
                 reads=[ps.b], writes=[pv.b])
        proj_fm(cx, w_up, 8, j * 128, 128, uT, 0, wh + 2, ev_g)
        proj_fm(cx, w_up, 8, DFF + j * 128, 128, uT, 0, wh + 2, ev_v)
        conv_fm(cx, pg, cw, cb, j, 3, wh, ag)
        conv_fm(cx, pv, cw, cb, 22 + j, 3, wh, av)
        P.op("act", lambda e, ag=ag: e.activation(out=ag.t[:, :wh], in_=ag.t[:, :wh], func=AF.Silu),
             reads=[ag.b], writes=[ag.b])
        P.op("dve", lambda e, ag=ag, av=av, j=j: e.tensor_tensor(out=actT.t[:, j, :wh], in0=ag.t[:, :wh],
                                                                in1=av.t[:, :wh], op=ALU.mult),
             reads=[ag.b, av.b], writes=[actT.b])

    def ev_d(ps, j, nsz, t0, tw):
        sl = hm.t[:, j, halo + t0:halo + t0 + tw]
        P.op("dve", lambda e: e.tensor_tensor(out=sl, in0=ps.t[:, :tw], in1=sl, op=ALU.add),
             reads=[ps.b, hm.b], writes=[hm.b])
    proj_fm(cx, w_down, 22, 0, D, actT, 0, wh, ev_d, ngroup=128)


def phase_token(P, kind, io):
    kcm = 16 if kind == "B" else 8
    HIN = 4 if kind == "B" else 2
    WIN = TQ + HIN
    WOUT = WIN - 2
    HW = WOUT // 2
    HWH = HW + 2
    cx = Ctx(P)
    outbufs = []
    hm = P.sb("hm", [128, 8, HWH], F32)
    uT = P.sb("uT", [128, 8, HWH], BF16)
    arena = P.sb("tkar", [128, max(kcm * HWH, 22 * HW)], BF16)
    opT = T(arena.t[:, 0:kcm * HWH].rearrange("p (k w) -> p k w", k=kcm), arena.b)
    actT = T(arena.t[:, 0:22 * HW].rearrange("p (k w) -> p k w", k=22), arena.b)
    pre = [P.sb(f"pre{i}", [128, HWH], F32) for i in range(4)]
    acc = [P.sb(f"acc{i}", [128, HW], F32) for i in range(4)]
    gf = load_small(cx, "gf", io["g_ffn"], [128, 8])
    cw = P.sb("cw", [128, 44, 3], F32)
    P.dma("sp", lambda e: e.dma_start(out=cw.t[:].rearrange("p a b -> p (a b)"), in_=io["cw"]), writes=[cw.b])
    cb = load_small(cx, "cb", io["cb"], [128, 44])
    stg_i = [0]
    resid, w_mix, w_up, w_down = io["resid"], io["w_mix"], io["w_up"], io["w_down"]
    sc, scb = io["sc"], io["scb"]
    if kind == "B":
        gkv = load_small(cx, "gkv", io["g_kv"], [128, 8])
        gq = load_small(cx, "gq", io["g_q"], [128, 8])
        hmask = load_small(cx, "hmask", io["hmask"], [128, 1])
        stg = [P.sb(f"stg{i}", [128, 512], BF16) for i in range(4)]
        outbufs += [s.b for s in stg] + [hm.b]
        w_kv, w_q, h1scr, b2, b2h = io["w_kv"], io["w_q"], io["h1scr"], io["b2"], io["b2h"]
    else:
        gfin = load_small(cx, "gfin", io["g_fin"], [128, 8])
        fo = P.sb("fo", [128, 8, 512], F32)
        outbufs.append(fo.b)
        outo = io["outo"]

    def do_half(a, first):
        for k in range(8):
            P.dma("sp", lambda e, k=k: e.dma_start(out=hm.t[:, k, :], in_=resid[k * 128:(k + 1) * 128, a:a + HWH]),
                  writes=[hm.b])

        for pt in range(kcm // 4):
            P.dma("sp", lambda e, pt=pt: e.dma_start(
                out=opT.t[:, pt * 4:(pt + 1) * 4, :], in_=sc[pt][:, :, a:a + HWH].rearrange("r p c -> p r c")),
                reads=[scb], writes=[opT.b])

        def tap(n):
            import os
            if kind == "D" and first and os.environ.get("FUSE_DTAP", "") == str(n):
                for k in range(8):
                    P.dma("sp", lambda e, k=k: e.dma_start(out=io["dtap"][k * 128:(k + 1) * 128, 0:HWH], in_=hm.t[:, k, :]),
                          reads=[hm.b], sembuf=hm.b)
        tap(1)

        def ev_mix(ps, j, nsz, t0, tw):
            sl = hm.t[:, j, t0:t0 + tw]
            P.op("dve", lambda e: e.tensor_tensor(out=sl, in0=ps.t[:, :tw], in1=sl, op=ALU.add),
                 reads=[ps.b, hm.b], writes=[hm.b])
        proj_fm(cx, w_mix, kcm, 0, D, opT, 0, HWH, ev_mix, ngroup=256 if kcm == 16 else 512)
        tap(2)
        rmsnorm_fm(cx, hm, gf, uT, 0, HWH)
        ffn_fm(cx, hm, uT, w_up, w_down, cw, cb, HW, 2, actT, pre, acc)
        tap(3)

        if kind == "B":
            if first:
                P.op("dve", lambda e: e.tensor_scalar(out=hm.t[:, :, 2:4], in0=hm.t[:, :, 2:4],
                                                      scalar1=hmask.t[:, 0:1], scalar2=None, op0=ALU.mult),
                     reads=[hm.b, hmask.b], writes=[hm.b])
            for k in range(8):
                P.dma("sp", lambda e, k=k: e.dma_start(out=h1scr[k * 128:(k + 1) * 128, a:a + HW],
                                                       in_=hm.t[:, k, 2:2 + HW]), reads=[hm.b], sembuf=hm.b)
            skip = 2 if first else 0
            c0 = 2 + skip
            wk = HW - skip
            rel0 = a + c0 - 4
            rmsnorm_fm(cx, hm, gkv, uT, c0, wk)

            def mk_ev(row0, scale):
                def ev(ps, j, nsz, t0, tw):
                    s = stg[stg_i[0] % 4]
                    stg_i[0] += 1
                    P.op("act", lambda e: e.activation(out=s.t[:, :tw], in_=ps.t[:, :tw], func=AF.Copy, scale=scale),
                         reads=[ps.b], writes=[s.b])
                    P.dma("sp", lambda e: e.dma_start(
                        out=b2[j // 2, row0 + j % 2][:, rel0 + t0:rel0 + t0 + tw], in_=s.t[:, :tw]),
                        reads=[s.b], sembuf=s.b)
                return ev
            proj_fm(cx, w_kv, 8, 0, D, uT, 0, wk, mk_ev(2, 1.0))

            def ev_v(ps, ci, t0, csz, n_off, nw):
                s = stg[stg_i[0] % 4]
                stg_i[0] += 1
                P.op("dve", lambda e: e.tensor_copy(out=s.t[:csz, :nw], in_=ps.t[:csz, :nw]),
                     reads=[ps.b], writes=[s.b])
                for u in range(nw // 128):
                    f0 = n_off + u * 128
                    off = ((f0 // 256) * 6 + 4 + (f0 % 256) // 128) * 128 * TQ + (rel0 + t0) * 128
                    P.dma("sp", lambda e, u=u, off=off: e.dma_start(
                        out=bass.AP(b2h, off, [[128, csz], [1, 128]]), in_=s.t[:csz, u * 128:(u + 1) * 128]),
                        reads=[s.b], sembuf=s.b)
            proj_tm(cx, w_kv, 8, D, D, uT, 0, wk, ev_v)
            rmsnorm_fm(cx, hm, gq, uT, c0, wk)
            proj_fm(cx, w_q, 8, 0, D, uT, 0, wk, mk_ev(0, 0.125))
        else:
            for (t0, tw) in tiles(HW):
                fin_tile(a, t0, tw)

    def fin_tile(a, t0, tw):
        rmsnorm_fm(cx, hm, gfin, fo, 2 + t0, tw)
        for k in range(8):
            P.dma("sp", lambda e, k=k: e.dma_start(out=outo[k * 128:(k + 1) * 128, a + t0:a + t0 + tw],
                                                   in_=fo.t[:, k, :tw]), reads=[fo.b], sembuf=fo.b)

    do_half(0, True)
    do_half(HW, False)
    return outbufs


def phase_attn(P, io):
    oTd = io["b3"]
    sc, sch, scb = io["sc"], io["sch"], io["scb"]
    NB = 65
    kT = P.sb("kT", [128, 4, LB], BF16)
    vS = P.sb("vS", [128, NB, 320], BF16)
    qt = [P.sb(f"qt{i}", [128, 4, 512], BF16) for i in range(2)]
    spA = P.sb("spA", [128, NB, 512], BF16)
    wt = [P.sb(f"wt{i}", [128, 1024], BF16) for i in range(2)]
    tmp = [P.sb(f"tmp{i}", [128, 1024], F32) for i in range(2)]
    carry = P.sb("carry", [128, 512], F32)
    ost = [P.sb(f"ost{i}", [64, 512], BF16) for i in range(2)]
    negU = P.sb("negU", [128, 128], BF16)
    onesb = P.sb("onesb", [128, 128], BF16)
    masks = [P.sb(f"mask{i}", [128, 512], BF16) for i in range(4)]
    ps1 = [P.psum(f"ps1_{i}", [128, 512]) for i in range(2)]
    ps2 = [P.psum(f"ps2_{i}", [128, 512]) for i in range(2)]
    pcb = [P.psum(f"pcb{i}", [128, 512]) for i in range(2)]
    po = [P.psum(f"po{i}", [128, 512]) for i in range(2)]

    P.op("pool", lambda e: e.memset(onesb.t[:], 1.0), writes=[onesb.b])
    P.op("pool", lambda e: e.memset(negU.t[:], -1.0), writes=[negU.b])
    P.op("pool", lambda e: e.memset(kT.t[64:128, :, :], 0.0), writes=[kT.b])
    P.op("pool", lambda e: e.memset(vS.t[:, :, 256:320], 0.0), writes=[vS.b])
    for qq in qt:
        P.op("pool", lambda e, qq=qq: e.memset(qq.t[64:128, :, :], 0.0), writes=[qq.b])
    P.op("pool", lambda e: e.affine_select(out=negU.t[:], in_=negU.t[:], pattern=[[-1, 128]], compare_op=ALU.is_ge,
                                           fill=0.0, base=0, channel_multiplier=1), reads=[negU.b], writes=[negU.b])
    for i in range(4):
        P.op("pool", lambda e, i=i: e.memset(masks[i].t[:], 1.0), writes=[masks[i].b])
        P.op("pool", lambda e, i=i: e.affine_select(out=masks[i].t[:], in_=masks[i].t[:], pattern=[[1, 512]],
                                                    compare_op=ALU.is_gt, fill=0.0, base=-128 * i,
                                                    channel_multiplier=-1), reads=[masks[i].b], writes=[masks[i].b])
    CH = 128 * TQ
    for X in range(2):
        for r in range(4):
            P.dma("sp", lambda e, X=X, r=r: e.dma_start(
                out=kT.t[0:64, 2 * X:2 * X + 2, r * TQ:(r + 1) * TQ],
                in_=sc[2 + X, r].rearrange("(two p) c -> p two c", p=64)), reads=[scb], writes=[kT.b])
    zt = P.sb("zt", [128, 2, 2], BF16)
    P.op("pool", lambda e: e.memset(zt.t[:], 0.0), writes=[zt.b])
    P.dma("sp", lambda e: e.dma_start(out=oTd[0][:, :, 0:2].rearrange("j p c -> p j c"), in_=zt.t[:]),
          reads=[zt.b], sembuf=zt.b)
    def ldv_piece(X, r, lrow, nrow, blk, p0, nblk):
        off = ((4 + X) * 4 + r) * CH + lrow * 128
        if nblk:
            P.dma("sp", lambda e: e.dma_start(out=vS.t[:, blk:blk + nblk, X * 128:(X + 1) * 128],
                                              in_=bass.AP(sch, off, [[128, 128], [128 * 128, nblk], [1, 128]])),
                  reads=[scb], writes=[vS.b])
        else:
            P.dma("sp", lambda e: e.dma_start(out=vS.t[p0:p0 + nrow, blk, X * 128:(X + 1) * 128],
                                              in_=bass.AP(sch, off, [[128, nrow], [1, 128]])),
                  reads=[scb], writes=[vS.b])
    for X in range(2):
        for r in range(4):
            lo, hi = TQ * r, TQ * r + TQ
            pos = lo
            if pos % 128:
                n = 128 - pos % 128
                ldv_piece(X, r, pos - lo, n, pos // 128, pos % 128, 0)
                pos += n
            nfull = (hi - pos) // 128
            if nfull:
                ldv_piece(X, r, pos - lo, 128 * nfull, pos // 128, 0, nfull)
                pos += 128 * nfull
            if pos < hi:
                ldv_piece(X, r, pos - lo, hi - pos, pos // 128, 0, 0)

    spb = [P.buf(f"sp{b_}") for b_ in range(NB)]
    pex = [P.sb(f"pex{i}", [128, 1024], F32) for i in range(2)]
    onef = P.sb("onef", [128, 1], F32)
    P.op("pool", lambda e: e.memset(onef.t[:], 1.0), writes=[onef.b])
    cnt = {"a": 0, "b": 0, "o": 0, "ap": 0, "bp": 0}

    def tile_geom(j):
        t0 = 512 * j
        tw = 512 if j < 16 else 16
        nblk = 4 * j + 4 if j < 16 else 65
        return t0, tw, nblk

    def load_q(j):
        t0, tw, nblk = tile_geom(j)
        q = qt[j % 2]
        for X in range(2):
            for r in range(4):
                lo, hi = max(t0, TQ * r), min(t0 + tw, TQ * r + TQ)
                if lo < hi:
                    P.dma("sp", lambda e, X=X, r=r, lo=lo, hi=hi: e.dma_start(
                        out=q.t[0:64, 2 * X:2 * X + 2, lo - t0:hi - t0],
                        in_=sc[X, r].rearrange("(two p) c -> p two c", p=64)[:, :, lo - TQ * r:hi - TQ * r]),
                        reads=[scb], writes=[q.b])

    def blkinfo(j, blk):
        ksz = 128 if blk < 64 else 16
        diag = None
        if j < 16 and blk >= 4 * j:
            diag = blk - 4 * j
        if j == 16 and blk == 64:
            diag = 0
        return ksz, diag

    def a_step(j, h, blk):
        t0, tw, nblk = tile_geom(j)
        q = qt[j % 2]
        ksz, diag = blkinfo(j, blk)
        i1 = cnt["a"] % 2
        cnt["a"] += 1
        p1, px = ps1[i1], pex[cnt["ap"] % 2]
        cnt["ap"] += 1
        P.op("pe", lambda e: e.matmul(p1.t[:ksz, :tw], lhsT=kT.t[:, h, blk * 128:blk * 128 + ksz], rhs=q.t[:, h, :tw],
                                      start=True, stop=True), reads=[kT.b, q.b], writes=[p1.b])
        P.op("act", lambda e: e.activation(out=px.t[:ksz, :tw], in_=p1.t[:ksz, :tw], func=AF.Exp),
             reads=[p1.b], writes=[px.b])
        P.op("act", lambda e: e.activation(out=spA.t[:ksz, blk, :tw], in_=px.t[:ksz, :tw], func=AF.Ln,
                                           bias=onef.t[:ksz, :]), reads=[px.b, onef.b], writes=[spb[blk]])
        if diag is not None:
            P.op("dve", lambda e: e.tensor_tensor(out=spA.t[:ksz, blk, :tw], in0=spA.t[:ksz, blk, :tw],
                                                   in1=masks[diag].t[:ksz, :tw], op=ALU.mult),
                 reads=[spb[blk], masks[diag].b], writes=[spb[blk]])

    class BState:
        pass

    def b_begin(j, h):
        st = BState()
        st.j, st.h = j, h
        st.t0, st.tw, st.nblk = tile_geom(j)
        st.pacc = po[cnt["o"] % 2]
        st.first = True
        st.pend = None
        return st

    def b_pv(st, blk, ksz, w, start):
        h, tw, pacc = st.h, st.tw, st.pacc
        P.op("pe", lambda e: e.matmul(pacc.t[:128, :tw], lhsT=vS.t[:ksz, blk, h * 64:h * 64 + 128],
                                      rhs=w.t[:ksz, :tw], start=start, stop=(blk == 0)),
             reads=[vS.b, w.b], writes=[pacc.b])

    def b_step(st, blk):
        j, h, tw = st.j, st.h, st.tw
        q = qt[j % 2]
        ksz, diag = blkinfo(j, blk)
        i2 = cnt["b"] % 2
        cnt["b"] += 1
        ip = cnt["bp"] % 2
        cnt["bp"] += 1
        p2, pc, w, tm = ps2[i2], pcb[i2], wt[ip], tmp[ip]
        first = st.first
        P.op("pe", lambda e: e.matmul(p2.t[:ksz, :tw], lhsT=negU.t[:ksz, :ksz], rhs=spA.t[:ksz, blk, :tw],
                                      start=True, stop=False), reads=[negU.b, spb[blk]], writes=[p2.b], sig=False)
        P.op("pe", lambda e: e.matmul(p2.t[:ksz, :tw], lhsT=kT.t[:, h, blk * 128:blk * 128 + ksz], rhs=q.t[:, h, :tw],
                                      start=False, stop=True), reads=[kT.b, q.b], writes=[p2.b])
        if blk > 0:
            P.op("pe", lambda e: e.matmul(pc.t[:, :tw], lhsT=onesb.t[:ksz, :], rhs=spA.t[:ksz, blk, :tw],
                                          start=True, stop=True), reads=[onesb.b, spb[blk]], writes=[pc.b])
        if st.pend is not None:
            for pe_ in st.pend:
                b_pv(st, *pe_)
        if first:
            P.op("act", lambda e: e.activation(out=w.t[:ksz, :tw], in_=p2.t[:ksz, :tw], func=AF.Exp),
                 reads=[p2.b], writes=[w.b])
        else:
            P.op("dve", lambda e: e.tensor_tensor(out=tm.t[:ksz, :tw], in0=p2.t[:ksz, :tw], in1=carry.t[:ksz, :tw],
                                                  op=ALU.subtract), reads=[p2.b, carry.b], writes=[tm.b])
            P.op("act", lambda e: e.activation(out=w.t[:ksz, :tw], in_=tm.t[:ksz, :tw], func=AF.Exp),
                 reads=[tm.b], writes=[w.b])
        if diag is not None:
            P.op("dve", lambda e: e.tensor_tensor(out=w.t[:ksz, :tw], in0=w.t[:ksz, :tw],
                                                   in1=masks[diag].t[:ksz, :tw], op=ALU.mult),
                 reads=[w.b, masks[diag].b], writes=[w.b])
        if blk > 0:
            if first:
                P.op("dve", lambda e: e.tensor_copy(out=carry.t[:, :tw], in_=pc.t[:, :tw]),
                     reads=[pc.b], writes=[carry.b])
            else:
                P.op("dve", lambda e: e.tensor_tensor(out=carry.t[:, :tw], in0=pc.t[:, :tw], in1=carry.t[:, :tw],
                                                      op=ALU.add), reads=[pc.b, carry.b], writes=[carry.b])
        st.pend = [(blk, ksz, w, first)]
        st.first = False

    def b_end(st):
        h, t0, tw, pacc = st.h, st.t0, st.tw, st.pacc
        for pe_ in st.pend:
            b_pv(st, *pe_)
        cnt["o"] += 1
        o = ost[cnt["o"] % 2]
        P.op("dve", lambda e: e.tensor_copy(out=o.t[:, :tw], in_=pacc.t[:64, :tw]), reads=[pacc.b], writes=[o.b])
        for d in range(4):
            lo, hi = max(t0, TQ * d - 2), min(t0 + tw, TQ * d + TQ)
            if lo < hi:
                c0d = TQ * d - 2
                P.dma("sp", lambda e, d=d, lo=lo, hi=hi, c0d=c0d: e.dma_start(
                    out=oTd[d, h // 2][(h % 2) * 64:(h % 2) * 64 + 64, lo - c0d:hi - c0d], in_=o.t[:, lo - t0:hi - t0]),
                    reads=[o.b], sembuf=o.b)

    def a_pair(j, h, blk):
        t0, tw, nblk = tile_geom(j)
        q = qt[j % 2]
        px = pex[cnt["ap"] % 2]
        cnt["ap"] += 1
        for u, bb in ((1, blk), (0, blk - 1)):
            p1 = ps1[cnt["a"] % 2]
            cnt["a"] += 1
            P.op("pe", lambda e, p1=p1, bb=bb: e.matmul(p1.t[:, :tw], lhsT=kT.t[:, h, bb * 128:bb * 128 + 128],
                                                       rhs=q.t[:, h, :tw], start=True, stop=True),
                 reads=[kT.b, q.b], writes=[p1.b])
            P.op("act", lambda e, p1=p1, u=u: e.activation(out=px.t[:, u * 512:u * 512 + tw], in_=p1.t[:, :tw],
                                                         func=AF.Exp), reads=[p1.b], writes=[px.b])
        P.op("act", lambda e: e.activation(out=spA.t[:, blk - 1:blk + 1, :tw],
                                           in_=px.t[:, :].rearrange("p (u c) -> p u c", u=2)[:, :, :tw],
                                           func=AF.Ln, bias=onef.t[:, :]),
             reads=[px.b, onef.b], writes=[spb[blk - 1], spb[blk]])
        for bb in (blk, blk - 1):
            ksz, diag = blkinfo(j, bb)
            if diag is not None:
                P.op("dve", lambda e, bb=bb, diag=diag: e.tensor_tensor(
                    out=spA.t[:, bb, :tw], in0=spA.t[:, bb, :tw], in1=masks[diag].t[:, :tw], op=ALU.mult),
                    reads=[spb[bb], masks[diag].b], writes=[spb[bb]])

    def b_pair(st, blk):
        j, h, tw = st.j, st.h, st.tw
        q = qt[j % 2]
        ip = cnt["bp"] % 2
        cnt["bp"] += 1
        w, tm = wt[ip], tmp[ip]
        first = st.first
        for u, bb in ((1, blk), (0, blk - 1)):
            i2 = cnt["b"] % 2
            cnt["b"] += 1
            p2, pc = ps2[i2], pcb[i2]
            P.op("pe", lambda e, p2=p2, bb=bb: e.matmul(p2.t[:, :tw], lhsT=negU.t[:, :], rhs=spA.t[:, bb, :tw],
                                                       start=True, stop=False),
                 reads=[negU.b, spb[bb]], writes=[p2.b], sig=False)
            P.op("pe", lambda e, p2=p2, bb=bb: e.matmul(p2.t[:, :tw], lhsT=kT.t[:, h, bb * 128:bb * 128 + 128],
                                                       rhs=q.t[:, h, :tw], start=False, stop=True),
                 reads=[kT.b, q.b], writes=[p2.b])
            if bb > 0:
                P.op("pe", lambda e, pc=pc, bb=bb: e.matmul(pc.t[:, :tw], lhsT=onesb.t[:, :], rhs=spA.t[:, bb, :tw],
                                                           start=True, stop=True),
                     reads=[onesb.b, spb[bb]], writes=[pc.b])
            if u == 1 and st.pend is not None:
                for pe_ in st.pend:
                    b_pv(st, *pe_)
                st.pend = None
            if first and u == 1:
                P.op("dve", lambda e, p2=p2, u=u: e.tensor_copy(out=tm.t[:, u * 512:u * 512 + tw], in_=p2.t[:, :tw]),
                     reads=[p2.b], writes=[tm.b])
            else:
                P.op("dve", lambda e, p2=p2, u=u: e.tensor_tensor(out=tm.t[:, u * 512:u * 512 + tw], in0=p2.t[:, :tw],
                                                                 in1=carry.t[:, :tw], op=ALU.subtract),
                     reads=[p2.b, carry.b], writes=[tm.b])
            if bb > 0:
                if first and u == 1:
                    P.op("dve", lambda e, pc=pc: e.tensor_copy(out=carry.t[:, :tw], in_=pc.t[:, :tw]),
                         reads=[pc.b], writes=[carry.b])
                else:
                    P.op("dve", lambda e, pc=pc: e.tensor_tensor(out=carry.t[:, :tw], in0=pc.t[:, :tw],
                                                                in1=carry.t[:, :tw], op=ALU.add),
                         reads=[pc.b, carry.b], writes=[carry.b])
        P.op("act", lambda e: e.activation(out=w.t[:, :].rearrange("p (u c) -> p u c", u=2)[:, :, :tw],
                                           in_=tm.t[:, :].rearrange("p (u c) -> p u c", u=2)[:, :, :tw], func=AF.Exp),
             reads=[tm.b], writes=[w.b])
        pend = []
        for u, bb in ((1, blk), (0, blk - 1)):
            ksz, diag = blkinfo(j, bb)
            wv = T(w.t[:, u * 512:(u + 1) * 512], w.b)
            if diag is not None:
                P.op("dve", lambda e, wv=wv, diag=diag: e.tensor_tensor(
                    out=wv.t[:, :tw], in0=wv.t[:, :tw], in1=masks[diag].t[:, :tw], op=ALU.mult),
                    reads=[w.b, masks[diag].b], writes=[w.b])
            pend.append((bb, 128, wv, first and u == 1))
        st.pend = pend
        st.first = False

    streams = [(j, h) for j in range(17) for h in range(4)]
    def a_range(j, h, hi, lo):
        blk = hi - 1
        while blk >= lo:
            if j < 16 and blk - 1 >= lo and blk < 64:
                a_pair(j, h, blk)
                blk -= 2
            else:
                a_step(j, h, blk)
                blk -= 1

    load_q(0)
    a_range(0, 0, tile_geom(0)[2], 0)
    for k, (j, h) in enumerate(streams):
        nb_cur = tile_geom(j)[2]
        nxt = streams[k + 1] if k + 1 < len(streams) else None
        if nxt is not None:
            if nxt[1] == 0:
                load_q(nxt[0])
            nb_nxt = tile_geom(nxt[0])[2]
            a_range(nxt[0], nxt[1], nb_nxt, nb_cur)
        st = b_begin(j, h)
        blk = nb_cur - 1
        while blk >= 0:
            if j < 16 and blk >= 1:
                b_pair(st, blk)
                if nxt is not None:
                    a_range(nxt[0], nxt[1], blk + 1, blk - 1)
                blk -= 2
            else:
                b_step(st, blk)
                if nxt is not None:
                    a_range(nxt[0], nxt[1], blk + 1, blk)
                blk -= 1
        b_end(st)
    return [o.b for o in ost] + [zt.b]


def phase_ssd(P, io):
    xTd, wsd, g_in, cwd, cbd = io["xT"], io["wsel"], io["g_in"], io["cw"], io["cb"]
    dtbd, alogd, dskd, ggd, hgo = io["dtb"], io["alog"], io["dsk"], io["gg"], io["b1"]
    cx = Ctx(P, wslot_elems=8 * 1288, nwslots=1, nps=4)
    W = TQ
    CH = chunks128(W)
    NCH = len(CH)
    ptr = [P.psum(f"ptr{i}", [128, 1024], BF16) for i in range(2)]
    pacc = [P.psum(f"pacc{i}", [128, 512]) for i in range(2)]
    gin = load_small(cx, "gin", g_in, [128, 8])
    cw = P.sb("cw", [128, 6, 4], F32)
    P.dma("sp", lambda e: e.dma_start(out=cw.t[:].rearrange("p a b -> p (a b)"), in_=cwd), writes=[cw.b])
    cb = load_small(cx, "cb", cbd, [128, 6])
    dtb = load_small(cx, "dtb", dtbd, [128, 8])
    aneg = load_small(cx, "aneg", alogd, [128, 8])
    dsk = load_small(cx, "dsk", dskd, [128, 8])
    gg = load_small(cx, "gg", ggd, [128, 512])
    P.op("act", lambda e: e.activation(out=aneg.t[:], in_=aneg.t[:], func=AF.Exp), reads=[aneg.b], writes=[aneg.b])
    P.op("dve", lambda e: e.tensor_scalar(out=aneg.t[:], in0=aneg.t[:], scalar1=-1.0, scalar2=None, op0=ALU.mult),
         reads=[aneg.b], writes=[aneg.b])
    wv, wb = cx.load_w(wsd, 0, 8, 0, 1288)

    ident = P.sb("ident", [128, 128], BF16)
    P.op("pool", lambda e: e.memset(ident.t[:], 1.0), writes=[ident.b])
    P.op("pool", lambda e: e.affine_select(out=ident.t[:], in_=ident.t[:], pattern=[[-1, 128]], compare_op=ALU.is_equal,
                                           fill=0.0, base=0, channel_multiplier=1), reads=[ident.b], writes=[ident.b])
    triI = P.sb("triI", [128, 128], F32)
    P.op("pool", lambda e: e.memset(triI.t[:], 1.0), writes=[triI.b])
    P.op("pool", lambda e: e.affine_select(out=triI.t[:], in_=triI.t[:], pattern=[[1, 128]], compare_op=ALU.is_ge,
                                           fill=0.0, base=0, channel_multiplier=-1), reads=[triI.b], writes=[triI.b])
    mstr = P.sb("mstr", [128, 128], F32)
    P.op("pool", lambda e: e.memset(mstr.t[:], 1.0), writes=[mstr.b])
    P.op("pool", lambda e: e.affine_select(out=mstr.t[:], in_=mstr.t[:], pattern=[[-1, 128]], compare_op=ALU.is_gt,
                                           fill=0.0, base=0, channel_multiplier=1), reads=[mstr.b], writes=[mstr.b])

    xins = [P.sb(f"xin{i}", [128, 8, 416], F32) for i in range(2)]
    xcnt = [0]
    uT = P.sb("uT", [128, 8, W + 3], BF16)
    pre = P.sb("pre", [128, W + 3], F32)
    acc = P.sb("acc", [128, W], F32)
    xsT = P.sb("xsT", [128, 6, W], BF16)
    xs_tm = P.sb("xs_tm", [128, NCH, 512], BF16)
    B_tm = P.sb("B_tm", [128, NCH, 128], BF16)
    zs_tm = P.sb("zs_tm", [128, NCH, 512], BF16)
    dt = P.sb("dt", [128, NCH, 8], F32)
    dta = P.sb("dta", [128, NCH, 8], F32)
    cs = P.sb("cs", [128, NCH, 8], F32)
    ecs = P.sb("ecs", [128, NCH, 8], F32)
    wst = P.sb("wst", [128, NCH, 8], F32)
    cdec = P.sb("cdec", [128, NCH, 8], F32)
    onef1 = P.sb("onef1", [128, 1], F32)
    P.op("pool", lambda e: e.memset(onef1.t[:], 1.0), writes=[onef1.b])
    for t_ in (dt, dta, cs, ecs, wst, cdec):
        P.op("pool", lambda e, t_=t_: e.memset(t_.t[:], 0.0), writes=[t_.b])
    S = P.sb("S", [128, 512], F32)
    Sb = P.sb("Sb", [128, 512], BF16)
    P.op("pool", lambda e: e.memset(S.t[:], 0.0), writes=[S.b])
    P.op("pool", lambda e: e.memset(Sb.t[:], 0.0), writes=[Sb.b])
    cbT = P.sb("cbT", [128, 128], F32)
    lh8 = [P.sb(f"lh8_{i}", [128, 8, 128], F32) for i in range(2)]
    dec8 = [P.sb(f"dec8_{i}", [128, 8, 128], F32) for i in range(2)]
    MT8 = [P.sb(f"MT8_{i}", [128, 8, 128], BF16) for i in range(2)]
    xdt = P.sb("xdt", [128, 512], BF16)
    xdte = P.sb("xdte", [128, 512], BF16)
    y1 = P.sb("y1", [128, 512], F32)
    y2 = P.sb("y2", [128, 512], F32)
    hgn = P.sb("hgn", [128, 512], BF16)
    ss = P.sb("ss", [128, 2], F32)
    hst = [P.sb(f"hst{i}", [128, 4, 128], BF16) for i in range(2)]
    cnt = {"h": 0, "l": 0}
    v3 = lambda ap: ap.rearrange("p (h d) -> p h d", h=8)
    bc = lambda ap: ap.unsqueeze(2).to_broadcast([ap.shape[0], 8, 64])

    def do_seg(si):
        s0 = si * W

        def ld(t0, tw):
            xin = xins[xcnt[0] % 2]
            xcnt[0] += 1
            ti = t0 // tw
            for k in range(8):
                P.dma(("sp", "act")[k % 2], lambda e, k=k: e.dma_start(
                    out=xin.t[:, k, :tw], in_=xTd[si, ti, k]), writes=[xin.b])
            rmsnorm_fm(cx, xin, gin, uT, 0, tw, dst_c0=t0)
        for (t0, tw) in tiles(W + 3):
            ld(t0, tw)

        def inproj(jc):
            for (t0, tw) in tiles(W + 3):
                ps = cx.psum()
                for k in range(8):
                    P.op("pe", lambda e, ps=ps, k=k, t0=t0, tw=tw: e.matmul(
                        ps.t[:, :tw], lhsT=wv[:, k, 512 + jc * 128:512 + (jc + 1) * 128], rhs=uT.t[:, k, t0:t0 + tw],
                        start=(k == 0), stop=(k == 7)), reads=[wb, uT.b], writes=[ps.b])
                P.op("act", lambda e, ps=ps, t0=t0, tw=tw: e.activation(out=pre.t[:, t0:t0 + tw], in_=ps.t[:, :tw],
                                                                       func=AF.Identity), reads=[ps.b], writes=[pre.b])
            conv_fm(cx, pre, cw, cb, jc, 4, W, acc)
            P.op("act", lambda e: e.activation(out=xsT.t[:, jc, :], in_=acc.t[:, :], func=AF.Silu),
                 reads=[acc.b], writes=[xsT.b])
        for jc in range(6):
            inproj(jc)

        def tr(ci, c0, csz):
            pt = ptr[ci % 2]
            for jc in range(5):
                P.op("pe", lambda e, jc=jc: e.transpose(pt.t[:csz, jc * 128:(jc + 1) * 128],
                                                        xsT.t[:, jc, c0:c0 + csz], ident.t[:]),
                     reads=[xsT.b, ident.b], writes=[pt.b])
            P.op("dve", lambda e: e.tensor_copy(out=xs_tm.t[:csz, ci, :], in_=pt.t[:csz, 0:512]),
                 reads=[pt.b], writes=[xs_tm.b])
            P.op("dve", lambda e: e.tensor_copy(out=B_tm.t[:csz, ci, :], in_=pt.t[:csz, 512:640]),
                 reads=[pt.b], writes=[B_tm.b])

        def zz(ci, c0, csz):
            ps = cx.psum()
            for k in range(8):
                P.op("pe", lambda e, k=k: e.matmul(ps.t[:csz, :512], lhsT=uT.t[:, k, 3 + c0:3 + c0 + csz],
                                                  rhs=wv[:, k, 0:512], start=(k == 0), stop=(k == 7)),
                     reads=[wb, uT.b], writes=[ps.b])
            P.op("act", lambda e: e.activation(out=zs_tm.t[:csz, ci, :], in_=ps.t[:csz, :512], func=AF.Silu),
                 reads=[ps.b], writes=[zs_tm.b])

        def dd(ci, c0, csz):
            ps = cx.psum()
            for k in range(8):
                P.op("pe", lambda e, k=k: e.matmul(ps.t[:csz, :8], lhsT=uT.t[:, k, 3 + c0:3 + c0 + csz],
                                                  rhs=wv[:, k, 1280:1288], start=(k == 0), stop=(k == 7)),
                     reads=[wb, uT.b], writes=[ps.b])
            P.op("dve", lambda e: e.tensor_tensor(out=dt.t[:csz, ci, :], in0=ps.t[:csz, :8], in1=dtb.t[:csz, :],
                                                  op=ALU.add), reads=[ps.b, dtb.b], writes=[dt.b])

        def da(ci, c0, csz):
            P.op("dve", lambda e: e.tensor_tensor(out=dta.t[:csz, ci, :], in0=dt.t[:csz, ci, :], in1=aneg.t[:csz, :],
                                                  op=ALU.mult), reads=[dt.b, aneg.b], writes=[dta.b])

        def cc(ci, c0, csz):
            ps = cx.psum()
            P.op("pe", lambda e: e.matmul(ps.t[:csz, 0:8], lhsT=triI.t[:csz, :csz], rhs=dta.t[:csz, ci, :],
                                          start=True, stop=True), reads=[triI.b, dta.b], writes=[ps.b])
            P.op("pe", lambda e: e.matmul(ps.t[:, 8:16], lhsT=cx.ones.t[:csz, :], rhs=dta.t[:csz, ci, :],
                                          start=True, stop=True), reads=[cx.ones.b, dta.b], writes=[ps.b])
            P.op("dve", lambda e: e.tensor_copy(out=cs.t[:csz, ci, :], in_=ps.t[:csz, 0:8]),
                 reads=[ps.b], writes=[cs.b])
            P.op("dve", lambda e: e.tensor_tensor(out=wst.t[:csz, ci, :], in0=ps.t[:csz, 8:16],
                                                  in1=cs.t[:csz, ci, :], op=ALU.subtract),
                 reads=[ps.b, cs.b], writes=[wst.b])
            P.op("act", lambda e: e.activation(out=cdec.t[:, ci, :], in_=ps.t[:, 8:16], func=AF.Exp),
                 reads=[ps.b], writes=[cdec.b])
        for ci, (c0, csz) in enumerate(CH):
            tr(ci, c0, csz)
        for ci, (c0, csz) in enumerate(CH):
            zz(ci, c0, csz)
        for ci, (c0, csz) in enumerate(CH):
            dd(ci, c0, csz)
        P.op("act", lambda e: e.activation(out=dt.t[:], in_=dt.t[:], func=AF.Exp), reads=[dt.b], writes=[dt.b])
        P.op("act", lambda e: e.activation(out=dt.t[:], in_=dt.t[:], func=AF.Ln, bias=onef1.t[:, :]),
             reads=[dt.b, onef1.b], writes=[dt.b])
        for ci, (c0, csz) in enumerate(CH):
            da(ci, c0, csz)
        for ci, (c0, csz) in enumerate(CH):
            cc(ci, c0, csz)
        P.op("act", lambda e: e.activation(out=ecs.t[:], in_=cs.t[:], func=AF.Exp), reads=[cs.b], writes=[ecs.b])
        P.op("act", lambda e: e.activation(out=wst.t[:], in_=wst.t[:], func=AF.Exp), reads=[wst.b], writes=[wst.b])
        P.op("dve", lambda e: e.tensor_tensor(out=wst.t[:], in0=wst.t[:], in1=dt.t[:], op=ALU.mult),
             reads=[wst.b, dt.b], writes=[wst.b])
        for ci, (c0, csz) in enumerate(CH):
            do_chunk(s0, ci, c0, csz)

    def do_chunk(s0, ci, c0, csz):
        P.op("dve", lambda e: e.tensor_tensor(out=v3(xdt.t[:csz, :]), in0=v3(xs_tm.t[:csz, ci, :]),
                                              in1=bc(dt.t[:csz, ci, :]), op=ALU.mult),
             reads=[xs_tm.b, dt.b], writes=[xdt.b])
        P.op("pool", lambda e: e.tensor_tensor(out=v3(xdte.t[:csz, :]), in0=v3(xs_tm.t[:csz, ci, :]),
                                               in1=bc(wst.t[:csz, ci, :]), op=ALU.mult),
             reads=[xs_tm.b, wst.b], writes=[xdte.b])
        ps = cx.psum()
        P.op("pe", lambda e: e.matmul(ps.t[:csz, :csz], lhsT=xsT.t[:, 4, c0:c0 + csz], rhs=xsT.t[:, 5, c0:c0 + csz],
                                      start=True, stop=True), reads=[xsT.b], writes=[ps.b])
        P.op("dve", lambda e: e.tensor_tensor(out=cbT.t[:csz, :csz], in0=ps.t[:csz, :csz], in1=triI.t[:csz, :csz],
                                              op=ALU.mult), reads=[ps.b, triI.b], writes=[cbT.b])
        yp = pacc[ci % 2]
        l8, d8, m8 = lh8[ci % 2], dec8[ci % 2], MT8[ci % 2]
        P.op("dve", lambda e: e.tensor_tensor(
            out=l8.t[:csz, :, :csz], in0=mstr.t[:csz, :csz].unsqueeze(1).to_broadcast([csz, 8, csz]),
            in1=dta.t[:csz, ci, :].unsqueeze(2).to_broadcast([csz, 8, csz]), op=ALU.mult),
            reads=[mstr.b, dta.b], writes=[l8.b])
        pgs = [cx.psum(), cx.psum()]
        for hh in range(8):
            pg = pgs[hh // 4]
            P.op("pe", lambda e, pg=pg, hh=hh: e.matmul(pg.t[:csz, (hh % 4) * 128:(hh % 4) * 128 + csz],
                                                       lhsT=l8.t[:csz, hh, :csz], rhs=triI.t[:csz, :csz],
                                                       start=True, stop=True), reads=[l8.b, triI.b], writes=[pg.b])
        for g4 in range(2):
            pg = pgs[g4]
            P.op("act", lambda e, pg=pg, g4=g4: e.activation(
                out=d8.t[:csz, 4 * g4:4 * g4 + 4, :csz],
                in_=pg.t[:csz, :].rearrange("p (h s) -> p h s", h=4)[:, :, :csz], func=AF.Exp),
                reads=[pg.b], writes=[d8.b])
        P.op("dve", lambda e: e.tensor_tensor(
            out=m8.t[:csz, :, :csz], in0=d8.t[:csz, :, :csz],
            in1=cbT.t[:csz, :csz].unsqueeze(1).to_broadcast([csz, 8, csz]), op=ALU.mult),
            reads=[d8.b, cbT.b], writes=[m8.b])
        for hh in range(8):
            P.op("pe", lambda e, hh=hh: e.matmul(yp.t[:csz, hh * 64:(hh + 1) * 64], lhsT=m8.t[:csz, hh, :csz],
                                                rhs=xdt.t[:csz, hh * 64:(hh + 1) * 64], start=True, stop=True),
                 reads=[m8.b, xdt.b], writes=[yp.b])
        po_ = cx.psum()
        P.op("pe", lambda e: e.matmul(po_.t[:csz, :512], lhsT=xsT.t[:, 5, c0:c0 + csz], rhs=Sb.t[:, :],
                                      start=True, stop=True), reads=[xsT.b, Sb.b], writes=[po_.b])
        P.op("dve", lambda e: e.tensor_tensor(out=v3(y1.t[:csz, :]), in0=v3(po_.t[:csz, :512]),
                                              in1=bc(ecs.t[:csz, ci, :]), op=ALU.mult),
             reads=[po_.b, ecs.b], writes=[y1.b])
        P.op("dve", lambda e: e.tensor_tensor(out=y1.t[:csz, :], in0=yp.t[:csz, :512], in1=y1.t[:csz, :], op=ALU.add),
             reads=[yp.b, y1.b], writes=[y1.b])
        P.op("pool", lambda e: e.tensor_tensor(out=v3(y2.t[:csz, :]), in0=v3(xs_tm.t[:csz, ci, :]),
                                               in1=bc(dsk.t[:csz, :]), op=ALU.mult),
             reads=[xs_tm.b, dsk.b], writes=[y2.b])
        P.op("dve", lambda e: e.tensor_tensor(out=y1.t[:csz, :], in0=y1.t[:csz, :], in1=y2.t[:csz, :], op=ALU.add),
             reads=[y1.b, y2.b], writes=[y1.b])
        P.op("dve", lambda e: e.tensor_tensor(out=y1.t[:csz, :], in0=y1.t[:csz, :], in1=zs_tm.t[:csz, ci, :],
                                              op=ALU.mult), reads=[y1.b, zs_tm.b], writes=[y1.b])
        P.op("act", lambda e: e.activation(out=y2.t[:csz, :], in_=y1.t[:csz, :], func=AF.Square,
                                           accum_out=ss.t[:csz, 0:1]), reads=[y1.b], writes=[y2.b, ss.b])
        P.op("act", lambda e: e.activation(out=ss.t[:csz, 1:2], in_=ss.t[:csz, 0:1], func=AF.Ln,
                                           bias=cx.epst.t[:csz, :], scale=1.0 / 512), reads=[ss.b, cx.epst.b],
             writes=[ss.b])
        P.op("act", lambda e: e.activation(out=ss.t[:csz, 1:2], in_=ss.t[:csz, 1:2], func=AF.Exp, scale=-0.5),
             reads=[ss.b], writes=[ss.b])
        P.op("dve", lambda e: e.scalar_tensor_tensor(out=hgn.t[:csz, :], in0=y1.t[:csz, :], scalar=ss.t[:csz, 1:2],
                                                     in1=gg.t[:csz, :], op0=ALU.mult, op1=ALU.mult),
             reads=[y1.b, ss.b, gg.b], writes=[hgn.b])
        pt = ptr[ci % 2]
        hs = hst[cnt["h"] % 2]
        cnt["h"] += 1
        for jc in range(4):
            P.op("pe", lambda e, jc=jc: e.transpose(pt.t[:, jc * 128:jc * 128 + csz], hgn.t[:csz, jc * 128:(jc + 1) * 128],
                                                    ident.t[:csz, :csz]), reads=[hgn.b, ident.b], writes=[pt.b])
        P.op("act", lambda e: e.activation(out=hs.t[:, :, :csz],
                                           in_=pt.t[:, 0:512].rearrange("p (j c) -> p j c", j=4)[:, :, :csz],
                                           func=AF.Identity), reads=[pt.b], writes=[hs.b])
        si = s0 // W
        P.dma("sp", lambda e: e.dma_start(
            out=hgo[si][:, :, 4 + c0:4 + c0 + csz].rearrange("j p c -> p j c"), in_=hs.t[:, :, :csz]),
            reads=[hs.b], sembuf=hs.b)
        if c0 + csz == W and si < 3:
            P.dma("sp", lambda e: e.dma_start(
                out=hgo[si + 1][:, :, 0:4].rearrange("j p c -> p j c"), in_=hs.t[:, :, csz - 4:csz]),
                reads=[hs.b], sembuf=hs.b)
        pn = cx.psum()
        P.op("pe", lambda e: e.matmul(pn.t[:, :512], lhsT=B_tm.t[:csz, ci, :], rhs=xdte.t[:csz, :], start=True,
                                      stop=True), reads=[B_tm.b, xdte.b], writes=[pn.b])
        P.op("dve", lambda e: e.tensor_tensor(out=v3(S.t[:, :]), in0=v3(S.t[:, :]), in1=bc(cdec.t[:, ci, :]),
                                              op=ALU.mult), reads=[S.b, cdec.b], writes=[S.b])
        P.op("dve", lambda e: e.tensor_tensor(out=S.t[:, :], in0=pn.t[:, :512], in1=S.t[:, :], op=ALU.add),
             reads=[pn.b, S.b], writes=[S.b])
        P.op("pool", lambda e: e.tensor_copy(out=Sb.t[:, :], in_=S.t[:, :]), reads=[S.b], writes=[Sb.b])

    zt = P.sb("zt", [128, 4, 4], BF16)
    P.op("pool", lambda e: e.memset(zt.t[:], 0.0), writes=[zt.b])
    P.dma("sp", lambda e: e.dma_start(out=hgo[0][:, :, 0:4].rearrange("j p c -> p j c"), in_=zt.t[:]),
          reads=[zt.b], sembuf=zt.b)
    for si in range(4):
        do_seg(si)
        io["after_seg"](si, [h.b for h in hst] + [zt.b])
    return [h.b for h in hst] + [zt.b]


GROUPS = [[0, 1, 2, 3], [4, 5, 6, 7]]


def build_fused():
    nc = bass.Bass("TRN2", target_bir_lowering=False)

    def dr(n, s, dt=F32, k="ExternalInput"):
        return nc.dram_tensor(n, list(s), dt, kind=k)
    ioA = {"xT": dr("A_xT", [4, 5, 8, 128, 411]).ap(), "wsel": dr("A_wsel", [D, 1288]).ap(), "g_in": dr("A_g_in", [128, 8]).ap(),
           "cw": dr("A_cw", [128, 24]).ap(), "cb": dr("A_cb", [128, 6]).ap(), "dtb": dr("A_dtb", [128, 8]).ap(),
           "alog": dr("A_alog", [128, 8]).ap(), "dsk": dr("A_dsk", [128, 8]).ap(), "gg": dr("A_gg", [128, 512]).ap()}

    def tok_io(pfx, kcm):
        return {"w_mix": dr(pfx + "w_mix", [kcm * 128, D]).ap(), "w_up": dr(pfx + "w_up", [D, 2 * DFF]).ap(),
                "w_down": dr(pfx + "w_down", [DFF, D]).ap(), "g_ffn": dr(pfx + "g_ffn", [128, 8]).ap(),
                "cw": dr(pfx + "cw", [128, 132]).ap(), "cb": dr(pfx + "cb", [128, 44]).ap()}
    ioB = tok_io("B_", 16)
    ioB.update({"resid": dr("B_resid", [D, TQ + 4]).ap(), "w_kv": dr("B_w_kv", [D, 2 * D]).ap(),
                "w_q": dr("B_w_q", [D, D]).ap(), "g_kv": dr("B_g_kv", [128, 8]).ap(), "g_q": dr("B_g_q", [128, 8]).ap(),
                "hmask": dr("B_hmask", [128, 1]).ap()})
    ioD = tok_io("D_", 8)
    ioD.update({"g_fin": dr("D_g_fin", [128, 8]).ap(), "outo": dr("out", [D, TQ], F32, "ExternalOutput").ap()})
    idxd = dr("idx", [1, 1], I32).ap()

    C1, C3, CH2 = TQ + 4, TQ + 2, 128 * TQ
    b1 = nc.dram_tensor("b1", [4, 4, 128, C1], BF16)
    g1 = nc.dram_tensor("g1", [4, 4, 4, 128, C1], BF16)
    b2 = nc.dram_tensor("b2", [4, 6, 128, TQ], BF16)
    g2 = nc.dram_tensor("g2", [4, 6, 4, 128, TQ], BF16)
    b3 = nc.dram_tensor("b3", [4, 2, 128, C3], BF16)
    g3 = nc.dram_tensor("g3", [4, 2, 4, 128, C3], BF16)
    h1scr = nc.dram_tensor("h1scr", [D, TQ + 2], F32)
    dtap = nc.dram_tensor("dtap", [D, 1028], F32)
    sc1 = nc.dram_tensor("sc1", [4, 4, 128, C1], BF16)
    sc2 = nc.dram_tensor("sc2", [6, 4, 128, TQ], BF16)
    sc3 = nc.dram_tensor("sc3", [2, 4, 128, C3], BF16)

    P = Prog(nc)
    regs = {n: P.stack.enter_context(nc.gpsimd.register(n)) for n in ("ridx", "r1", "r2q", "r3", "rtmp")}
    it = P.sb("idxt", [1, 2], I32)
    scr = P.sb("scr", [1, 16], BF16)
    P.persist = P.off
    P.dma("pool", lambda e: e.dma_start(out=it.t[0:1, 0:1], in_=idxd), writes=[it.b])

    def setup(e):
        e.reg_load(regs["ridx"], it.t[0:1, 0:1])
        e.reg_mul(regs["r1"], regs["ridx"], 16 * 128 * C1)
        e.reg_mul(regs["r2q"], regs["ridx"], 24 * CH2)
        e.reg_mul(regs["r3"], regs["ridx"], 8 * 128 * C3)
        return e.memset(scr.t[:], 0.0)
    P.op("pool", setup, reads=[it.b], writes=[scr.b])

    def pull(sct, gt, reg, nrows, ncols):
        b = P.buf("sc")
        P.dma("pool", lambda e: e.dma_start(out=sct.ap().rearrange("a b c d -> (a b c) d"),
                                            in_=bass.AP(gt, reg, [[ncols, nrows], [1, ncols]])), writes=[b])
        return b

    def gather_dest(bt, gt, d, n1, ob):
        for c in range(n1):
            P.collective("AllGather", bt.ap()[d, c].opt(), gt.ap()[d, c].opt(), GROUPS, ob if c == 0 else [])

    def gather_all(bt, gt, n0, n1, ob):
        for d in range(n0):
            gather_dest(bt, gt, d, n1, ob if d == 0 else [])
        P.collective_wait()

    import os
    upto = int(os.environ.get("FUSE_UPTO", "4"))
    nocc = os.environ.get("FUSE_NOCC", "0") == "1"
    if nocc:
        P.collective = lambda *a, **k: None

    def finish():
        dbg = os.environ.get("FUSE_DEBUG", "")
        if dbg:
            src = {"h1scr": h1scr, "sc1": sc1, "sc2": sc2, "sc3": sc3, "b1": b1, "b2": b2, "b3": b3, "dtap": dtap}[dbg]
            shp = list(src.ap().shape)
            n = 1
            for d_ in shp[:-1]:
                n *= d_
            dt_ = F32 if dbg in ("h1scr", "dtap") else BF16
            dbo = nc.dram_tensor("dbg", [n, shp[-1]], dt_, kind="ExternalOutput")
            P.barrier()
            bb = P.buf("dbg")
            names = "abcdefg"[:len(shp) - 1]
            view = src.ap() if len(shp) == 2 else src.ap().rearrange(" ".join(names) + " z -> (" + " ".join(names) + ") z")
            P.dma("sp", lambda e: e.dma_start(out=dbo.ap(), in_=view), writes=[bb])
            P.wait_all("sp", [bb])
        P.barrier()
        print("sems", len(P.sems), "instr", {e: len(q) for e, q in P.q.items()})
        P.emit()
        P.close()
        return nc
    P.phase_start()
    ioA["b1"] = b1.ap()
    ioA["after_seg"] = lambda si, ob: gather_dest(b1, g1, si, 4, ob)
    phase_ssd(P, ioA)
    P.collective_wait()
    if upto == 1:
        return finish()
    P.phase_start()
    ioB.update({"sc": sc1.ap(), "scb": pull(sc1, g1, regs["r1"], 16 * 128, C1), "h1scr": h1scr.ap(),
                "b2": b2.ap(), "b2h": b2})
    ob = phase_token(P, "B", ioB)
    gather_all(b2, g2, 4, 6, ob)
    if upto == 2:
        return finish()
    P.phase_start()
    ob = phase_attn(P, {"b3": b3.ap(), "sc": sc2.ap(), "sch": sc2, "scb": pull(sc2, g2, regs["r2q"], 24 * 128, TQ)})
    gather_all(b3, g3, 4, 2, ob)
    if upto == 3:
        return finish()
    P.phase_start()
    ioD.update({"dtap": dtap.ap(), "sc": sc3.ap(), "scb": pull(sc3, g3, regs["r3"], 8 * 128, C3), "resid": h1scr.ap()})
    ob = phase_token(P, "D", ioD)
    P.wait_all("sp", ob)
    return finish()


_NC_CACHE = {}


def get_nc(key, fn, *a):
    if key not in _NC_CACHE:
        _NC_CACHE[key] = fn(*a)
    return _NC_CACHE[key]


def fm(v, n):
    return np.ascontiguousarray(np.asarray(v, np.float32).reshape(n, 128).T)


def _halo_cols(full, s, halo, W):
    out = np.zeros((full.shape[0], halo + W), full.dtype)
    lo = max(0, s - halo)
    out[:, lo - (s - halo):] = full[:, lo:s + W]
    return out


def _ffn_params(inp, layer, pfx):
    cwT = np.ascontiguousarray(np.asarray(inp["ffn_conv_w"][layer], np.float32).T.reshape(44, 128, 3)
                               .transpose(1, 0, 2).reshape(128, 132))
    return {pfx + "w_up": np.asarray(inp["ffn_w_up"][layer], np.float32),
            pfx + "w_down": np.asarray(inp["ffn_w_down"][layer], np.float32),
            pfx + "g_ffn": fm(inp["ffn_norm"][layer], 8), pfx + "cw": cwT, pfx + "cb": fm(inp["ffn_conv_b"][layer], 44)}


def kernel(**inp):
    inp = {k: np.asarray(v) for k, v in inp.items()}
    x = inp["x"].astype(np.float32)
    nb = x.shape[0]
    h0 = np.concatenate([np.broadcast_to(inp["meta_tokens"][None].astype(np.float32), (nb, 16, D)), x], axis=1)
    h0T = [np.ascontiguousarray(h0[b].T) for b in range(nb)]
    cores = list(range(8))
    nc = get_nc("F", build_fused)
    mA = ssd_maps(inp, h0)
    fB = _ffn_params(inp, 0, "B_")
    fD = _ffn_params(inp, 1, "D_")
    maps = []
    for c in cores:
        b, i = divmod(c, 4)
        m = {"A_" + k: v for k, v in mA[c].items()}
        m.update(fB)
        m.update(fD)
        m.update({"B_resid": _halo_cols(h0T[b], i * TQ, 4, TQ), "B_w_mix": np.ascontiguousarray(np.asarray(inp["ssd_w_out"][0], np.float32)
                                                   .reshape(4, 4, 128, D).transpose(1, 0, 2, 3).reshape(DI, D)),
                  "B_w_kv": np.asarray(inp["w_kv"], np.float32), "B_w_q": np.asarray(inp["sb_w_q"][0], np.float32),
                  "B_g_kv": fm(inp["kv_norm"], 8), "B_g_q": fm(inp["sb_norm"][0], 8),
                  "B_hmask": np.full((128, 1), 0.0 if i == 0 else 1.0, np.float32),
                  "D_w_mix": np.ascontiguousarray(np.asarray(inp["sb_w_o"][0], np.float32)
                                                   .reshape(4, 2, 128, D).transpose(1, 0, 2, 3).reshape(D, D)), "D_g_fin": fm(inp["final_norm"], 8),
                  "idx": np.array([[i]], np.int32)})
        maps.append(m)
    res = run_bass_kernel_spmd(nc, maps, core_ids=cores).results
    out = np.empty((nb, LB - 16, D), np.float32)
    for b in range(nb):
        full = np.concatenate([res[b * 4 + t]["out"] for t in range(4)], axis=1)
        out[b] = full[:, 16:].T
    return out


def ssd_maps(inp, h0):
    w_in = inp["ssd_w_in"][0]
    cwf = inp["ssd_conv_w"][0]
    cbf = inp["ssd_conv_b"][0]
    maps = []
    for c in range(8):
        b, g = divmod(c, 4)
        cols = np.concatenate([np.arange(512 * g, 512 * g + 512), 2048 + np.arange(512 * g, 512 * g + 512),
                               4096 + np.arange(128 * g, 128 * g + 128), 4608 + np.arange(128 * g, 128 * g + 128),
                               5120 + np.arange(8 * g, 8 * g + 8)])
        cch = np.concatenate([np.arange(512 * g, 512 * g + 512), 2048 + np.arange(128 * g, 128 * g + 128),
                              2560 + np.arange(128 * g, 128 * g + 128)])
        xpad = np.zeros((D, 3 + LB), np.float32)
        xpad[:, 3:] = h0[b].T
        xT = np.empty((4, 5, 8, 128, 411), np.float32)
        for si_ in range(4):
            for ti_ in range(5):
                c0_ = si_ * TQ + ti_ * 411
                xT[si_, ti_] = xpad[:, c0_:c0_ + 411].reshape(8, 128, 411)
        rep = lambda v: np.ascontiguousarray(np.broadcast_to(np.asarray(v, np.float32)[None, :], (128, len(v))))
        maps.append({
            "xT": xT, "wsel": np.ascontiguousarray(w_in[:, cols]), "g_in": fm(inp["ssd_norm"][0], 8),
            "cw": np.ascontiguousarray(cwf[:, cch].T.reshape(6, 128, 4).transpose(1, 0, 2).reshape(128, 24)),
            "cb": fm(cbf[cch], 6), "dtb": rep(inp["ssd_dt_bias"][0][8 * g:8 * g + 8]),
            "alog": rep(inp["ssd_a_log"][0][8 * g:8 * g + 8]), "dsk": rep(inp["ssd_d_skip"][0][8 * g:8 * g + 8]),
            "gg": rep(inp["ssd_gate_norm"][0][512 * g:512 * g + 512]),
        })
    return maps
```

```python
import numpy as np
import ml_dtypes
from contextlib import ExitStack
import concourse.bass as bass
import concourse.mybir as mybir
from concourse.bass_utils import run_bass_kernel_spmd

F32 = mybir.dt.float32
BF16 = mybir.dt.bfloat16
AF = mybir.ActivationFunctionType
ALU = mybir.AluOpType
AX = mybir.AxisListType
NPBF = ml_dtypes.bfloat16

D = 1024
LB = 8208
TQ = 2052
DI = 2048
DFF = 2816
EPS = 1e-6
EPOCH = 30000


I32 = mybir.dt.int32
ARENA = 106400
ISZ = {F32: 4, BF16: 2, I32: 4}


class Buf:
    __slots__ = ("name", "w", "r", "dsem", "dcnt")

    def __init__(self, name):
        self.name = name
        self.w = None
        self.r = {}
        self.dsem = None
        self.dcnt = 0


class T:
    __slots__ = ("t", "b")

    def __init__(self, t, b):
        self.t = t
        self.b = b


class Prog:
    ENGS = ("pe", "act", "dve", "pool", "sp")
    EMAP = {"pe": "tensor", "act": "scalar", "dve": "vector", "pool": "gpsimd", "sp": "sync"}

    def __init__(self, nc):
        self.nc = nc
        self.q = {e: [] for e in self.ENGS}
        self.cnt = {e: 0 for e in self.ENGS}
        self.seen = {e: {} for e in self.ENGS}
        self.sems = {}
        self.latest = {}
        self.stack = ExitStack()
        self.nbuf = 0
        self.arena = self.stack.enter_context(nc.sbuf_tensor("arena", [128, ARENA], BF16))
        self.banks = [self.stack.enter_context(nc.psum_tensor(f"bank{i}", [128, 512], F32)) for i in range(8)]
        self.off = 0
        self.persist = 0
        self.nbank = 0
        self.ncc = 0
        self.ccpend = []
        self.dfree = {"sw": [], "hw": []}
        self.dlive = []

    def _sem(self, key):
        if key not in self.sems:
            self.sems[key] = self.stack.enter_context(self.nc.semaphore("s_" + key.replace("#", "_")))
        return self.sems[key]

    def buf(self, name=None):
        self.nbuf += 1
        return Buf(f"{name or 'b'}{self.nbuf}")

    def sb(self, name, shape, dtype, stack=None):
        shape = list(shape)
        n = 1
        for d in shape[1:]:
            n *= d
        nel = (n * ISZ[dtype] + 1) // 2
        nel = (nel + 15) // 16 * 16
        assert self.off + nel <= ARENA, f"SBUF arena overflow at {name}: {self.off}+{nel}"
        v = self.arena[0:shape[0], self.off:self.off + n * ISZ[dtype] // 2]
        self.off += nel
        if dtype != BF16:
            v = v.bitcast(dtype)
        if len(shape) == 3:
            v = v.rearrange("p (a b) -> p a b", a=shape[1])
        return T(v, self.buf(name))

    def psum(self, name, shape, dtype=F32, stack=None):
        assert self.nbank < 8, "out of PSUM banks"
        bk = self.banks[self.nbank]
        self.nbank += 1
        v = bk[:, :]
        if dtype == BF16:
            v = v.bitcast(BF16)
        return T(v, self.buf(name))

    def _waits(self, eng, reads, writes):
        need = {}

        def add(k, v):
            if need.get(k, 0) < v:
                need[k] = v
        for b in reads:
            if b.w:
                add(*b.w)
        for b in writes:
            if b.w:
                add(*b.w)
            for k, v in b.r.items():
                add(k, v)
        if eng == "pe":
            for k in [k for k in need if k.startswith("pe#")]:
                del need[k]
        out = []
        seen = self.seen[eng]
        for k, v in need.items():
            if seen.get(k, 0) < v:
                seen[k] = v
                out.append((k, v))
        return out

    def _mark(self, ev, reads, writes):
        k, v = ev
        if self.latest.get(k, 0) < v:
            self.latest[k] = v
        for b in reads:
            if b.r.get(k, 0) < v:
                b.r[k] = v
        for b in writes:
            b.w = ev
            b.r = {}

    def op(self, eng, fn, reads=(), writes=(), sig=True):
        waits = self._waits(eng, reads, writes)
        c = self.cnt[eng]
        key = f"{eng}#{c // EPOCH}"
        self._sem(key)
        ev = (key, c % EPOCH + 1)
        if sig:
            self.cnt[eng] = c + 1
            self.q[eng].append((waits, fn, (key, 1)))
        else:
            assert eng == "pe"
            self.q[eng].append((waits, fn, None))
        self._mark(ev, reads, writes)
        return ev

    def dma(self, eng, fn, reads=(), writes=(), sembuf=None):
        waits = self._waits(eng, reads, writes)
        sb = sembuf or (writes[0] if writes else reads[0])
        if sb.dsem is None or sb.dcnt >= EPOCH:
            cls = "sw" if eng == "pool" else "hw"
            fl = self.dfree[cls]
            while fl and self.latest.get(fl[-1], 0) >= EPOCH - 4096:
                fl.pop()
            if fl:
                sb.dsem = fl.pop()
                sb.dcnt = self.latest.get(sb.dsem, 0)
            else:
                sb.dsem = f"d{cls}{len(self.sems)}"
                sb.dcnt = 0
                self._sem(sb.dsem)
            self.dlive.append((cls, sb.dsem))
        sb.dcnt += 16
        ev = (sb.dsem, sb.dcnt)
        self.q[eng].append((waits, fn, (sb.dsem, 16)))
        self._mark(ev, reads, writes)
        return ev

    def wait_all(self, eng, bufs):
        waits = self._waits(eng, bufs, bufs)
        self.q[eng].append((waits, None, None))

    def barrier(self):
        for e in self.ENGS:
            seen = self.seen[e]
            waits = []
            for k, v in self.latest.items():
                if seen.get(k, 0) < v:
                    seen[k] = v
                    waits.append((k, v))
            self.q[e].append((waits, None, None))

    def phase_start(self):
        self.barrier()
        self.off = self.persist
        self.nbank = 0
        for cls, k in self.dlive:
            self.dfree[cls].append(k)
        self.dlive = []

    def collective(self, kind, in_ap, out_ap, groups, wait_bufs):
        waits = self._waits("pool", wait_bufs, wait_bufs)
        key = f"cc{self.ncc}"
        self.ncc += 1
        self._sem(key)
        self.q["pool"].append((waits, lambda e: e.collective_compute(kind, ALU.bypass, replica_groups=groups,
                                                                    ins=[in_ap], outs=[out_ap]), (key, 1)))
        self.latest[key] = 1
        self.ccpend.append(key)

    def collective_wait(self):
        waits = [(k, 1) for k in self.ccpend if self.seen["pool"].get(k, 0) < 1]
        for k, _ in waits:
            self.seen["pool"][k] = 1
        self.ccpend = []
        self.q["pool"].append((waits, None, None))

    def emit(self):
        nc = self.nc
        sems = self.sems
        with nc.Block() as block:
            for e in self.ENGS:
                items = self.q[e]

                def body(engine, items=items):
                    for waits, fn, inc in items:
                        for k, v in waits:
                            engine.wait_ge(sems[k], v)
                        if fn is not None:
                            ins = fn(engine)
                            if inc is not None:
                                ins.then_inc(sems[inc[0]], inc[1])
                getattr(block, self.EMAP[e])(body)

    def close(self):
        self.stack.close()


def tiles(width, maxw=512):
    n = -(-width // maxw)
    base, rem = divmod(width, n)
    out, o = [], 0
    for i in range(n):
        w = base + (1 if i < rem else 0)
        out.append((o, w))
        o += w
    return out


def chunks128(width):
    out, o = [], 0
    while o < width:
        w = min(128, width - o)
        out.append((o, w))
        o += w
    return out


class Ctx:
    def __init__(self, P, wslot_elems=4096, nwslots=3, nps=8):
        self.nc = P.nc
        self.P = P
        self.ps = [P.psum(f"ps{i}", [128, 512]) for i in range(nps)]
        self.psi = 0
        self.wslots = [P.sb(f"wsl{i}", [128, wslot_elems], BF16) for i in range(nwslots)]
        self.wsi = 0
        self.wslot_elems = wslot_elems
        self.ones = P.sb("ones_f", [128, 128], F32)
        P.op("pool", lambda e: e.memset(self.ones.t[:], 1.0), writes=[self.ones.b])
        self.epst = P.sb("epst", [128, 1], F32)
        P.op("pool", lambda e: e.memset(self.epst.t[:], EPS), writes=[self.epst.b])
        self.onesb = P.sb("ones_b", [128, 128], BF16)
        P.op("pool", lambda e: e.memset(self.onesb.t[:], 1.0), writes=[self.onesb.b])
        self.sq = [P.sb(f"sq{i}", [128, 512], BF16) for i in range(4)]
        self.rs = [P.sb(f"rs{i}", [128, 512], F32) for i in range(2)]
        self.sqi = 0
        self.rsi = 0

    def psum(self):
        p = self.ps[self.psi % len(self.ps)]
        self.psi += 1
        return p

    def wslot(self):
        w = self.wslots[self.wsi % len(self.wslots)]
        self.wsi += 1
        return w

    def load_w(self, w_ap, k0, kc, n0, ncols):
        assert kc * ncols <= self.wslot_elems, (kc, ncols)
        sl = self.wslot()
        view = sl.t[:, 0:kc * ncols].rearrange("p (k n) -> p k n", k=kc)
        src = w_ap[k0:k0 + kc * 128, n0:n0 + ncols].rearrange("(k p) n -> p k n", p=128)
        self.P.dma("pool", lambda e: e.dma_start(out=view, in_=src), writes=[sl.b])
        return view, sl.b


def load_small(cx, name, dram_ap, shape, dtype=F32):
    t = cx.P.sb(name, shape, dtype)
    cx.P.dma("sp", lambda e: e.dma_start(out=t.t[:], in_=dram_ap), writes=[t.b])
    return t


def rmsnorm_fm(cx, src, g, dst, c0, width, dst_c0=0, kc=8):
    P = cx.P
    for (t0, tw) in tiles(width):
        ps = cx.psum()
        for k in range(kc):
            sq = cx.sq[cx.sqi % 4]
            cx.sqi += 1
            sl = src.t[:, k, c0 + t0:c0 + t0 + tw]
            P.op("pool", lambda e, sq=sq, sl=sl, tw=tw: e.tensor_tensor(out=sq.t[:, :tw], in0=sl, in1=sl, op=ALU.mult),
                 reads=[src.b], writes=[sq.b])
            P.op("pe", lambda e, ps=ps, sq=sq, tw=tw, k=k: e.matmul(ps.t[:, :tw], lhsT=cx.onesb.t[:], rhs=sq.t[:, :tw],
                                                              start=(k == 0), stop=(k == kc - 1)),
                 reads=[cx.onesb.b, sq.b], writes=[ps.b])
        rs = cx.rs[cx.rsi % 2]
        cx.rsi += 1
        P.op("act", lambda e, rs=rs, ps=ps, tw=tw: e.activation(out=rs.t[:, :tw], in_=ps.t[:, :tw], func=AF.Ln,
                                                             bias=cx.epst.t[:], scale=1.0 / (128 * kc)),
             reads=[ps.b, cx.epst.b], writes=[rs.b])
        P.op("act", lambda e, rs=rs, tw=tw: e.activation(out=rs.t[:, :tw], in_=rs.t[:, :tw], func=AF.Exp, scale=-0.5),
             reads=[rs.b], writes=[rs.b])
        for k in range(kc):
            P.op("dve", lambda e, k=k, rs=rs, t0=t0, tw=tw: e.scalar_tensor_tensor(
                out=dst.t[:, k, dst_c0 + t0:dst_c0 + t0 + tw], in0=src.t[:, k, c0 + t0:c0 + t0 + tw],
                scalar=g.t[:, k:k + 1], in1=rs.t[:, :tw], op0=ALU.mult, op1=ALU.mult),
                reads=[src.b, g.b, rs.b], writes=[dst.b])


def proj_fm(cx, w_ap, kc, n0, ncols, uT, c0, width, evac, ngroup=None):
    P = cx.P
    gcols = ngroup or max(128, (cx.wslot_elems // kc) // 128 * 128)
    gcols = min(gcols, 512)
    tl = tiles(width)
    for g0 in range(0, ncols, gcols):
        gc = min(gcols, ncols - g0)
        wv, wb = cx.load_w(w_ap, 0, kc, n0 + g0, gc)
        for jj, (j0, nsz) in enumerate(chunks128(gc)):
            j = (g0 + j0) // 128
            for (t0, tw) in tl:
                ps = cx.psum()
                for k in range(kc):
                    P.op("pe", lambda e, ps=ps, k=k, j0=j0, nsz=nsz, t0=t0, tw=tw, wv=wv: e.matmul(
                        ps.t[:nsz, :tw], lhsT=wv[:, k, j0:j0 + nsz], rhs=uT.t[:, k, c0 + t0:c0 + t0 + tw],
                        start=(k == 0), stop=(k == kc - 1)), reads=[wb, uT.b], writes=[ps.b], sig=(k == kc - 1))
                evac(ps, j, nsz, t0, tw)


def proj_tm(cx, w_ap, kc, n0, ncols, uT, c0, width, evac):
    P = cx.P
    gcols = min(512, max(1, (cx.wslot_elems // kc)))
    ch = chunks128(width)
    for g0 in range(0, ncols, gcols):
        gc = min(gcols, ncols - g0)
        wv, wb = cx.load_w(w_ap, 0, kc, n0 + g0, gc)
        for ci, (t0, csz) in enumerate(ch):
            ps = cx.psum()
            for k in range(kc):
                P.op("pe", lambda e, ps=ps, k=k, t0=t0, csz=csz, gc=gc, wv=wv: e.matmul(
                    ps.t[:csz, :gc], lhsT=uT.t[:, k, c0 + t0:c0 + t0 + csz], rhs=wv[:, k, 0:gc],
                    start=(k == 0), stop=(k == kc - 1)), reads=[wb, uT.b], writes=[ps.b], sig=(k == kc - 1))
            evac(ps, ci, t0, csz, g0, gc)


def conv_fm(cx, pre, w_t, b_t, j, taps, wo, acc):
    P = cx.P
    kl = taps - 1
    P.op("dve", lambda e: e.tensor_scalar(out=acc.t[:, :wo], in0=pre.t[:, kl:kl + wo], scalar1=w_t.t[:, j, kl:kl + 1],
                                          scalar2=b_t.t[:, j:j + 1], op0=ALU.mult, op1=ALU.add),
         reads=[pre.b, w_t.b, b_t.b], writes=[acc.b])
    for k in range(taps - 1):
        P.op("dve", lambda e, k=k: e.scalar_tensor_tensor(out=acc.t[:, :wo], in0=pre.t[:, k:k + wo],
                                                        scalar=w_t.t[:, j, k:k + 1], in1=acc.t[:, :wo],
                                                        op0=ALU.mult, op1=ALU.add),
             reads=[pre.b, w_t.b, acc.b], writes=[acc.b])


def ffn_fm(cx, hm, uT, w_up, w_down, cw, cb, wh, halo, actT, pre, acc):
    P = cx.P
    for j in range(22):
        pg, pv = pre[(2 * j) % 4], pre[(2 * j + 1) % 4]
        ag, av = acc[(2 * j) % 4], acc[(2 * j + 1) % 4]

        def ev_g(ps, jj, nsz, t0, tw, pg=pg):
            P.op("act", lambda e: e.activation(out=pg.t[:, t0:t0 + tw], in_=ps.t[:, :tw], func=AF.Identity),
                 reads=[ps.b], writes=[pg.b])

        def ev_v(ps, jj, nsz, t0, tw, pv=pv):
            P.op("act", lambda e: e.activation(out=pv.t[:, t0:t0 + tw], in_=ps.t[:, :tw], func=AF.Identity),
                 reads=[ps.b], writes=[pv.b])
        proj_fm(cx, w_up, 8, j * 128, 128, uT, 0, wh + 2, ev_g)
        proj_fm(cx, w_up, 8, DFF + j * 128, 128, uT, 0, wh + 2, ev_v)
        conv_fm(cx, pg, cw, cb, j, 3, wh, ag)
        conv_fm(cx, pv, cw, cb, 22 + j, 3, wh, av)
        P.op("act", lambda e, ag=ag: e.activation(out=ag.t[:, :wh], in_=ag.t[:, :wh], func=AF.Silu),
             reads=[ag.b], writes=[ag.b])
        P.op("dve", lambda e, ag=ag, av=av, j=j: e.tensor_tensor(out=actT.t[:, j, :wh], in0=ag.t[:, :wh],
                                                                in1=av.t[:, :wh], op=ALU.mult),
             reads=[ag.b, av.b], writes=[actT.b])

    def ev_d(ps, j, nsz, t0, tw):
        sl = hm.t[:, j, halo + t0:halo + t0 + tw]
        P.op("dve", lambda e: e.tensor_tensor(out=sl, in0=ps.t[:, :tw], in1=sl, op=ALU.add),
             reads=[ps.b, hm.b], writes=[hm.b])
    proj_fm(cx, w_down, 22, 0, D, actT, 0, wh, ev_d, ngroup=128)


def phase_token(P, kind, io):
    kcm = 16 if kind == "B" else 8
    HIN = 4 if kind == "B" else 2
    WIN = TQ + HIN
    WOUT = WIN - 2
    HW = WOUT // 2
    HWH = HW + 2
    cx = Ctx(P)
    outbufs = []
    hm = P.sb("hm", [128, 8, HWH], F32)
    uT = P.sb("uT", [128, 8, HWH], BF16)
    arena = P.sb("tkar", [128, max(kcm * HWH, 22 * HW)], BF16)
    opT = T(arena.t[:, 0:kcm * HWH].rearrange("p (k w) -> p k w", k=kcm), arena.b)
    actT = T(arena.t[:, 0:22 * HW].rearrange("p (k w) -> p k w", k=22), arena.b)
    pre = [P.sb(f"pre{i}", [128, HWH], F32) for i in range(4)]
    acc = [P.sb(f"acc{i}", [128, HW], F32) for i in range(4)]
    gf = load_small(cx, "gf", io["g_ffn"], [128, 8])
    cw = P.sb("cw", [128, 44, 3], F32)
    P.dma("sp", lambda e: e.dma_start(out=cw.t[:].rearrange("p a b -> p (a b)"), in_=io["cw"]), writes=[cw.b])
    cb = load_small(cx, "cb", io["cb"], [128, 44])
    stg_i = [0]
    resid, w_mix, w_up, w_down = io["resid"], io["w_mix"], io["w_up"], io["w_down"]
    sc, scb = io["sc"], io["scb"]
    if kind == "B":
        gkv = load_small(cx, "gkv", io["g_kv"], [128, 8])
        gq = load_small(cx, "gq", io["g_q"], [128, 8])
        hmask = load_small(cx, "hmask", io["hmask"], [128, 1])
        stg = [P.sb(f"stg{i}", [128, 512], BF16) for i in range(4)]
        outbufs += [s.b for s in stg] + [hm.b]
        w_kv, w_q, h1scr, b2, b2h = io["w_kv"], io["w_q"], io["h1scr"], io["b2"], io["b2h"]
    else:
        gfin = load_small(cx, "gfin", io["g_fin"], [128, 8])
        fo = P.sb("fo", [128, 8, 512], F32)
        outbufs.append(fo.b)
        outo = io["outo"]

    def do_half(a, first):
        for k in range(8):
            P.dma(("sp", "act")[k % 2], lambda e, k=k: e.dma_start(out=hm.t[:, k, :],
                                                                 in_=resid[k * 128:(k + 1) * 128, a:a + HWH]),
                  writes=[hm.b])

        for pt in range(kcm // 4):
            P.dma(("act", "sp")[pt % 2], lambda e, pt=pt: e.dma_start(
                out=opT.t[:, pt * 4:(pt + 1) * 4, :], in_=sc[pt][:, :, a:a + HWH].rearrange("r p c -> p r c")),
                reads=[scb], writes=[opT.b])

        def tap(n):
            import os
            if kind == "D" and first and os.environ.get("FUSE_DTAP", "") == str(n):
                for k in range(8):
                    P.dma("sp", lambda e, k=k: e.dma_start(out=io["dtap"][k * 128:(k + 1) * 128, 0:HWH], in_=hm.t[:, k, :]),
                          reads=[hm.b], sembuf=hm.b)
        tap(1)

        def ev_mix(ps, j, nsz, t0, tw):
            sl = hm.t[:, j, t0:t0 + tw]
            P.op("dve", lambda e: e.tensor_tensor(out=sl, in0=ps.t[:, :tw], in1=sl, op=ALU.add),
                 reads=[ps.b, hm.b], writes=[hm.b])
        proj_fm(cx, w_mix, kcm, 0, D, opT, 0, HWH, ev_mix, ngroup=256 if kcm == 16 else 512)
        tap(2)
        rmsnorm_fm(cx, hm, gf, uT, 0, HWH)
        ffn_fm(cx, hm, uT, w_up, w_down, cw, cb, HW, 2, actT, pre, acc)
        tap(3)

        if kind == "B":
            if first:
                P.op("dve", lambda e: e.tensor_scalar(out=hm.t[:, :, 2:4], in0=hm.t[:, :, 2:4],
                                                      scalar1=hmask.t[:, 0:1], scalar2=None, op0=ALU.mult),
                     reads=[hm.b, hmask.b], writes=[hm.b])
            for k in range(8):
                P.dma("sp", lambda e, k=k: e.dma_start(out=h1scr[k * 128:(k + 1) * 128, a:a + HW],
                                                       in_=hm.t[:, k, 2:2 + HW]), reads=[hm.b], sembuf=hm.b)
            skip = 2 if first else 0
            c0 = 2 + skip
            wk = HW - skip
            rel0 = a + c0 - 4
            rmsnorm_fm(cx, hm, gkv, uT, c0, wk)

            def mk_ev(row0, scale):
                def ev(ps, j, nsz, t0, tw):
                    s = stg[stg_i[0] % 4]
                    stg_i[0] += 1
                    P.op("act", lambda e: e.activation(out=s.t[:, :tw], in_=ps.t[:, :tw], func=AF.Copy, scale=scale),
                         reads=[ps.b], writes=[s.b])
                    P.dma("sp", lambda e: e.dma_start(
                        out=b2[j // 2, row0 + j % 2][:, rel0 + t0:rel0 + t0 + tw], in_=s.t[:, :tw]),
                        reads=[s.b], sembuf=s.b)
                return ev
            proj_fm(cx, w_kv, 8, 0, D, uT, 0, wk, mk_ev(2, 1.0))

            def ev_v(ps, ci, t0, csz, n_off, nw):
                s = stg[stg_i[0] % 4]
                stg_i[0] += 1
                P.op("dve", lambda e: e.tensor_copy(out=s.t[:csz, :nw], in_=ps.t[:csz, :nw]),
                     reads=[ps.b], writes=[s.b])
                for u in range(nw // 128):
                    f0 = n_off + u * 128
                    off = ((f0 // 256) * 6 + 4 + (f0 % 256) // 128) * 128 * TQ + (rel0 + t0) * 128
                    P.dma("sp", lambda e, u=u, off=off: e.dma_start(
                        out=bass.AP(b2h, off, [[128, csz], [1, 128]]), in_=s.t[:csz, u * 128:(u + 1) * 128]),
                        reads=[s.b], sembuf=s.b)
            proj_tm(cx, w_kv, 8, D, D, uT, 0, wk, ev_v)
            rmsnorm_fm(cx, hm, gq, uT, c0, wk)
            proj_fm(cx, w_q, 8, 0, D, uT, 0, wk, mk_ev(0, 0.125))
        else:
            for (t0, tw) in tiles(HW):
                fin_tile(a, t0, tw)

    def fin_tile(a, t0, tw):
        rmsnorm_fm(cx, hm, gfin, fo, 2 + t0, tw)
        for k in range(8):
            P.dma("sp", lambda e, k=k: e.dma_start(out=outo[k * 128:(k + 1) * 128, a + t0:a + t0 + tw],
                                                   in_=fo.t[:, k, :tw]), reads=[fo.b], sembuf=fo.b)

    do_half(0, True)
    do_half(HW, False)
    return outbufs


def phase_attn(P, io):
    oTd = io["b3"]
    sc, sch, scb = io["sc"], io["sch"], io["scb"]
    NB = 65
    kT = P.sb("kT", [128, 4, LB], BF16)
    vS = P.sb("vS", [128, NB, 320], BF16)
    qt = [P.sb(f"qt{i}", [128, 4, 512], BF16) for i in range(2)]
    spA = P.sb("spA", [128, NB, 512], BF16)
    wt = [P.sb(f"wt{i}", [128, 1024], BF16) for i in range(2)]
    tmp = [P.sb(f"tmp{i}", [128, 1024], F32) for i in range(2)]
    carry = P.sb("carry", [128, 512], F32)
    ost = [P.sb(f"ost{i}", [64, 512], BF16) for i in range(2)]
    negU = P.sb("negU", [128, 128], BF16)
    onesb = P.sb("onesb", [128, 128], BF16)
    masks = [P.sb(f"mask{i}", [128, 512], BF16) for i in range(4)]
    ps1 = [P.psum(f"ps1_{i}", [128, 512]) for i in range(2)]
    ps2 = [P.psum(f"ps2_{i}", [128, 512]) for i in range(2)]
    pcb = [P.psum(f"pcb{i}", [128, 512]) for i in range(2)]
    po = [P.psum(f"po{i}", [128, 512]) for i in range(2)]

    P.op("pool", lambda e: e.memset(onesb.t[:], 1.0), writes=[onesb.b])
    P.op("pool", lambda e: e.memset(negU.t[:], -1.0), writes=[negU.b])
    P.op("pool", lambda e: e.memset(kT.t[64:128, :, :], 0.0), writes=[kT.b])
    P.op("pool", lambda e: e.memset(vS.t[:, :, 256:320], 0.0), writes=[vS.b])
    for qq in qt:
        P.op("pool", lambda e, qq=qq: e.memset(qq.t[64:128, :, :], 0.0), writes=[qq.b])
    P.op("pool", lambda e: e.affine_select(out=negU.t[:], in_=negU.t[:], pattern=[[-1, 128]], compare_op=ALU.is_ge,
                                           fill=0.0, base=0, channel_multiplier=1), reads=[negU.b], writes=[negU.b])
    for i in range(4):
        P.op("pool", lambda e, i=i: e.memset(masks[i].t[:], 1.0), writes=[masks[i].b])
        P.op("pool", lambda e, i=i: e.affine_select(out=masks[i].t[:], in_=masks[i].t[:], pattern=[[1, 512]],
                                                    compare_op=ALU.is_gt, fill=0.0, base=-128 * i,
                                                    channel_multiplier=-1), reads=[masks[i].b], writes=[masks[i].b])
    CH = 128 * TQ
    for X in range(2):
        for r in range(4):
            P.dma("sp", lambda e, X=X, r=r: e.dma_start(
                out=kT.t[0:64, 2 * X:2 * X + 2, r * TQ:(r + 1) * TQ],
                in_=sc[2 + X, r].rearrange("(two p) c -> p two c", p=64)), reads=[scb], writes=[kT.b])
    zt = P.sb("zt", [128, 2, 2], BF16)
    P.op("pool", lambda e: e.memset(zt.t[:], 0.0), writes=[zt.b])
    P.dma("sp", lambda e: e.dma_start(out=oTd[0][:, :, 0:2].rearrange("j p c -> p j c"), in_=zt.t[:]),
          reads=[zt.b], sembuf=zt.b)
    def ldv_piece(X, r, lrow, nrow, blk, p0, nblk):
        off = ((4 + X) * 4 + r) * CH + lrow * 128
        if nblk:
            P.dma("act", lambda e: e.dma_start(out=vS.t[:, blk:blk + nblk, X * 128:(X + 1) * 128],
                                              in_=bass.AP(sch, off, [[128, 128], [128 * 128, nblk], [1, 128]])),
                  reads=[scb], writes=[vS.b])
        else:
            P.dma("act", lambda e: e.dma_start(out=vS.t[p0:p0 + nrow, blk, X * 128:(X + 1) * 128],
                                              in_=bass.AP(sch, off, [[128, nrow], [1, 128]])),
                  reads=[scb], writes=[vS.b])
    for X in range(2):
        for r in range(4):
            lo, hi = TQ * r, TQ * r + TQ
            pos = lo
            if pos % 128:
                n = 128 - pos % 128
                ldv_piece(X, r, pos - lo, n, pos // 128, pos % 128, 0)
                pos += n
            nfull = (hi - pos) // 128
            if nfull:
                ldv_piece(X, r, pos - lo, 128 * nfull, pos // 128, 0, nfull)
                pos += 128 * nfull
            if pos < hi:
                ldv_piece(X, r, pos - lo, hi - pos, pos // 128, 0, 0)

    spb = [P.buf(f"sp{b_}") for b_ in range(NB)]
    pex = [P.sb(f"pex{i}", [128, 1024], F32) for i in range(2)]
    onef = P.sb("onef", [128, 1], F32)
    P.op("pool", lambda e: e.memset(onef.t[:], 1.0), writes=[onef.b])
    cnt = {"a": 0, "b": 0, "o": 0, "ap": 0, "bp": 0}

    def tile_geom(j):
        t0 = 512 * j
        tw = 512 if j < 16 else 16
        nblk = 4 * j + 4 if j < 16 else 65
        return t0, tw, nblk

    def load_q(j):
        t0, tw, nblk = tile_geom(j)
        q = qt[j % 2]
        for X in range(2):
            for r in range(4):
                lo, hi = max(t0, TQ * r), min(t0 + tw, TQ * r + TQ)
                if lo < hi:
                    P.dma("sp", lambda e, X=X, r=r, lo=lo, hi=hi: e.dma_start(
                        out=q.t[0:64, 2 * X:2 * X + 2, lo - t0:hi - t0],
                        in_=sc[X, r].rearrange("(two p) c -> p two c", p=64)[:, :, lo - TQ * r:hi - TQ * r]),
                        reads=[scb], writes=[q.b])

    def blkinfo(j, blk):
        ksz = 128 if blk < 64 else 16
        diag = None
        if j < 16 and blk >= 4 * j:
            diag = blk - 4 * j
        if j == 16 and blk == 64:
            diag = 0
        return ksz, diag

    def a_step(j, h, blk):
        t0, tw, nblk = tile_geom(j)
        q = qt[j % 2]
        ksz, diag = blkinfo(j, blk)
        i1 = cnt["a"] % 2
        cnt["a"] += 1
        p1, px = ps1[i1], pex[cnt["ap"] % 2]
        cnt["ap"] += 1
        P.op("pe", lambda e: e.matmul(p1.t[:ksz, :tw], lhsT=kT.t[:, h, blk * 128:blk * 128 + ksz], rhs=q.t[:, h, :tw],
                                      start=True, stop=True), reads=[kT.b, q.b], writes=[p1.b])
        P.op("act", lambda e: e.activation(out=px.t[:ksz, :tw], in_=p1.t[:ksz, :tw], func=AF.Exp),
             reads=[p1.b], writes=[px.b])
        P.op("act", lambda e: e.activation(out=spA.t[:ksz, blk, :tw], in_=px.t[:ksz, :tw], func=AF.Ln,
                                           bias=onef.t[:ksz, :]), reads=[px.b, onef.b], writes=[spb[blk]])
        if diag is not None:
            P.op("dve", lambda e: e.tensor_tensor(out=spA.t[:ksz, blk, :tw], in0=spA.t[:ksz, blk, :tw],
                                                   in1=masks[diag].t[:ksz, :tw], op=ALU.mult),
                 reads=[spb[blk], masks[diag].b], writes=[spb[blk]])

    class BState:
        pass

    def b_begin(j, h):
        st = BState()
        st.j, st.h = j, h
        st.t0, st.tw, st.nblk = tile_geom(j)
        st.pacc = po[cnt["o"] % 2]
        st.first = True
        st.pend = None
        return st

    def b_pv(st, blk, ksz, w, start):
        h, tw, pacc = st.h, st.tw, st.pacc
        P.op("pe", lambda e: e.matmul(pacc.t[:128, :tw], lhsT=vS.t[:ksz, blk, h * 64:h * 64 + 128],
                                      rhs=w.t[:ksz, :tw], start=start, stop=(blk == 0)),
             reads=[vS.b, w.b], writes=[pacc.b])

    def b_step(st, blk):
        j, h, tw = st.j, st.h, st.tw
        q = qt[j % 2]
        ksz, diag = blkinfo(j, blk)
        i2 = cnt["b"] % 2
        cnt["b"] += 1
        ip = cnt["bp"] % 2
        cnt["bp"] += 1
        p2, pc, w, tm = ps2[i2], pcb[i2], wt[ip], tmp[ip]
        first = st.first
        P.op("pe", lambda e: e.matmul(p2.t[:ksz, :tw], lhsT=negU.t[:ksz, :ksz], rhs=spA.t[:ksz, blk, :tw],
                                      start=True, stop=False), reads=[negU.b, spb[blk]], writes=[p2.b], sig=False)
        P.op("pe", lambda e: e.matmul(p2.t[:ksz, :tw], lhsT=kT.t[:, h, blk * 128:blk * 128 + ksz], rhs=q.t[:, h, :tw],
                                      start=False, stop=True), reads=[kT.b, q.b], writes=[p2.b])
        if blk > 0:
            P.op("pe", lambda e: e.matmul(pc.t[:, :tw], lhsT=onesb.t[:ksz, :], rhs=spA.t[:ksz, blk, :tw],
                                          start=True, stop=True), reads=[onesb.b, spb[blk]], writes=[pc.b])
        if st.pend is not None:
            for pe_ in st.pend:
                b_pv(st, *pe_)
        if first:
            P.op("act", lambda e: e.activation(out=w.t[:ksz, :tw], in_=p2.t[:ksz, :tw], func=AF.Exp),
                 reads=[p2.b], writes=[w.b])
        else:
            P.op("dve", lambda e: e.tensor_tensor(out=tm.t[:ksz, :tw], in0=p2.t[:ksz, :tw], in1=carry.t[:ksz, :tw],
                                                  op=ALU.subtract), reads=[p2.b, carry.b], writes=[tm.b])
            P.op("act", lambda e: e.activation(out=w.t[:ksz, :tw], in_=tm.t[:ksz, :tw], func=AF.Exp),
                 reads=[tm.b], writes=[w.b])
        if diag is not None:
            P.op("dve", lambda e: e.tensor_tensor(out=w.t[:ksz, :tw], in0=w.t[:ksz, :tw],
                                                   in1=masks[diag].t[:ksz, :tw], op=ALU.mult),
                 reads=[w.b, masks[diag].b], writes=[w.b])
        if blk > 0:
            if first:
                P.op("dve", lambda e: e.tensor_copy(out=carry.t[:, :tw], in_=pc.t[:, :tw]),
                     reads=[pc.b], writes=[carry.b])
            else:
                P.op("dve", lambda e: e.tensor_tensor(out=carry.t[:, :tw], in0=pc.t[:, :tw], in1=carry.t[:, :tw],
                                                      op=ALU.add), reads=[pc.b, carry.b], writes=[carry.b])
        st.pend = [(blk, ksz, w, first)]
        st.first = False

    def b_end(st):
        h, t0, tw, pacc = st.h, st.t0, st.tw, st.pacc
        for pe_ in st.pend:
            b_pv(st, *pe_)
        cnt["o"] += 1
        o = ost[cnt["o"] % 2]
        P.op("dve", lambda e: e.tensor_copy(out=o.t[:, :tw], in_=pacc.t[:64, :tw]), reads=[pacc.b], writes=[o.b])
        for d in range(4):
            lo, hi = max(t0, TQ * d - 2), min(t0 + tw, TQ * d + TQ)
            if lo < hi:
                c0d = TQ * d - 2
                P.dma("sp", lambda e, d=d, lo=lo, hi=hi, c0d=c0d: e.dma_start(
                    out=oTd[d, h // 2][(h % 2) * 64:(h % 2) * 64 + 64, lo - c0d:hi - c0d], in_=o.t[:, lo - t0:hi - t0]),
                    reads=[o.b], sembuf=o.b)

    def a_pair(j, h, blk):
        t0, tw, nblk = tile_geom(j)
        q = qt[j % 2]
        px = pex[cnt["ap"] % 2]
        cnt["ap"] += 1
        for u, bb in ((1, blk), (0, blk - 1)):
            p1 = ps1[cnt["a"] % 2]
            cnt["a"] += 1
            P.op("pe", lambda e, p1=p1, bb=bb: e.matmul(p1.t[:, :tw], lhsT=kT.t[:, h, bb * 128:bb * 128 + 128],
                                                       rhs=q.t[:, h, :tw], start=True, stop=True),
                 reads=[kT.b, q.b], writes=[p1.b])
            P.op("act", lambda e, p1=p1, u=u: e.activation(out=px.t[:, u * 512:u * 512 + tw], in_=p1.t[:, :tw],
                                                         func=AF.Exp), reads=[p1.b], writes=[px.b])
        P.op("act", lambda e: e.activation(out=spA.t[:, blk - 1:blk + 1, :tw],
                                           in_=px.t[:, :].rearrange("p (u c) -> p u c", u=2)[:, :, :tw],
                                           func=AF.Ln, bias=onef.t[:, :]),
             reads=[px.b, onef.b], writes=[spb[blk - 1], spb[blk]])
        for bb in (blk, blk - 1):
            ksz, diag = blkinfo(j, bb)
            if diag is not None:
                P.op("dve", lambda e, bb=bb, diag=diag: e.tensor_tensor(
                    out=spA.t[:, bb, :tw], in0=spA.t[:, bb, :tw], in1=masks[diag].t[:, :tw], op=ALU.mult),
                    reads=[spb[bb], masks[diag].b], writes=[spb[bb]])

    def b_pair(st, blk):
        j, h, tw = st.j, st.h, st.tw
        q = qt[j % 2]
        ip = cnt["bp"] % 2
        cnt["bp"] += 1
        w, tm = wt[ip], tmp[ip]
        first = st.first
        for u, bb in ((1, blk), (0, blk - 1)):
            i2 = cnt["b"] % 2
            cnt["b"] += 1
            p2, pc = ps2[i2], pcb[i2]
            P.op("pe", lambda e, p2=p2, bb=bb: e.matmul(p2.t[:, :tw], lhsT=negU.t[:, :], rhs=spA.t[:, bb, :tw],
                                                       start=True, stop=False),
                 reads=[negU.b, spb[bb]], writes=[p2.b], sig=False)
            P.op("pe", lambda e, p2=p2, bb=bb: e.matmul(p2.t[:, :tw], lhsT=kT.t[:, h, bb * 128:bb * 128 + 128],
                                                       rhs=q.t[:, h, :tw], start=False, stop=True),
                 reads=[kT.b, q.b], writes=[p2.b])
            if bb > 0:
                P.op("pe", lambda e, pc=pc, bb=bb: e.matmul(pc.t[:, :tw], lhsT=onesb.t[:, :], rhs=spA.t[:, bb, :tw],
                                                           start=True, stop=True),
                     reads=[onesb.b, spb[bb]], writes=[pc.b])
            if u == 1 and st.pend is not None:
                for pe_ in st.pend:
                    b_pv(st, *pe_)
                st.pend = None
            if first and u == 1:
                P.op("dve", lambda e, p2=p2, u=u: e.tensor_copy(out=tm.t[:, u * 512:u * 512 + tw], in_=p2.t[:, :tw]),
                     reads=[p2.b], writes=[tm.b])
            else:
                P.op("dve", lambda e, p2=p2, u=u: e.tensor_tensor(out=tm.t[:, u * 512:u * 512 + tw], in0=p2.t[:, :tw],
                                                                 in1=carry.t[:, :tw], op=ALU.subtract),
                     reads=[p2.b, carry.b], writes=[tm.b])
            if bb > 0:
                if first and u == 1:
                    P.op("dve", lambda e, pc=pc: e.tensor_copy(out=carry.t[:, :tw], in_=pc.t[:, :tw]),
                         reads=[pc.b], writes=[carry.b])
                else:
                    P.op("dve", lambda e, pc=pc: e.tensor_tensor(out=carry.t[:, :tw], in0=pc.t[:, :tw],
                                                                in1=carry.t[:, :tw], op=ALU.add),
                         reads=[pc.b, carry.b], writes=[carry.b])
        P.op("act", lambda e: e.activation(out=w.t[:, :].rearrange("p (u c) -> p u c", u=2)[:, :, :tw],
                                           in_=tm.t[:, :].rearrange("p (u c) -> p u c", u=2)[:, :, :tw], func=AF.Exp),
             reads=[tm.b], writes=[w.b])
        pend = []
        for u, bb in ((1, blk), (0, blk - 1)):
            ksz, diag = blkinfo(j, bb)
            wv = T(w.t[:, u * 512:(u + 1) * 512], w.b)
            if diag is not None:
                P.op("dve", lambda e, wv=wv, diag=diag: e.tensor_tensor(
                    out=wv.t[:, :tw], in0=wv.t[:, :tw], in1=masks[diag].t[:, :tw], op=ALU.mult),
                    reads=[w.b, masks[diag].b], writes=[w.b])
            pend.append((bb, 128, wv, first and u == 1))
        st.pend = pend
        st.first = False

    streams = [(j, h) for j in range(17) for h in range(4)]
    def a_range(j, h, hi, lo):
        blk = hi - 1
        while blk >= lo:
            if j < 16 and blk - 1 >= lo and blk < 64:
                a_pair(j, h, blk)
                blk -= 2
            else:
                a_step(j, h, blk)
                blk -= 1

    load_q(0)
    a_range(0, 0, tile_geom(0)[2], 0)
    for k, (j, h) in enumerate(streams):
        nb_cur = tile_geom(j)[2]
        nxt = streams[k + 1] if k + 1 < len(streams) else None
        if nxt is not None:
            if nxt[1] == 0:
                load_q(nxt[0])
            nb_nxt = tile_geom(nxt[0])[2]
            a_range(nxt[0], nxt[1], nb_nxt, nb_cur)
        st = b_begin(j, h)
        blk = nb_cur - 1
        while blk >= 0:
            if j < 16 and blk >= 1:
                b_pair(st, blk)
                if nxt is not None:
                    a_range(nxt[0], nxt[1], blk + 1, blk - 1)
                blk -= 2
            else:
                b_step(st, blk)
                if nxt is not None:
                    a_range(nxt[0], nxt[1], blk + 1, blk)
                blk -= 1
        b_end(st)
    return [o.b for o in ost] + [zt.b]


def phase_ssd(P, io):
    xTd, wsd, g_in, cwd, cbd = io["xT"], io["wsel"], io["g_in"], io["cw"], io["cb"]
    dtbd, alogd, dskd, ggd, hgo = io["dtb"], io["alog"], io["dsk"], io["gg"], io["b1"]
    cx = Ctx(P, wslot_elems=8 * 1288, nwslots=1, nps=4)
    W = TQ
    CH = chunks128(W)
    NCH = len(CH)
    ptr = [P.psum(f"ptr{i}", [128, 1024], BF16) for i in range(2)]
    pacc = [P.psum(f"pacc{i}", [128, 512]) for i in range(2)]
    gin = load_small(cx, "gin", g_in, [128, 8])
    cw = P.sb("cw", [128, 6, 4], F32)
    P.dma("sp", lambda e: e.dma_start(out=cw.t[:].rearrange("p a b -> p (a b)"), in_=cwd), writes=[cw.b])
    cb = load_small(cx, "cb", cbd, [128, 6])
    dtb = load_small(cx, "dtb", dtbd, [128, 8])
    aneg = load_small(cx, "aneg", alogd, [128, 8])
    dsk = load_small(cx, "dsk", dskd, [128, 8])
    gg = load_small(cx, "gg", ggd, [128, 512])
    P.op("act", lambda e: e.activation(out=aneg.t[:], in_=aneg.t[:], func=AF.Exp), reads=[aneg.b], writes=[aneg.b])
    P.op("dve", lambda e: e.tensor_scalar(out=aneg.t[:], in0=aneg.t[:], scalar1=-1.0, scalar2=None, op0=ALU.mult),
         reads=[aneg.b], writes=[aneg.b])
    wv, wb = cx.load_w(wsd, 0, 8, 0, 1288)

    ident = P.sb("ident", [128, 128], BF16)
    P.op("pool", lambda e: e.memset(ident.t[:], 1.0), writes=[ident.b])
    P.op("pool", lambda e: e.affine_select(out=ident.t[:], in_=ident.t[:], pattern=[[-1, 128]], compare_op=ALU.is_equal,
                                           fill=0.0, base=0, channel_multiplier=1), reads=[ident.b], writes=[ident.b])
    triI = P.sb("triI", [128, 128], F32)
    P.op("pool", lambda e: e.memset(triI.t[:], 1.0), writes=[triI.b])
    P.op("pool", lambda e: e.affine_select(out=triI.t[:], in_=triI.t[:], pattern=[[1, 128]], compare_op=ALU.is_ge,
                                           fill=0.0, base=0, channel_multiplier=-1), reads=[triI.b], writes=[triI.b])
    mstr = P.sb("mstr", [128, 128], F32)
    P.op("pool", lambda e: e.memset(mstr.t[:], 1.0), writes=[mstr.b])
    P.op("pool", lambda e: e.affine_select(out=mstr.t[:], in_=mstr.t[:], pattern=[[-1, 128]], compare_op=ALU.is_gt,
                                           fill=0.0, base=0, channel_multiplier=1), reads=[mstr.b], writes=[mstr.b])

    xins = [P.sb(f"xin{i}", [128, 8, 416], F32) for i in range(2)]
    xcnt = [0]
    uT = P.sb("uT", [128, 8, W + 3], BF16)
    pre = P.sb("pre", [128, W + 3], F32)
    acc = P.sb("acc", [128, W], F32)
    xsT = P.sb("xsT", [128, 6, W], BF16)
    xs_tm = P.sb("xs_tm", [128, NCH, 512], BF16)
    B_tm = P.sb("B_tm", [128, NCH, 128], BF16)
    zs_tm = P.sb("zs_tm", [128, NCH, 512], BF16)
    dt = P.sb("dt", [128, NCH, 8], F32)
    dta = P.sb("dta", [128, NCH, 8], F32)
    cs = P.sb("cs", [128, NCH, 8], F32)
    ecs = P.sb("ecs", [128, NCH, 8], F32)
    wst = P.sb("wst", [128, NCH, 8], F32)
    cdec = P.sb("cdec", [128, NCH, 8], F32)
    onef1 = P.sb("onef1", [128, 1], F32)
    P.op("pool", lambda e: e.memset(onef1.t[:], 1.0), writes=[onef1.b])
    for t_ in (dt, dta, cs, ecs, wst, cdec):
        P.op("pool", lambda e, t_=t_: e.memset(t_.t[:], 0.0), writes=[t_.b])
    S = P.sb("S", [128, 512], F32)
    Sb = P.sb("Sb", [128, 512], BF16)
    P.op("pool", lambda e: e.memset(S.t[:], 0.0), writes=[S.b])
    P.op("pool", lambda e: e.memset(Sb.t[:], 0.0), writes=[Sb.b])
    cbT = P.sb("cbT", [128, 128], F32)
    lh8 = [P.sb(f"lh8_{i}", [128, 8, 128], F32) for i in range(2)]
    dec8 = [P.sb(f"dec8_{i}", [128, 8, 128], F32) for i in range(2)]
    MT8 = [P.sb(f"MT8_{i}", [128, 8, 128], BF16) for i in range(2)]
    xdt = P.sb("xdt", [128, 512], BF16)
    xdte = P.sb("xdte", [128, 512], BF16)
    y1 = P.sb("y1", [128, 512], F32)
    y2 = P.sb("y2", [128, 512], F32)
    hgn = P.sb("hgn", [128, 512], BF16)
    ss = P.sb("ss", [128, 2], F32)
    hst = [P.sb(f"hst{i}", [128, 4, 128], BF16) for i in range(2)]
    cnt = {"h": 0, "l": 0}
    v3 = lambda ap: ap.rearrange("p (h d) -> p h d", h=8)
    bc = lambda ap: ap.unsqueeze(2).to_broadcast([ap.shape[0], 8, 64])

    def do_seg(si):
        s0 = si * W

        def ld(t0, tw):
            xin = xins[xcnt[0] % 2]
            xcnt[0] += 1
            ti = t0 // tw
            for k in range(8):
                P.dma(("sp", "act")[k % 2], lambda e, k=k: e.dma_start(
                    out=xin.t[:, k, :tw], in_=xTd[si, ti, k]), writes=[xin.b])
            rmsnorm_fm(cx, xin, gin, uT, 0, tw, dst_c0=t0)
        for (t0, tw) in tiles(W + 3):
            ld(t0, tw)

        def inproj(jc):
            for (t0, tw) in tiles(W + 3):
                ps = cx.psum()
                for k in range(8):
                    P.op("pe", lambda e, ps=ps, k=k, t0=t0, tw=tw: e.matmul(
                        ps.t[:, :tw], lhsT=wv[:, k, 512 + jc * 128:512 + (jc + 1) * 128], rhs=uT.t[:, k, t0:t0 + tw],
                        start=(k == 0), stop=(k == 7)), reads=[wb, uT.b], writes=[ps.b])
                P.op("act", lambda e, ps=ps, t0=t0, tw=tw: e.activation(out=pre.t[:, t0:t0 + tw], in_=ps.t[:, :tw],
                                                                       func=AF.Identity), reads=[ps.b], writes=[pre.b])
            conv_fm(cx, pre, cw, cb, jc, 4, W, acc)
            P.op("act", lambda e: e.activation(out=xsT.t[:, jc, :], in_=acc.t[:, :], func=AF.Silu),
                 reads=[acc.b], writes=[xsT.b])
        for jc in range(6):
            inproj(jc)

        def tr(ci, c0, csz):
            pt = ptr[ci % 2]
            for jc in range(5):
                P.op("pe", lambda e, jc=jc: e.transpose(pt.t[:csz, jc * 128:(jc + 1) * 128],
                                                        xsT.t[:, jc, c0:c0 + csz], ident.t[:]),
                     reads=[xsT.b, ident.b], writes=[pt.b])
            P.op("dve", lambda e: e.tensor_copy(out=xs_tm.t[:csz, ci, :], in_=pt.t[:csz, 0:512]),
                 reads=[pt.b], writes=[xs_tm.b])
            P.op("dve", lambda e: e.tensor_copy(out=B_tm.t[:csz, ci, :], in_=pt.t[:csz, 512:640]),
                 reads=[pt.b], writes=[B_tm.b])

        def zz(ci, c0, csz):
            ps = cx.psum()
            for k in range(8):
                P.op("pe", lambda e, k=k: e.matmul(ps.t[:csz, :512], lhsT=uT.t[:, k, 3 + c0:3 + c0 + csz],
                                                  rhs=wv[:, k, 0:512], start=(k == 0), stop=(k == 7)),
                     reads=[wb, uT.b], writes=[ps.b])
            P.op("act", lambda e: e.activation(out=zs_tm.t[:csz, ci, :], in_=ps.t[:csz, :512], func=AF.Silu),
                 reads=[ps.b], writes=[zs_tm.b])

        def dd(ci, c0, csz):
            ps = cx.psum()
            for k in range(8):
                P.op("pe", lambda e, k=k: e.matmul(ps.t[:csz, :8], lhsT=uT.t[:, k, 3 + c0:3 + c0 + csz],
                                                  rhs=wv[:, k, 1280:1288], start=(k == 0), stop=(k == 7)),
                     reads=[wb, uT.b], writes=[ps.b])
            P.op("dve", lambda e: e.tensor_tensor(out=dt.t[:csz, ci, :], in0=ps.t[:csz, :8], in1=dtb.t[:csz, :],
                                                  op=ALU.add), reads=[ps.b, dtb.b], writes=[dt.b])

        def da(ci, c0, csz):
            P.op("dve", lambda e: e.tensor_tensor(out=dta.t[:csz, ci, :], in0=dt.t[:csz, ci, :], in1=aneg.t[:csz, :],
                                                  op=ALU.mult), reads=[dt.b, aneg.b], writes=[dta.b])

        def cc(ci, c0, csz):
            ps = cx.psum()
            P.op("pe", lambda e: e.matmul(ps.t[:csz, 0:8], lhsT=triI.t[:csz, :csz], rhs=dta.t[:csz, ci, :],
                                          start=True, stop=True), reads=[triI.b, dta.b], writes=[ps.b])
            P.op("pe", lambda e: e.matmul(ps.t[:, 8:16], lhsT=cx.ones.t[:csz, :], rhs=dta.t[:csz, ci, :],
                                          start=True, stop=True), reads=[cx.ones.b, dta.b], writes=[ps.b])
            P.op("dve", lambda e: e.tensor_copy(out=cs.t[:csz, ci, :], in_=ps.t[:csz, 0:8]),
                 reads=[ps.b], writes=[cs.b])
            P.op("dve", lambda e: e.tensor_tensor(out=wst.t[:csz, ci, :], in0=ps.t[:csz, 8:16],
                                                  in1=cs.t[:csz, ci, :], op=ALU.subtract),
                 reads=[ps.b, cs.b], writes=[wst.b])
            P.op("act", lambda e: e.activation(out=cdec.t[:, ci, :], in_=ps.t[:, 8:16], func=AF.Exp),
                 reads=[ps.b], writes=[cdec.b])
        for ci, (c0, csz) in enumerate(CH):
            tr(ci, c0, csz)
        for ci, (c0, csz) in enumerate(CH):
            zz(ci, c0, csz)
        for ci, (c0, csz) in enumerate(CH):
            dd(ci, c0, csz)
        P.op("act", lambda e: e.activation(out=dt.t[:], in_=dt.t[:], func=AF.Exp), reads=[dt.b], writes=[dt.b])
        P.op("act", lambda e: e.activation(out=dt.t[:], in_=dt.t[:], func=AF.Ln, bias=onef1.t[:, :]),
             reads=[dt.b, onef1.b], writes=[dt.b])
        for ci, (c0, csz) in enumerate(CH):
            da(ci, c0, csz)
        for ci, (c0, csz) in enumerate(CH):
            cc(ci, c0, csz)
        P.op("act", lambda e: e.activation(out=ecs.t[:], in_=cs.t[:], func=AF.Exp), reads=[cs.b], writes=[ecs.b])
        P.op("act", lambda e: e.activation(out=wst.t[:], in_=wst.t[:], func=AF.Exp), reads=[wst.b], writes=[wst.b])
        P.op("dve", lambda e: e.tensor_tensor(out=wst.t[:], in0=wst.t[:], in1=dt.t[:], op=ALU.mult),
             reads=[wst.b, dt.b], writes=[wst.b])
        for ci, (c0, csz) in enumerate(CH):
            do_chunk(s0, ci, c0, csz)

    def do_chunk(s0, ci, c0, csz):
        P.op("dve", lambda e: e.tensor_tensor(out=v3(xdt.t[:csz, :]), in0=v3(xs_tm.t[:csz, ci, :]),
                                              in1=bc(dt.t[:csz, ci, :]), op=ALU.mult),
             reads=[xs_tm.b, dt.b], writes=[xdt.b])
        P.op("pool", lambda e: e.tensor_tensor(out=v3(xdte.t[:csz, :]), in0=v3(xs_tm.t[:csz, ci, :]),
                                               in1=bc(wst.t[:csz, ci, :]), op=ALU.mult),
             reads=[xs_tm.b, wst.b], writes=[xdte.b])
        ps = cx.psum()
        P.op("pe", lambda e: e.matmul(ps.t[:csz, :csz], lhsT=xsT.t[:, 4, c0:c0 + csz], rhs=xsT.t[:, 5, c0:c0 + csz],
                                      start=True, stop=True), reads=[xsT.b], writes=[ps.b])
        P.op("dve", lambda e: e.tensor_tensor(out=cbT.t[:csz, :csz], in0=ps.t[:csz, :csz], in1=triI.t[:csz, :csz],
                                              op=ALU.mult), reads=[ps.b, triI.b], writes=[cbT.b])
        yp = pacc[ci % 2]
        l8, d8, m8 = lh8[ci % 2], dec8[ci % 2], MT8[ci % 2]
        P.op("dve", lambda e: e.tensor_tensor(
            out=l8.t[:csz, :, :csz], in0=mstr.t[:csz, :csz].unsqueeze(1).to_broadcast([csz, 8, csz]),
            in1=dta.t[:csz, ci, :].unsqueeze(2).to_broadcast([csz, 8, csz]), op=ALU.mult),
            reads=[mstr.b, dta.b], writes=[l8.b])
        pgs = [cx.psum(), cx.psum()]
        for hh in range(8):
            pg = pgs[hh // 4]
            P.op("pe", lambda e, pg=pg, hh=hh: e.matmul(pg.t[:csz, (hh % 4) * 128:(hh % 4) * 128 + csz],
                                                       lhsT=l8.t[:csz, hh, :csz], rhs=triI.t[:csz, :csz],
                                                       start=True, stop=True), reads=[l8.b, triI.b], writes=[pg.b])
        for g4 in range(2):
            pg = pgs[g4]
            P.op("act", lambda e, pg=pg, g4=g4: e.activation(
                out=d8.t[:csz, 4 * g4:4 * g4 + 4, :csz],
                in_=pg.t[:csz, :].rearrange("p (h s) -> p h s", h=4)[:, :, :csz], func=AF.Exp),
                reads=[pg.b], writes=[d8.b])
        P.op("dve", lambda e: e.tensor_tensor(
            out=m8.t[:csz, :, :csz], in0=d8.t[:csz, :, :csz],
            in1=cbT.t[:csz, :csz].unsqueeze(1).to_broadcast([csz, 8, csz]), op=ALU.mult),
            reads=[d8.b, cbT.b], writes=[m8.b])
        for hh in range(8):
            P.op("pe", lambda e, hh=hh: e.matmul(yp.t[:csz, hh * 64:(hh + 1) * 64], lhsT=m8.t[:csz, hh, :csz],
                                                rhs=xdt.t[:csz, hh * 64:(hh + 1) * 64], start=True, stop=True),
                 reads=[m8.b, xdt.b], writes=[yp.b])
        po_ = cx.psum()
        P.op("pe", lambda e: e.matmul(po_.t[:csz, :512], lhsT=xsT.t[:, 5, c0:c0 + csz], rhs=Sb.t[:, :],
                                      start=True, stop=True), reads=[xsT.b, Sb.b], writes=[po_.b])
        P.op("dve", lambda e: e.tensor_tensor(out=v3(y1.t[:csz, :]), in0=v3(po_.t[:csz, :512]),
                                              in1=bc(ecs.t[:csz, ci, :]), op=ALU.mult),
             reads=[po_.b, ecs.b], writes=[y1.b])
        P.op("dve", lambda e: e.tensor_tensor(out=y1.t[:csz, :], in0=yp.t[:csz, :512], in1=y1.t[:csz, :], op=ALU.add),
             reads=[yp.b, y1.b], writes=[y1.b])
        P.op("pool", lambda e: e.tensor_tensor(out=v3(y2.t[:csz, :]), in0=v3(xs_tm.t[:csz, ci, :]),
                                               in1=bc(dsk.t[:csz, :]), op=ALU.mult),
             reads=[xs_tm.b, dsk.b], writes=[y2.b])
        P.op("dve", lambda e: e.tensor_tensor(out=y1.t[:csz, :], in0=y1.t[:csz, :], in1=y2.t[:csz, :], op=ALU.add),
             reads=[y1.b, y2.b], writes=[y1.b])
        P.op("dve", lambda e: e.tensor_tensor(out=y1.t[:csz, :], in0=y1.t[:csz, :], in1=zs_tm.t[:csz, ci, :],
                                              op=ALU.mult), reads=[y1.b, zs_tm.b], writes=[y1.b])
        P.op("act", lambda e: e.activation(out=y2.t[:csz, :], in_=y1.t[:csz, :], func=AF.Square,
                                           accum_out=ss.t[:csz, 0:1]), reads=[y1.b], writes=[y2.b, ss.b])
        P.op("act", lambda e: e.activation(out=ss.t[:csz, 1:2], in_=ss.t[:csz, 0:1], func=AF.Ln,
                                           bias=cx.epst.t[:csz, :], scale=1.0 / 512), reads=[ss.b, cx.epst.b],
             writes=[ss.b])
        P.op("act", lambda e: e.activation(out=ss.t[:csz, 1:2], in_=ss.t[:csz, 1:2], func=AF.Exp, scale=-0.5),
             reads=[ss.b], writes=[ss.b])
        P.op("dve", lambda e: e.scalar_tensor_tensor(out=hgn.t[:csz, :], in0=y1.t[:csz, :], scalar=ss.t[:csz, 1:2],
                                                     in1=gg.t[:csz, :], op0=ALU.mult, op1=ALU.mult),
             reads=[y1.b, ss.b, gg.b], writes=[hgn.b])
        pt = ptr[ci % 2]
        hs = hst[cnt["h"] % 2]
        cnt["h"] += 1
        for jc in range(4):
            P.op("pe", lambda e, jc=jc: e.transpose(pt.t[:, jc * 128:jc * 128 + csz], hgn.t[:csz, jc * 128:(jc + 1) * 128],
                                                    ident.t[:csz, :csz]), reads=[hgn.b, ident.b], writes=[pt.b])
        P.op("act", lambda e: e.activation(out=hs.t[:, :, :csz],
                                           in_=pt.t[:, 0:512].rearrange("p (j c) -> p j c", j=4)[:, :, :csz],
                                           func=AF.Identity), reads=[pt.b], writes=[hs.b])
        si = s0 // W
        P.dma("sp", lambda e: e.dma_start(
            out=hgo[si][:, :, 4 + c0:4 + c0 + csz].rearrange("j p c -> p j c"), in_=hs.t[:, :, :csz]),
            reads=[hs.b], sembuf=hs.b)
        if c0 + csz == W and si < 3:
            P.dma("sp", lambda e: e.dma_start(
                out=hgo[si + 1][:, :, 0:4].rearrange("j p c -> p j c"), in_=hs.t[:, :, csz - 4:csz]),
                reads=[hs.b], sembuf=hs.b)
        pn = cx.psum()
        P.op("pe", lambda e: e.matmul(pn.t[:, :512], lhsT=B_tm.t[:csz, ci, :], rhs=xdte.t[:csz, :], start=True,
                                      stop=True), reads=[B_tm.b, xdte.b], writes=[pn.b])
        P.op("dve", lambda e: e.tensor_tensor(out=v3(S.t[:, :]), in0=v3(S.t[:, :]), in1=bc(cdec.t[:, ci, :]),
                                              op=ALU.mult), reads=[S.b, cdec.b], writes=[S.b])
        P.op("dve", lambda e: e.tensor_tensor(out=S.t[:, :], in0=pn.t[:, :512], in1=S.t[:, :], op=ALU.add),
             reads=[pn.b, S.b], writes=[S.b])
        P.op("pool", lambda e: e.tensor_copy(out=Sb.t[:, :], in_=S.t[:, :]), reads=[S.b], writes=[Sb.b])

    zt = P.sb("zt", [128, 4, 4], BF16)
    P.op("pool", lambda e: e.memset(zt.t[:], 0.0), writes=[zt.b])
    P.dma("sp", lambda e: e.dma_start(out=hgo[0][:, :, 0:4].rearrange("j p c -> p j c"), in_=zt.t[:]),
          reads=[zt.b], sembuf=zt.b)
    for si in range(4):
        do_seg(si)
        io["after_seg"](si, [h.b for h in hst] + [zt.b])
    return [h.b for h in hst] + [zt.b]


GROUPS = [[0, 1, 2, 3], [4, 5, 6, 7]]


def build_fused():
    nc = bass.Bass("TRN2", target_bir_lowering=False)

    def dr(n, s, dt=F32, k="ExternalInput"):
        return nc.dram_tensor(n, list(s), dt, kind=k)
    ioA = {"xT": dr("A_xT", [4, 5, 8, 128, 411]).ap(), "wsel": dr("A_wsel", [D, 1288]).ap(), "g_in": dr("A_g_in", [128, 8]).ap(),
           "cw": dr("A_cw", [128, 24]).ap(), "cb": dr("A_cb", [128, 6]).ap(), "dtb": dr("A_dtb", [128, 8]).ap(),
           "alog": dr("A_alog", [128, 8]).ap(), "dsk": dr("A_dsk", [128, 8]).ap(), "gg": dr("A_gg", [128, 512]).ap()}

    def tok_io(pfx, kcm):
        return {"w_mix": dr(pfx + "w_mix", [kcm * 128, D]).ap(), "w_up": dr(pfx + "w_up", [D, 2 * DFF]).ap(),
                "w_down": dr(pfx + "w_down", [DFF, D]).ap(), "g_ffn": dr(pfx + "g_ffn", [128, 8]).ap(),
                "cw": dr(pfx + "cw", [128, 132]).ap(), "cb": dr(pfx + "cb", [128, 44]).ap()}
    ioB = tok_io("B_", 16)
    ioB.update({"resid": dr("B_resid", [D, TQ + 4]).ap(), "w_kv": dr("B_w_kv", [D, 2 * D]).ap(),
                "w_q": dr("B_w_q", [D, D]).ap(), "g_kv": dr("B_g_kv", [128, 8]).ap(), "g_q": dr("B_g_q", [128, 8]).ap(),
                "hmask": dr("B_hmask", [128, 1]).ap()})
    ioD = tok_io("D_", 8)
    ioD.update({"g_fin": dr("D_g_fin", [128, 8]).ap(), "outo": dr("out", [D, TQ], F32, "ExternalOutput").ap()})
    idxd = dr("idx", [1, 1], I32).ap()

    C1, C3, CH2 = TQ + 4, TQ + 2, 128 * TQ
    b1 = nc.dram_tensor("b1", [4, 4, 128, C1], BF16)
    g1 = nc.dram_tensor("g1", [4, 4, 4, 128, C1], BF16)
    b2 = nc.dram_tensor("b2", [4, 6, 128, TQ], BF16)
    g2 = nc.dram_tensor("g2", [4, 6, 4, 128, TQ], BF16)
    b3 = nc.dram_tensor("b3", [4, 2, 128, C3], BF16)
    g3 = nc.dram_tensor("g3", [4, 2, 4, 128, C3], BF16)
    h1scr = nc.dram_tensor("h1scr", [D, TQ + 2], F32)
    dtap = nc.dram_tensor("dtap", [D, 1028], F32)
    sc1 = nc.dram_tensor("sc1", [4, 4, 128, C1], BF16)
    sc2 = nc.dram_tensor("sc2", [6, 4, 128, TQ], BF16)
    sc3 = nc.dram_tensor("sc3", [2, 4, 128, C3], BF16)

    P = Prog(nc)
    regs = {n: P.stack.enter_context(nc.gpsimd.register(n)) for n in ("ridx", "r1", "r2q", "r3", "rtmp")}
    it = P.sb("idxt", [1, 2], I32)
    scr = P.sb("scr", [1, 16], BF16)
    P.persist = P.off
    P.dma("pool", lambda e: e.dma_start(out=it.t[0:1, 0:1], in_=idxd), writes=[it.b])

    def setup(e):
        e.reg_load(regs["ridx"], it.t[0:1, 0:1])
        e.reg_mul(regs["r1"], regs["ridx"], 16 * 128 * C1)
        e.reg_mul(regs["r2q"], regs["ridx"], 24 * CH2)
        e.reg_mul(regs["r3"], regs["ridx"], 8 * 128 * C3)
        return e.memset(scr.t[:], 0.0)
    P.op("pool", setup, reads=[it.b], writes=[scr.b])

    def pull(sct, gt, reg, nrows, ncols):
        b = P.buf("sc")
        P.dma("pool", lambda e: e.dma_start(out=sct.ap().rearrange("a b c d -> (a b c) d"),
                                            in_=bass.AP(gt, reg, [[ncols, nrows], [1, ncols]])), writes=[b])
        return b

    def gather_dest(bt, gt, d, n1, ob):
        for c in range(n1):
            P.collective("AllGather", bt.ap()[d, c].opt(), gt.ap()[d, c].opt(), GROUPS, ob if c == 0 else [])

    def gather_all(bt, gt, n0, n1, ob):
        for d in range(n0):
            gather_dest(bt, gt, d, n1, ob if d == 0 else [])
        P.collective_wait()

    import os
    upto = int(os.environ.get("FUSE_UPTO", "4"))
    nocc = os.environ.get("FUSE_NOCC", "0") == "1"
    if nocc:
        P.collective = lambda *a, **k: None

    def finish():
        dbg = os.environ.get("FUSE_DEBUG", "")
        if dbg:
            src = {"h1scr": h1scr, "sc1": sc1, "sc2": sc2, "sc3": sc3, "b1": b1, "b2": b2, "b3": b3, "dtap": dtap}[dbg]
            shp = list(src.ap().shape)
            n = 1
            for d_ in shp[:-1]:
                n *= d_
            dt_ = F32 if dbg in ("h1scr", "dtap") else BF16
            dbo = nc.dram_tensor("dbg", [n, shp[-1]], dt_, kind="ExternalOutput")
            P.barrier()
            bb = P.buf("dbg")
            names = "abcdefg"[:len(shp) - 1]
            view = src.ap() if len(shp) == 2 else src.ap().rearrange(" ".join(names) + " z -> (" + " ".join(names) + ") z")
            P.dma("sp", lambda e: e.dma_start(out=dbo.ap(), in_=view), writes=[bb])
            P.wait_all("sp", [bb])
        P.barrier()
        print("sems", len(P.sems), "instr", {e: len(q) for e, q in P.q.items()})
        P.emit()
        P.close()
        return nc
    P.phase_start()
    ioA["b1"] = b1.ap()
    ioA["after_seg"] = lambda si, ob: gather_dest(b1, g1, si, 4, ob)
    phase_ssd(P, ioA)
    P.collective_wait()
    if upto == 1:
        return finish()
    P.phase_start()
    ioB.update({"sc": sc1.ap(), "scb": pull(sc1, g1, regs["r1"], 16 * 128, C1), "h1scr": h1scr.ap(),
                "b2": b2.ap(), "b2h": b2})
    ob = phase_token(P, "B", ioB)
    gather_all(b2, g2, 4, 6, ob)
    if upto == 2:
        return finish()
    P.phase_start()
    ob = phase_attn(P, {"b3": b3.ap(), "sc": sc2.ap(), "sch": sc2, "scb": pull(sc2, g2, regs["r2q"], 24 * 128, TQ)})
    gather_all(b3, g3, 4, 2, ob)
    if upto == 3:
        return finish()
    P.phase_start()
    ioD.update({"dtap": dtap.ap(), "sc": sc3.ap(), "scb": pull(sc3, g3, regs["r3"], 8 * 128, C3), "resid": h1scr.ap()})
    ob = phase_token(P, "D", ioD)
    P.wait_all("sp", ob)
    return finish()


_NC_CACHE = {}


def get_nc(key, fn, *a):
    if key not in _NC_CACHE:
        _NC_CACHE[key] = fn(*a)
    return _NC_CACHE[key]


def fm(v, n):
    return np.ascontiguousarray(np.asarray(v, np.float32).reshape(n, 128).T)


def _halo_cols(full, s, halo, W):
    out = np.zeros((full.shape[0], halo + W), full.dtype)
    lo = max(0, s - halo)
    out[:, lo - (s - halo):] = full[:, lo:s + W]
    return out


def _ffn_params(inp, layer, pfx):
    cwT = np.ascontiguousarray(np.asarray(inp["ffn_conv_w"][layer], np.float32).T.reshape(44, 128, 3)
                               .transpose(1, 0, 2).reshape(128, 132))
    return {pfx + "w_up": np.asarray(inp["ffn_w_up"][layer], np.float32),
            pfx + "w_down": np.asarray(inp["ffn_w_down"][layer], np.float32),
            pfx + "g_ffn": fm(inp["ffn_norm"][layer], 8), pfx + "cw": cwT, pfx + "cb": fm(inp["ffn_conv_b"][layer], 44)}


def kernel(**inp):
    inp = {k: np.asarray(v) for k, v in inp.items()}
    x = inp["x"].astype(np.float32)
    nb = x.shape[0]
    h0 = np.concatenate([np.broadcast_to(inp["meta_tokens"][None].astype(np.float32), (nb, 16, D)), x], axis=1)
    h0T = [np.ascontiguousarray(h0[b].T) for b in range(nb)]
    cores = list(range(8))
    nc = get_nc("F", build_fused)
    mA = ssd_maps(inp, h0)
    fB = _ffn_params(inp, 0, "B_")
    fD = _ffn_params(inp, 1, "D_")
    maps = []
    for c in cores:
        b, i = divmod(c, 4)
        m = {"A_" + k: v for k, v in mA[c].items()}
        m.update(fB)
        m.update(fD)
        m.update({"B_resid": _halo_cols(h0T[b], i * TQ, 4, TQ), "B_w_mix": np.ascontiguousarray(np.asarray(inp["ssd_w_out"][0], np.float32)
                                                   .reshape(4, 4, 128, D).transpose(1, 0, 2, 3).reshape(DI, D)),
                  "B_w_kv": np.asarray(inp["w_kv"], np.float32), "B_w_q": np.asarray(inp["sb_w_q"][0], np.float32),
                  "B_g_kv": fm(inp["kv_norm"], 8), "B_g_q": fm(inp["sb_norm"][0], 8),
                  "B_hmask": np.full((128, 1), 0.0 if i == 0 else 1.0, np.float32),
                  "D_w_mix": np.ascontiguousarray(np.asarray(inp["sb_w_o"][0], np.float32)
                                                   .reshape(4, 2, 128, D).transpose(1, 0, 2, 3).reshape(D, D)), "D_g_fin": fm(inp["final_norm"], 8),
                  "idx": np.array([[i]], np.int32)})
        maps.append(m)
    res = run_bass_kernel_spmd(nc, maps, core_ids=cores).results
    out = np.empty((nb, LB - 16, D), np.float32)
    for b in range(nb):
        full = np.concatenate([res[b * 4 + t]["out"] for t in range(4)], axis=1)
        out[b] = full[:, 16:].T
    return out


def ssd_maps(inp, h0):
    w_in = inp["ssd_w_in"][0]
    cwf = inp["ssd_conv_w"][0]
    cbf = inp["ssd_conv_b"][0]
    maps = []
    for c in range(8):
        b, g = divmod(c, 4)
        cols = np.concatenate([np.arange(512 * g, 512 * g + 512), 2048 + np.arange(512 * g, 512 * g + 512),
                               4096 + np.arange(128 * g, 128 * g + 128), 4608 + np.arange(128 * g, 128 * g + 128),
                               5120 + np.arange(8 * g, 8 * g + 8)])
        cch = np.concatenate([np.arange(512 * g, 512 * g + 512), 2048 + np.arange(128 * g, 128 * g + 128),
                              2560 + np.arange(128 * g, 128 * g + 128)])
        xpad = np.zeros((D, 3 + LB), np.float32)
        xpad[:, 3:] = h0[b].T
        xT = np.empty((4, 5, 8, 128, 411), np.float32)
        for si_ in range(4):
            for ti_ in range(5):
                c0_ = si_ * TQ + ti_ * 411
                xT[si_, ti_] = xpad[:, c0_:c0_ + 411].reshape(8, 128, 411)
        rep = lambda v: np.ascontiguousarray(np.broadcast_to(np.asarray(v, np.float32)[None, :], (128, len(v))))
        maps.append({
            "xT": xT, "wsel": np.ascontiguousarray(w_in[:, cols]), "g_in": fm(inp["ssd_norm"][0], 8),
            "cw": np.ascontiguousarray(cwf[:, cch].T.reshape(6, 128, 4).transpose(1, 0, 2).reshape(128, 24)),
            "cb": fm(cbf[cch], 6), "dtb": rep(inp["ssd_dt_bias"][0][8 * g:8 * g + 8]),
            "alog": rep(inp["ssd_a_log"][0][8 * g:8 * g + 8]), "dsk": rep(inp["ssd_d_skip"][0][8 * g:8 * g + 8]),
            "gg": rep(inp["ssd_gate_norm"][0][512 * g:512 * g + 512]),
        })
    return maps
```

```python
import numpy as np
import ml_dtypes
from contextlib import ExitStack
import concourse.bass as bass
import concourse.mybir as mybir
from concourse.bass_utils import run_bass_kernel_spmd

F32 = mybir.dt.float32
BF16 = mybir.dt.bfloat16
AF = mybir.ActivationFunctionType
ALU = mybir.AluOpType
AX = mybir.AxisListType
NPBF = ml_dtypes.bfloat16

D = 1024
LB = 8208
TQ = 2052
DI = 2048
DFF = 2816
EPS = 1e-6
EPOCH = 30000


I32 = mybir.dt.int32
ARENA = 106400
ISZ = {F32: 4, BF16: 2, I32: 4}


class Buf:
    __slots__ = ("name", "w", "r", "dsem", "dcnt")

    def __init__(self, name):
        self.name = name
        self.w = None
        self.r = {}
        self.dsem = None
        self.dcnt = 0


class T:
    __slots__ = ("t", "b")

    def __init__(self, t, b):
        self.t = t
        self.b = b


class Prog:
    ENGS = ("pe", "act", "dve", "pool", "sp")
    EMAP = {"pe": "tensor", "act": "scalar", "dve": "vector", "pool": "gpsimd", "sp": "sync"}

    def __init__(self, nc):
        self.nc = nc
        self.q = {e: [] for e in self.ENGS}
        self.cnt = {e: 0 for e in self.ENGS}
        self.seen = {e: {} for e in self.ENGS}
        self.sems = {}
        self.latest = {}
        self.stack = ExitStack()
        self.nbuf = 0
        self.arena = self.stack.enter_context(nc.sbuf_tensor("arena", [128, ARENA], BF16))
        self.banks = [self.stack.enter_context(nc.psum_tensor(f"bank{i}", [128, 512], F32)) for i in range(8)]
        self.off = 0
        self.persist = 0
        self.nbank = 0
        self.ncc = 0
        self.ccpend = []
        self.dfree = {"sw": [], "hw": []}
        self.dlive = []

    def _sem(self, key):
        if key not in self.sems:
            self.sems[key] = self.stack.enter_context(self.nc.semaphore("s_" + key.replace("#", "_")))
        return self.sems[key]

    def buf(self, name=None):
        self.nbuf += 1
        return Buf(f"{name or 'b'}{self.nbuf}")

    def sb(self, name, shape, dtype, stack=None):
        shape = list(shape)
        n = 1
        for d in shape[1:]:
            n *= d
        nel = (n * ISZ[dtype] + 1) // 2
        nel = (nel + 15) // 16 * 16
        assert self.off + nel <= ARENA, f"SBUF arena overflow at {name}: {self.off}+{nel}"
        v = self.arena[0:shape[0], self.off:self.off + n * ISZ[dtype] // 2]
        self.off += nel
        if dtype != BF16:
            v = v.bitcast(dtype)
        if len(shape) == 3:
            v = v.rearrange("p (a b) -> p a b", a=shape[1])
        return T(v, self.buf(name))

    def psum(self, name, shape, dtype=F32, stack=None):
        assert self.nbank < 8, "out of PSUM banks"
        bk = self.banks[self.nbank]
        self.nbank += 1
        v = bk[:, :]
        if dtype == BF16:
            v = v.bitcast(BF16)
        return T(v, self.buf(name))

    def _waits(self, eng, reads, writes):
        need = {}

        def add(k, v):
            if need.get(k, 0) < v:
                need[k] = v
        for b in reads:
            if b.w:
                add(*b.w)
        for b in writes:
            if b.w:
                add(*b.w)
            for k, v in b.r.items():
                add(k, v)
        if eng == "pe":
            for k in [k for k in need if k.startswith("pe#")]:
                del need[k]
        out = []
        seen = self.seen[eng]
        for k, v in need.items():
            if seen.get(k, 0) < v:
                seen[k] = v
                out.append((k, v))
        return out

    def _mark(self, ev, reads, writes):
        k, v = ev
        if self.latest.get(k, 0) < v:
            self.latest[k] = v
        for b in reads:
            if b.r.get(k, 0) < v:
                b.r[k] = v
        for b in writes:
            b.w = ev
            b.r = {}

    def op(self, eng, fn, reads=(), writes=(), sig=True):
        waits = self._waits(eng, reads, writes)
        c = self.cnt[eng]
        key = f"{eng}#{c // EPOCH}"
        self._sem(key)
        ev = (key, c % EPOCH + 1)
        if sig:
            self.cnt[eng] = c + 1
            self.q[eng].append((waits, fn, (key, 1)))
        else:
            assert eng == "pe"
            self.q[eng].append((waits, fn, None))
        self._mark(ev, reads, writes)
        return ev

    def dma(self, eng, fn, reads=(), writes=(), sembuf=None):
        waits = self._waits(eng, reads, writes)
        sb = sembuf or (writes[0] if writes else reads[0])
        if sb.dsem is None or sb.dcnt >= EPOCH:
            cls = "sw" if eng == "pool" else "hw"
            fl = self.dfree[cls]
            while fl and self.latest.get(fl[-1], 0) >= EPOCH - 4096:
                fl.pop()
            if fl:
                sb.dsem = fl.pop()
                sb.dcnt = self.latest.get(sb.dsem, 0)
            else:
                sb.dsem = f"d{cls}{len(self.sems)}"
                sb.dcnt = 0
                self._sem(sb.dsem)
            self.dlive.append((cls, sb.dsem))
        sb.dcnt += 16
        ev = (sb.dsem, sb.dcnt)
        self.q[eng].append((waits, fn, (sb.dsem, 16)))
        self._mark(ev, reads, writes)
        return ev

    def wait_all(self, eng, bufs):
        waits = self._waits(eng, bufs, bufs)
        self.q[eng].append((waits, None, None))

    def barrier(self):
        for e in self.ENGS:
            seen = self.seen[e]
            waits = []
            for k, v in self.latest.items():
                if seen.get(k, 0) < v:
                    seen[k] = v
                    waits.append((k, v))
            self.q[e].append((waits, None, None))

    def phase_start(self):
        self.barrier()
        self.off = self.persist
        self.nbank = 0
        for cls, k in self.dlive:
            self.dfree[cls].append(k)
        self.dlive = []

    def collective(self, kind, in_ap, out_ap, groups, wait_bufs):
        waits = self._waits("pool", wait_bufs, wait_bufs)
        key = f"cc{self.ncc}"
        self.ncc += 1
        self._sem(key)
        self.q["pool"].append((waits, lambda e: e.collective_compute(kind, ALU.bypass, replica_groups=groups,
                                                                    ins=[in_ap], outs=[out_ap]), (key, 1)))
        self.latest[key] = 1
        self.ccpend.append(key)

    def collective_wait(self):
        waits = [(k, 1) for k in self.ccpend if self.seen["pool"].get(k, 0) < 1]
        for k, _ in waits:
            self.seen["pool"][k] = 1
        self.ccpend = []
        self.q["pool"].append((waits, None, None))

    def emit(self):
        nc = self.nc
        sems = self.sems
        with nc.Block() as block:
            for e in self.ENGS:
                items = self.q[e]

                def body(engine, items=items):
                    for waits, fn, inc in items:
                        for k, v in waits:
                            engine.wait_ge(sems[k], v)
                        if fn is not None:
                            ins = fn(engine)
                            if inc is not None:
                                ins.then_inc(sems[inc[0]], inc[1])
                getattr(block, self.EMAP[e])(body)

    def close(self):
        self.stack.close()


def tiles(width, maxw=512):
    n = -(-width // maxw)
    base, rem = divmod(width, n)
    out, o = [], 0
    for i in range(n):
        w = base + (1 if i < rem else 0)
        out.append((o, w))
        o += w
    return out


def chunks128(width):
    out, o = [], 0
    while o < width:
        w = min(128, width - o)
        out.append((o, w))
        o += w
    return out


class Ctx:
    def __init__(self, P, wslot_elems=4096, nwslots=3, nps=8):
        self.nc = P.nc
        self.P = P
        self.ps = [P.psum(f"ps{i}", [128, 512]) for i in range(nps)]
        self.psi = 0
        self.wslots = [P.sb(f"wsl{i}", [128, wslot_elems], BF16) for i in range(nwslots)]
        self.wsi = 0
        self.wslot_elems = wslot_elems
        self.ones = P.sb("ones_f", [128, 128], F32)
        P.op("pool", lambda e: e.memset(self.ones.t[:], 1.0), writes=[self.ones.b])
        self.epst = P.sb("epst", [128, 1], F32)
        P.op("pool", lambda e: e.memset(self.epst.t[:], EPS), writes=[self.epst.b])
        self.onesb = P.sb("ones_b", [128, 128], BF16)
        P.op("pool", lambda e: e.memset(self.onesb.t[:], 1.0), writes=[self.onesb.b])
        self.sq = [P.sb(f"sq{i}", [128, 512], BF16) for i in range(4)]
        self.rs = [P.sb(f"rs{i}", [128, 512], F32) for i in range(2)]
        self.sqi = 0
        self.rsi = 0

    def psum(self):
        p = self.ps[self.psi % len(self.ps)]
        self.psi += 1
        return p

    def wslot(self):
        w = self.wslots[self.wsi % len(self.wslots)]
        self.wsi += 1
        return w

    def load_w(self, w_ap, k0, kc, n0, ncols):
        assert kc * ncols <= self.wslot_elems, (kc, ncols)
        sl = self.wslot()
        view = sl.t[:, 0:kc * ncols].rearrange("p (k n) -> p k n", k=kc)
        src = w_ap[k0:k0 + kc * 128, n0:n0 + ncols].rearrange("(k p) n -> p k n", p=128)
        self.P.dma("pool", lambda e: e.dma_start(out=view, in_=src), writes=[sl.b])
        return view, sl.b


def load_small(cx, name, dram_ap, shape, dtype=F32):
    t = cx.P.sb(name, shape, dtype)
    cx.P.dma("sp", lambda e: e.dma_start(out=t.t[:], in_=dram_ap), writes=[t.b])
    return t


def rmsnorm_fm(cx, src, g, dst, c0, width, dst_c0=0, kc=8):
    P = cx.P
    for (t0, tw) in tiles(width):
        ps = cx.psum()
        for k in range(kc):
            sq = cx.sq[cx.sqi % 4]
            cx.sqi += 1
            sl = src.t[:, k, c0 + t0:c0 + t0 + tw]
            P.op("pool", lambda e, sq=sq, sl=sl, tw=tw: e.tensor_tensor(out=sq.t[:, :tw], in0=sl, in1=sl, op=ALU.mult),
                 reads=[src.b], writes=[sq.b])
            P.op("pe", lambda e, ps=ps, sq=sq, tw=tw, k=k: e.matmul(ps.t[:, :tw], lhsT=cx.onesb.t[:], rhs=sq.t[:, :tw],
                                                              start=(k == 0), stop=(k == kc - 1)),
                 reads=[cx.onesb.b, sq.b], writes=[ps.b])
        rs = cx.rs[cx.rsi % 2]
        cx.rsi += 1
        P.op("act", lambda e, rs=rs, ps=ps, tw=tw: e.activation(out=rs.t[:, :tw], in_=ps.t[:, :tw], func=AF.Ln,
                                                             bias=cx.epst.t[:], scale=1.0 / (128 * kc)),
             reads=[ps.b, cx.epst.b], writes=[rs.b])
        P.op("act", lambda e, rs=rs, tw=tw: e.activation(out=rs.t[:, :tw], in_=rs.t[:, :tw], func=AF.Exp, scale=-0.5),
             reads=[rs.b], writes=[rs.b])
        for k in range(kc):
            P.op("dve", lambda e, k=k, rs=rs, t0=t0, tw=tw: e.scalar_tensor_tensor(
                out=dst.t[:, k, dst_c0 + t0:dst_c0 + t0 + tw], in0=src.t[:, k, c0 + t0:c0 + t0 + tw],
                scalar=g.t[:, k:k + 1], in1=rs.t[:, :tw], op0=ALU.mult, op1=ALU.mult),
                reads=[src.b, g.b, rs.b], writes=[dst.b])


def proj_fm(cx, w_ap, kc, n0, ncols, uT, c0, width, evac, ngroup=None):
    P = cx.P
    gcols = ngroup or max(128, (cx.wslot_elems // kc) // 128 * 128)
    gcols = min(gcols, 512)
    tl = tiles(width)
    for g0 in range(0, ncols, gcols):
        gc = min(gcols, ncols - g0)
        wv, wb = cx.load_w(w_ap, 0, kc, n0 + g0, gc)
        for jj, (j0, nsz) in enumerate(chunks128(gc)):
            j = (g0 + j0) // 128
            for (t0, tw) in tl:
                ps = cx.psum()
                for k in range(kc):
                    P.op("pe", lambda e, ps=ps, k=k, j0=j0, nsz=nsz, t0=t0, tw=tw, wv=wv: e.matmul(
                        ps.t[:nsz, :tw], lhsT=wv[:, k, j0:j0 + nsz], rhs=uT.t[:, k, c0 + t0:c0 + t0 + tw],
                        start=(k == 0), stop=(k == kc - 1)), reads=[wb, uT.b], writes=[ps.b], sig=(k == kc - 1))
                evac(ps, j, nsz, t0, tw)


def proj_tm(cx, w_ap, kc, n0, ncols, uT, c0, width, evac):
    P = cx.P
    gcols = min(512, max(1, (cx.wslot_elems // kc)))
    ch = chunks128(width)
    for g0 in range(0, ncols, gcols):
        gc = min(gcols, ncols - g0)
        wv, wb = cx.load_w(w_ap, 0, kc, n0 + g0, gc)
        for ci, (t0, csz) in enumerate(ch):
            ps = cx.psum()
            for k in range(kc):
                P.op("pe", lambda e, ps=ps, k=k, t0=t0, csz=csz, gc=gc, wv=wv: e.matmul(
                    ps.t[:csz, :gc], lhsT=uT.t[:, k, c0 + t0:c0 + t0 + csz], rhs=wv[:, k, 0:gc],
                    start=(k == 0), stop=(k == kc - 1)), reads=[wb, uT.b], writes=[ps.b], sig=(k == kc - 1))
            evac(ps, ci, t0, csz, g0, gc)


def conv_fm(cx, pre, w_t, b_t, j, taps, wo, acc):
    P = cx.P
    kl = taps - 1
    P.op("dve", lambda e: e.tensor_scalar(out=acc.t[:, :wo], in0=pre.t[:, kl:kl + wo], scalar1=w_t.t[:, j, kl:kl + 1],
                                          scalar2=b_t.t[:, j:j + 1], op0=ALU.mult, op1=ALU.add),
         reads=[pre.b, w_t.b, b_t.b], writes=[acc.b])
    for k in range(taps - 1):
        P.op("dve", lambda e, k=k: e.scalar_tensor_tensor(out=acc.t[:, :wo], in0=pre.t[:, k:k + wo],
                                                        scalar=w_t.t[:, j, k:k + 1], in1=acc.t[:, :wo],
                                                        op0=ALU.mult, op1=ALU.add),
             reads=[pre.b, w_t.b, acc.b], writes=[acc.b])


def ffn_fm(cx, hm, uT, w_up, w_down, cw, cb, wh, halo, actT, pre, acc):
    P = cx.P
    for j in range(22):
        pg, pv = pre[(2 * j) % 4], pre[(2 * j + 1) % 4]
        ag, av = acc[(2 * j) % 4], acc[(2 * j + 1) % 4]

        def ev_g(ps, jj, nsz, t0, tw, pg=pg):
            P.op("act", lambda e: e.activation(out=pg.t[:, t0:t0 + tw], in_=ps.t[:, :tw], func=AF.Identity),
                 reads=[ps.b], writes=[pg.b])

        def ev_v(ps, jj, nsz, t0, tw, pv=pv):
            P.op("act", lambda e: e.activation(out=pv.t[:, t0:t0 + tw], in_=ps.t[:, :tw], func=AF.Identity),
                 reads=[ps.b], writes=[pv.b])
        proj_fm(cx, w_up, 8, j * 128, 128, uT, 0, wh + 2, ev_g)
        proj_fm(cx, w_up, 8, DFF + j * 128, 128, uT, 0, wh + 2, ev_v)
        conv_fm(cx, pg, cw, cb, j, 3, wh, ag)
        conv_fm(cx, pv, cw, cb, 22 + j, 3, wh, av)
        P.op("act", lambda e, ag=ag: e.activation(out=ag.t[:, :wh], in_=ag.t[:, :wh], func=AF.Silu),
             reads=[ag.b], writes=[ag.b])
        P.op("dve", lambda e, ag=ag, av=av, j=j: e.tensor_tensor(out=actT.t[:, j, :wh], in0=ag.t[:, :wh],
                                                                in1=av.t[:, :wh], op=ALU.mult),
             reads=[ag.b, av.b], writes=[actT.b])

    def ev_d(ps, j, nsz, t0, tw):
        sl = hm.t[:, j, halo + t0:halo + t0 + tw]
        P.op("dve", lambda e: e.tensor_tensor(out=sl, in0=ps.t[:, :tw], in1=sl, op=ALU.add),
             reads=[ps.b, hm.b], writes=[hm.b])
    proj_fm(cx, w_down, 22, 0, D, actT, 0, wh, ev_d, ngroup=128)


def phase_token(P, kind, io):
    kcm = 16 if kind == "B" else 8
    HIN = 4 if kind == "B" else 2
    WIN = TQ + HIN
    WOUT = WIN - 2
    HW = WOUT // 2
    HWH = HW + 2
    cx = Ctx(P)
    outbufs = []
    hm = P.sb("hm", [128, 8, HWH], F32)
    uT = P.sb("uT", [128, 8, HWH], BF16)
    arena = P.sb("tkar", [128, max(kcm * HWH, 22 * HW)], BF16)
    opT = T(arena.t[:, 0:kcm * HWH].rearrange("p (k w) -> p k w", k=kcm), arena.b)
    actT = T(arena.t[:, 0:22 * HW].rearrange("p (k w) -> p k w", k=22), arena.b)
    pre = [P.sb(f"pre{i}", [128, HWH], F32) for i in range(4)]
    acc = [P.sb(f"acc{i}", [128, HW], F32) for i in range(4)]
    gf = load_small(cx, "gf", io["g_ffn"], [128, 8])
    cw = P.sb("cw", [128, 44, 3], F32)
    P.dma("sp", lambda e: e.dma_start(out=cw.t[:].rearrange("p a b -> p (a b)"), in_=io["cw"]), writes=[cw.b])
    cb = load_small(cx, "cb", io["cb"], [128, 44])
    stg_i = [0]
    resid, w_mix, w_up, w_down = io["resid"], io["w_mix"], io["w_up"], io["w_down"]
    sc, scb = io["sc"], io["scb"]
    if kind == "B":
        gkv = load_small(cx, "gkv", io["g_kv"], [128, 8])
        gq = load_small(cx, "gq", io["g_q"], [128, 8])
        hmask = load_small(cx, "hmask", io["hmask"], [128, 1])
        stg = [P.sb(f"stg{i}", [128, 512], BF16) for i in range(4)]
        outbufs += [s.b for s in stg] + [hm.b]
        w_kv, w_q, h1scr, b2, b2h = io["w_kv"], io["w_q"], io["h1scr"], io["b2"], io["b2h"]
    else:
        gfin = load_small(cx, "gfin", io["g_fin"], [128, 8])
        fo = P.sb("fo", [128, 8, 512], F32)
        outbufs.append(fo.b)
        outo = io["outo"]

    def do_half(a, first):
        for k in range(8):
            P.dma("sp", lambda e, k=k: e.dma_start(out=hm.t[:, k, :], in_=resid[k * 128:(k + 1) * 128, a:a + HWH]),
                  writes=[hm.b])

        for pt in range(kcm // 4):
            P.dma("sp", lambda e, pt=pt: e.dma_start(
                out=opT.t[:, pt * 4:(pt + 1) * 4, :], in_=sc[pt][:, :, a:a + HWH].rearrange("r p c -> p r c")),
                reads=[scb], writes=[opT.b])

        def tap(n):
            import os
            if kind == "D" and first and os.environ.get("FUSE_DTAP", "") == str(n):
                for k in range(8):
                    P.dma("sp", lambda e, k=k: e.dma_start(out=io["dtap"][k * 128:(k + 1) * 128, 0:HWH], in_=hm.t[:, k, :]),
                          reads=[hm.b], sembuf=hm.b)
        tap(1)

        def ev_mix(ps, j, nsz, t0, tw):
            sl = hm.t[:, j, t0:t0 + tw]
            P.op("dve", lambda e: e.tensor_tensor(out=sl, in0=ps.t[:, :tw], in1=sl, op=ALU.add),
                 reads=[ps.b, hm.b], writes=[hm.b])
        proj_fm(cx, w_mix, kcm, 0, D, opT, 0, HWH, ev_mix, ngroup=256 if kcm == 16 else 512)
        tap(2)
        rmsnorm_fm(cx, hm, gf, uT, 0, HWH)
        ffn_fm(cx, hm, uT, w_up, w_down, cw, cb, HW, 2, actT, pre, acc)
        tap(3)

        if kind == "B":
            if first:
                P.op("dve", lambda e: e.tensor_scalar(out=hm.t[:, :, 2:4], in0=hm.t[:, :, 2:4],
                                                      scalar1=hmask.t[:, 0:1], scalar2=None, op0=ALU.mult),
                     reads=[hm.b, hmask.b], writes=[hm.b])
            for k in range(8):
                P.dma("sp", lambda e, k=k: e.dma_start(out=h1scr[k * 128:(k + 1) * 128, a:a + HW],
                                                       in_=hm.t[:, k, 2:2 + HW]), reads=[hm.b], sembuf=hm.b)
            skip = 2 if first else 0
            c0 = 2 + skip
            wk = HW - skip
            rel0 = a + c0 - 4
            rmsnorm_fm(cx, hm, gkv, uT, c0, wk)

            def mk_ev(row0, scale):
                def ev(ps, j, nsz, t0, tw):
                    s = stg[stg_i[0] % 4]
                    stg_i[0] += 1
                    P.op("act", lambda e: e.activation(out=s.t[:, :tw], in_=ps.t[:, :tw], func=AF.Copy, scale=scale),
                         reads=[ps.b], writes=[s.b])
                    P.dma("sp", lambda e: e.dma_start(
                        out=b2[j // 2, row0 + j % 2][:, rel0 + t0:rel0 + t0 + tw], in_=s.t[:, :tw]),
                        reads=[s.b], sembuf=s.b)
                return ev
            proj_fm(cx, w_kv, 8, 0, D, uT, 0, wk, mk_ev(2, 1.0))

            def ev_v(ps, ci, t0, csz, n_off, nw):
                s = stg[stg_i[0] % 4]
                stg_i[0] += 1
                P.op("dve", lambda e: e.tensor_copy(out=s.t[:csz, :nw], in_=ps.t[:csz, :nw]),
                     reads=[ps.b], writes=[s.b])
                for u in range(nw // 128):
                    f0 = n_off + u * 128
                    off = ((f0 // 256) * 6 + 4 + (f0 % 256) // 128) * 128 * TQ + (rel0 + t0) * 128
                    P.dma("sp", lambda e, u=u, off=off: e.dma_start(
                        out=bass.AP(b2h, off, [[128, csz], [1, 128]]), in_=s.t[:csz, u * 128:(u + 1) * 128]),
                        reads=[s.b], sembuf=s.b)
            proj_tm(cx, w_kv, 8, D, D, uT, 0, wk, ev_v)
            rmsnorm_fm(cx, hm, gq, uT, c0, wk)
            proj_fm(cx, w_q, 8, 0, D, uT, 0, wk, mk_ev(0, 0.125))
        else:
            for (t0, tw) in tiles(HW):
                fin_tile(a, t0, tw)

    def fin_tile(a, t0, tw):
        rmsnorm_fm(cx, hm, gfin, fo, 2 + t0, tw)
        for k in range(8):
            P.dma("sp", lambda e, k=k: e.dma_start(out=outo[k * 128:(k + 1) * 128, a + t0:a + t0 + tw],
                                                   in_=fo.t[:, k, :tw]), reads=[fo.b], sembuf=fo.b)

    do_half(0, True)
    do_half(HW, False)
    return outbufs


def phase_attn(P, io):
    oTd = io["b3"]
    sc, sch, scb = io["sc"], io["sch"], io["scb"]
    NB = 65
    kT = P.sb("kT", [128, 4, LB], BF16)
    vS = P.sb("vS", [128, NB, 320], BF16)
    qt = [P.sb(f"qt{i}", [128, 4, 512], BF16) for i in range(2)]
    spA = P.sb("spA", [128, NB, 512], BF16)
    wt = [P.sb(f"wt{i}", [128, 1024], BF16) for i in range(2)]
    tmp = [P.sb(f"tmp{i}", [128, 1024], F32) for i in range(2)]
    carry = P.sb("carry", [128, 512], F32)
    ost = [P.sb(f"ost{i}", [64, 512], BF16) for i in range(2)]
    negU = P.sb("negU", [128, 128], BF16)
    onesb = P.sb("onesb", [128, 128], BF16)
    masks = [P.sb(f"mask{i}", [128, 512], BF16) for i in range(4)]
    ps1 = [P.psum(f"ps1_{i}", [128, 512]) for i in range(2)]
    ps2 = [P.psum(f"ps2_{i}", [128, 512]) for i in range(2)]
    pcb = [P.psum(f"pcb{i}", [128, 512]) for i in range(2)]
    po = [P.psum(f"po{i}", [128, 512]) for i in range(2)]

    P.op("pool", lambda e: e.memset(onesb.t[:], 1.0), writes=[onesb.b])
    P.op("pool", lambda e: e.memset(negU.t[:], -1.0), writes=[negU.b])
    P.op("pool", lambda e: e.memset(kT.t[64:128, :, :], 0.0), writes=[kT.b])
    P.op("pool", lambda e: e.memset(vS.t[:, :, 256:320], 0.0), writes=[vS.b])
    for qq in qt:
        P.op("pool", lambda e, qq=qq: e.memset(qq.t[64:128, :, :], 0.0), writes=[qq.b])
    P.op("pool", lambda e: e.affine_select(out=negU.t[:], in_=negU.t[:], pattern=[[-1, 128]], compare_op=ALU.is_ge,
                                           fill=0.0, base=0, channel_multiplier=1), reads=[negU.b], writes=[negU.b])
    for i in range(4):
        P.op("pool", lambda e, i=i: e.memset(masks[i].t[:], 1.0), writes=[masks[i].b])
        P.op("pool", lambda e, i=i: e.affine_select(out=masks[i].t[:], in_=masks[i].t[:], pattern=[[1, 512]],
                                                    compare_op=ALU.is_gt, fill=0.0, base=-128 * i,
                                                    channel_multiplier=-1), reads=[masks[i].b], writes=[masks[i].b])
    CH = 128 * TQ
    for X in range(2):
        for r in range(4):
            P.dma("sp", lambda e, X=X, r=r: e.dma_start(
                out=kT.t[0:64, 2 * X:2 * X + 2, r * TQ:(r + 1) * TQ],
                in_=sc[2 + X, r].rearrange("(two p) c -> p two c", p=64)), reads=[scb], writes=[kT.b])
    zt = P.sb("zt", [128, 2, 2], BF16)
    P.op("pool", lambda e: e.memset(zt.t[:], 0.0), writes=[zt.b])
    P.dma("sp", lambda e: e.dma_start(out=oTd[0][:, :, 0:2].rearrange("j p c -> p j c"), in_=zt.t[:]),
          reads=[zt.b], sembuf=zt.b)
    def ldv_piece(X, r, lrow, nrow, blk, p0, nblk):
        off = ((4 + X) * 4 + r) * CH + lrow * 128
        if nblk:
            P.dma("sp", lambda e: e.dma_start(out=vS.t[:, blk:blk + nblk, X * 128:(X + 1) * 128],
                                              in_=bass.AP(sch, off, [[128, 128], [128 * 128, nblk], [1, 128]])),
                  reads=[scb], writes=[vS.b])
        else:
            P.dma("sp", lambda e: e.dma_start(out=vS.t[p0:p0 + nrow, blk, X * 128:(X + 1) * 128],
                                              in_=bass.AP(sch, off, [[128, nrow], [1, 128]])),
                  reads=[scb], writes=[vS.b])
    for X in range(2):
        for r in range(4):
            lo, hi = TQ * r, TQ * r + TQ
            pos = lo
            if pos % 128:
                n = 128 - pos % 128
                ldv_piece(X, r, pos - lo, n, pos // 128, pos % 128, 0)
                pos += n
            nfull = (hi - pos) // 128
            if nfull:
                ldv_piece(X, r, pos - lo, 128 * nfull, pos // 128, 0, nfull)
                pos += 128 * nfull
            if pos < hi:
                ldv_piece(X, r, pos - lo, hi - pos, pos // 128, 0, 0)

    spb = [P.buf(f"sp{b_}") for b_ in range(NB)]
    pex = [P.sb(f"pex{i}", [128, 1024], F32) for i in range(2)]
    onef = P.sb("onef", [128, 1], F32)
    P.op("pool", lambda e: e.memset(onef.t[:], 1.0), writes=[onef.b])
    cnt = {"a": 0, "b": 0, "o": 0, "ap": 0, "bp": 0}

    def tile_geom(j):
        t0 = 512 * j
        tw = 512 if j < 16 else 16
        nblk = 4 * j + 4 if j < 16 else 65
        return t0, tw, nblk

    def load_q(j):
        t0, tw, nblk = tile_geom(j)
        q = qt[j % 2]
        for X in range(2):
            for r in range(4):
                lo, hi = max(t0, TQ * r), min(t0 + tw, TQ * r + TQ)
                if lo < hi:
                    P.dma("sp", lambda e, X=X, r=r, lo=lo, hi=hi: e.dma_start(
                        out=q.t[0:64, 2 * X:2 * X + 2, lo - t0:hi - t0],
                        in_=sc[X, r].rearrange("(two p) c -> p two c", p=64)[:, :, lo - TQ * r:hi - TQ * r]),
                        reads=[scb], writes=[q.b])

    def blkinfo(j, blk):
        ksz = 128 if blk < 64 else 16
        diag = None
        if j < 16 and blk >= 4 * j:
            diag = blk - 4 * j
        if j == 16 and blk == 64:
            diag = 0
        return ksz, diag

    def a_step(j, h, blk):
        t0, tw, nblk = tile_geom(j)
        q = qt[j % 2]
        ksz, diag = blkinfo(j, blk)
        i1 = cnt["a"] % 2
        cnt["a"] += 1
        p1, px = ps1[i1], pex[cnt["ap"] % 2]
        cnt["ap"] += 1
        P.op("pe", lambda e: e.matmul(p1.t[:ksz, :tw], lhsT=kT.t[:, h, blk * 128:blk * 128 + ksz], rhs=q.t[:, h, :tw],
                                      start=True, stop=True), reads=[kT.b, q.b], writes=[p1.b])
        P.op("act", lambda e: e.activation(out=px.t[:ksz, :tw], in_=p1.t[:ksz, :tw], func=AF.Exp),
             reads=[p1.b], writes=[px.b])
        P.op("act", lambda e: e.activation(out=spA.t[:ksz, blk, :tw], in_=px.t[:ksz, :tw], func=AF.Ln,
                                           bias=onef.t[:ksz, :]), reads=[px.b, onef.b], writes=[spb[blk]])
        if diag is not None:
            P.op("dve", lambda e: e.tensor_tensor(out=spA.t[:ksz, blk, :tw], in0=spA.t[:ksz, blk, :tw],
                                                   in1=masks[diag].t[:ksz, :tw], op=ALU.mult),
                 reads=[spb[blk], masks[diag].b], writes=[spb[blk]])

    class BState:
        pass

    def b_begin(j, h):
        st = BState()
        st.j, st.h = j, h
        st.t0, st.tw, st.nblk = tile_geom(j)
        st.pacc = po[cnt["o"] % 2]
        st.first = True
        st.pend = None
        return st

    def b_pv(st, blk, ksz, w, start):
        h, tw, pacc = st.h, st.tw, st.pacc
        P.op("pe", lambda e: e.matmul(pacc.t[:128, :tw], lhsT=vS.t[:ksz, blk, h * 64:h * 64 + 128],
                                      rhs=w.t[:ksz, :tw], start=start, stop=(blk == 0)),
             reads=[vS.b, w.b], writes=[pacc.b])

    def b_step(st, blk):
        j, h, tw = st.j, st.h, st.tw
        q = qt[j % 2]
        ksz, diag = blkinfo(j, blk)
        i2 = cnt["b"] % 2
        cnt["b"] += 1
        ip = cnt["bp"] % 2
        cnt["bp"] += 1
        p2, pc, w, tm = ps2[i2], pcb[i2], wt[ip], tmp[ip]
        first = st.first
        P.op("pe", lambda e: e.matmul(p2.t[:ksz, :tw], lhsT=negU.t[:ksz, :ksz], rhs=spA.t[:ksz, blk, :tw],
                                      start=True, stop=False), reads=[negU.b, spb[blk]], writes=[p2.b], sig=False)
        P.op("pe", lambda e: e.matmul(p2.t[:ksz, :tw], lhsT=kT.t[:, h, blk * 128:blk * 128 + ksz], rhs=q.t[:, h, :tw],
                                      start=False, stop=True), reads=[kT.b, q.b], writes=[p2.b])
        if blk > 0:
            P.op("pe", lambda e: e.matmul(pc.t[:, :tw], lhsT=onesb.t[:ksz, :], rhs=spA.t[:ksz, blk, :tw],
                                          start=True, stop=True), reads=[onesb.b, spb[blk]], writes=[pc.b])
        if st.pend is not None:
            for pe_ in st.pend:
                b_pv(st, *pe_)
        if first:
            P.op("act", lambda e: e.activation(out=w.t[:ksz, :tw], in_=p2.t[:ksz, :tw], func=AF.Exp),
                 reads=[p2.b], writes=[w.b])
        else:
            P.op("dve", lambda e: e.tensor_tensor(out=tm.t[:ksz, :tw], in0=p2.t[:ksz, :tw], in1=carry.t[:ksz, :tw],
                                                  op=ALU.subtract), reads=[p2.b, carry.b], writes=[tm.b])
            P.op("act", lambda e: e.activation(out=w.t[:ksz, :tw], in_=tm.t[:ksz, :tw], func=AF.Exp),
                 reads=[tm.b], writes=[w.b])
        if diag is not None:
            P.op("dve", lambda e: e.tensor_tensor(out=w.t[:ksz, :tw], in0=w.t[:ksz, :tw],
                                                   in1=masks[diag].t[:ksz, :tw], op=ALU.mult),
                 reads=[w.b, masks[diag].b], writes=[w.b])
        if blk > 0:
            if first:
                P.op("dve", lambda e: e.tensor_copy(out=carry.t[:, :tw], in_=pc.t[:, :tw]),
                     reads=[pc.b], writes=[carry.b])
            else:
                P.op("dve", lambda e: e.tensor_tensor(out=carry.t[:, :tw], in0=pc.t[:, :tw], in1=carry.t[:, :tw],
                                                      op=ALU.add), reads=[pc.b, carry.b], writes=[carry.b])
        st.pend = [(blk, ksz, w, first)]
        st.first = False

    def b_end(st):
        h, t0, tw, pacc = st.h, st.t0, st.tw, st.pacc
        for pe_ in st.pend:
            b_pv(st, *pe_)
        cnt["o"] += 1
        o = ost[cnt["o"] % 2]
        P.op("dve", lambda e: e.tensor_copy(out=o.t[:, :tw], in_=pacc.t[:64, :tw]), reads=[pacc.b], writes=[o.b])
        for d in range(4):
            lo, hi = max(t0, TQ * d - 2), min(t0 + tw, TQ * d + TQ)
            if lo < hi:
                c0d = TQ * d - 2
                P.dma("sp", lambda e, d=d, lo=lo, hi=hi, c0d=c0d: e.dma_start(
                    out=oTd[d, h // 2][(h % 2) * 64:(h % 2) * 64 + 64, lo - c0d:hi - c0d], in_=o.t[:, lo - t0:hi - t0]),
                    reads=[o.b], sembuf=o.b)

    def a_pair(j, h, blk):
        t0, tw, nblk = tile_geom(j)
        q = qt[j % 2]
        px = pex[cnt["ap"] % 2]
        cnt["ap"] += 1
        for u, bb in ((1, blk), (0, blk - 1)):
            p1 = ps1[cnt["a"] % 2]
            cnt["a"] += 1
            P.op("pe", lambda e, p1=p1, bb=bb: e.matmul(p1.t[:, :tw], lhsT=kT.t[:, h, bb * 128:bb * 128 + 128],
                                                       rhs=q.t[:, h, :tw], start=True, stop=True),
                 reads=[kT.b, q.b], writes=[p1.b])
            P.op("act", lambda e, p1=p1, u=u: e.activation(out=px.t[:, u * 512:u * 512 + tw], in_=p1.t[:, :tw],
                                                         func=AF.Exp), reads=[p1.b], writes=[px.b])
        P.op("act", lambda e: e.activation(out=spA.t[:, blk - 1:blk + 1, :tw],
                                           in_=px.t[:, :].rearrange("p (u c) -> p u c", u=2)[:, :, :tw],
                                           func=AF.Ln, bias=onef.t[:, :]),
             reads=[px.b, onef.b], writes=[spb[blk - 1], spb[blk]])
        for bb in (blk, blk - 1):
            ksz, diag = blkinfo(j, bb)
            if diag is not None:
                P.op("dve", lambda e, bb=bb, diag=diag: e.tensor_tensor(
                    out=spA.t[:, bb, :tw], in0=spA.t[:, bb, :tw], in1=masks[diag].t[:, :tw], op=ALU.mult),
                    reads=[spb[bb], masks[diag].b], writes=[spb[bb]])

    def b_pair(st, blk):
        j, h, tw = st.j, st.h, st.tw
        q = qt[j % 2]
        ip = cnt["bp"] % 2
        cnt["bp"] += 1
        w, tm = wt[ip], tmp[ip]
        first = st.first
        for u, bb in ((1, blk), (0, blk - 1)):
            i2 = cnt["b"] % 2
            cnt["b"] += 1
            p2, pc = ps2[i2], pcb[i2]
            P.op("pe", lambda e, p2=p2, bb=bb: e.matmul(p2.t[:, :tw], lhsT=negU.t[:, :], rhs=spA.t[:, bb, :tw],
                                                       start=True, stop=False),
                 reads=[negU.b, spb[bb]], writes=[p2.b], sig=False)
            P.op("pe", lambda e, p2=p2, bb=bb: e.matmul(p2.t[:, :tw], lhsT=kT.t[:, h, bb * 128:bb * 128 + 128],
                                                       rhs=q.t[:, h, :tw], start=False, stop=True),
                 reads=[kT.b, q.b], writes=[p2.b])
            if bb > 0:
                P.op("pe", lambda e, pc=pc, bb=bb: e.matmul(pc.t[:, :tw], lhsT=onesb.t[:, :], rhs=spA.t[:, bb, :tw],
                                                           start=True, stop=True),
                     reads=[onesb.b, spb[bb]], writes=[pc.b])
            if u == 1 and st.pend is not None:
                for pe_ in st.pend:
                    b_pv(st, *pe_)
                st.pend = None
            if first and u == 1:
                P.op("dve", lambda e, p2=p2, u=u: e.tensor_copy(out=tm.t[:, u * 512:u * 512 + tw], in_=p2.t[:, :tw]),
                     reads=[p2.b], writes=[tm.b])
            else:
                P.op("dve", lambda e, p2=p2, u=u: e.tensor_tensor(out=tm.t[:, u * 512:u * 512 + tw], in0=p2.t[:, :tw],
                                                                 in1=carry.t[:, :tw], op=ALU.subtract),
                     reads=[p2.b, carry.b], writes=[tm.b])
            if bb > 0:
                if first and u == 1:
                    P.op("dve", lambda e, pc=pc: e.tensor_copy(out=carry.t[:, :tw], in_=pc.t[:, :tw]),
                         reads=[pc.b], writes=[carry.b])
                else:
                    P.op("dve", lambda e, pc=pc: e.tensor_tensor(out=carry.t[:, :tw], in0=pc.t[:, :tw],
                                                                in1=carry.t[:, :tw], op=ALU.add),
                         reads=[pc.b, carry.b], writes=[carry.b])
        P.op("act", lambda e: e.activation(out=w.t[:, :].rearrange("p (u c) -> p u c", u=2)[:, :, :tw],
                                           in_=tm.t[:, :].rearrange("p (u c) -> p u c", u=2)[:, :, :tw], func=AF.Exp),
             reads=[tm.b], writes=[w.b])
        pend = []
        for u, bb in ((1, blk), (0, blk - 1)):
            ksz, diag = blkinfo(j, bb)
            wv = T(w.t[:, u * 512:(u + 1) * 512], w.b)
            if diag is not None:
                P.op("dve", lambda e, wv=wv, diag=diag: e.tensor_tensor(
                    out=wv.t[:, :tw], in0=wv.t[:, :tw], in1=masks[diag].t[:, :tw], op=ALU.mult),
                    reads=[w.b, masks[diag].b], writes=[w.b])
            pend.append((bb, 128, wv, first and u == 1))
        st.pend = pend
        st.first = False

    streams = [(j, h) for j in range(17) for h in range(4)]
    def a_range(j, h, hi, lo):
        blk = hi - 1
        while blk >= lo:
            if j < 16 and blk - 1 >= lo and blk < 64:
                a_pair(j, h, blk)
                blk -= 2
            else:
                a_step(j, h, blk)
                blk -= 1

    load_q(0)
    a_range(0, 0, tile_geom(0)[2], 0)
    for k, (j, h) in enumerate(streams):
        nb_cur = tile_geom(j)[2]
        nxt = streams[k + 1] if k + 1 < len(streams) else None
        if nxt is not None:
            if nxt[1] == 0:
                load_q(nxt[0])
            nb_nxt = tile_geom(nxt[0])[2]
            a_range(nxt[0], nxt[1], nb_nxt, nb_cur)
        st = b_begin(j, h)
        blk = nb_cur - 1
        while blk >= 0:
            if j < 16 and blk >= 1:
                b_pair(st, blk)
                if nxt is not None:
                    a_range(nxt[0], nxt[1], blk + 1, blk - 1)
                blk -= 2
            else:
                b_step(st, blk)
                if nxt is not None:
                    a_range(nxt[0], nxt[1], blk + 1, blk)
                blk -= 1
        b_end(st)
    return [o.b for o in ost] + [zt.b]


def phase_ssd(P, io):
    xTd, wsd, g_in, cwd, cbd = io["xT"], io["wsel"], io["g_in"], io["cw"], io["cb"]
    dtbd, alogd, dskd, ggd, hgo = io["dtb"], io["alog"], io["dsk"], io["gg"], io["b1"]
    cx = Ctx(P, wslot_elems=8 * 1288, nwslots=1, nps=4)
    W = TQ
    CH = chunks128(W)
    NCH = len(CH)
    ptr = [P.psum(f"ptr{i}", [128, 1024], BF16) for i in range(2)]
    pacc = [P.psum(f"pacc{i}", [128, 512]) for i in range(2)]
    gin = load_small(cx, "gin", g_in, [128, 8])
    cw = P.sb("cw", [128, 6, 4], F32)
    P.dma("sp", lambda e: e.dma_start(out=cw.t[:].rearrange("p a b -> p (a b)"), in_=cwd), writes=[cw.b])
    cb = load_small(cx, "cb", cbd, [128, 6])
    dtb = load_small(cx, "dtb", dtbd, [128, 8])
    aneg = load_small(cx, "aneg", alogd, [128, 8])
    dsk = load_small(cx, "dsk", dskd, [128, 8])
    gg = load_small(cx, "gg", ggd, [128, 512])
    P.op("act", lambda e: e.activation(out=aneg.t[:], in_=aneg.t[:], func=AF.Exp), reads=[aneg.b], writes=[aneg.b])
    P.op("dve", lambda e: e.tensor_scalar(out=aneg.t[:], in0=aneg.t[:], scalar1=-1.0, scalar2=None, op0=ALU.mult),
         reads=[aneg.b], writes=[aneg.b])
    wv, wb = cx.load_w(wsd, 0, 8, 0, 1288)

    ident = P.sb("ident", [128, 128], BF16)
    P.op("pool", lambda e: e.memset(ident.t[:], 1.0), writes=[ident.b])
    P.op("pool", lambda e: e.affine_select(out=ident.t[:], in_=ident.t[:], pattern=[[-1, 128]], compare_op=ALU.is_equal,
                                           fill=0.0, base=0, channel_multiplier=1), reads=[ident.b], writes=[ident.b])
    triI = P.sb("triI", [128, 128], F32)
    P.op("pool", lambda e: e.memset(triI.t[:], 1.0), writes=[triI.b])
    P.op("pool", lambda e: e.affine_select(out=triI.t[:], in_=triI.t[:], pattern=[[1, 128]], compare_op=ALU.is_ge,
                                           fill=0.0, base=0, channel_multiplier=-1), reads=[triI.b], writes=[triI.b])
    mstr = P.sb("mstr", [128, 128], F32)
    P.op("pool", lambda e: e.memset(mstr.t[:], 1.0), writes=[mstr.b])
    P.op("pool", lambda e: e.affine_select(out=mstr.t[:], in_=mstr.t[:], pattern=[[-1, 128]], compare_op=ALU.is_gt,
                                           fill=0.0, base=0, channel_multiplier=1), reads=[mstr.b], writes=[mstr.b])

    xins = [P.sb(f"xin{i}", [128, 8, 416], F32) for i in range(2)]
    xcnt = [0]
    prefetched = {}
    uT = P.sb("uT", [128, 8, W + 3], BF16)
    pre = P.sb("pre", [128, W + 3], F32)
    acc = P.sb("acc", [128, W], F32)
    xsT = P.sb("xsT", [128, 6, W], BF16)
    xs_tm = P.sb("xs_tm", [128, NCH, 512], BF16)
    B_tm = P.sb("B_tm", [128, NCH, 128], BF16)
    zs_tm = P.sb("zs_tm", [128, NCH, 512], BF16)
    dt = P.sb("dt", [128, NCH, 8], F32)
    dta = P.sb("dta", [128, NCH, 8], F32)
    cs = P.sb("cs", [128, NCH, 8], F32)
    ecs = P.sb("ecs", [128, NCH, 8], F32)
    wst = P.sb("wst", [128, NCH, 8], F32)
    cdec = P.sb("cdec", [128, NCH, 8], F32)
    onef1 = P.sb("onef1", [128, 1], F32)
    P.op("pool", lambda e: e.memset(onef1.t[:], 1.0), writes=[onef1.b])
    for t_ in (dt, dta, cs, ecs, wst, cdec):
        P.op("pool", lambda e, t_=t_: e.memset(t_.t[:], 0.0), writes=[t_.b])
    S = P.sb("S", [128, 512], F32)
    Sb = P.sb("Sb", [128, 512], BF16)
    P.op("pool", lambda e: e.memset(S.t[:], 0.0), writes=[S.b])
    P.op("pool", lambda e: e.memset(Sb.t[:], 0.0), writes=[Sb.b])
    cbT = P.sb("cbT", [128, 128], F32)
    lh8 = [P.sb(f"lh8_{i}", [128, 8, 128], F32) for i in range(2)]
    dec8 = [P.sb(f"dec8_{i}", [128, 8, 128], F32) for i in range(2)]
    MT8 = [P.sb(f"MT8_{i}", [128, 8, 128], BF16) for i in range(2)]
    xdt = P.sb("xdt", [128, 512], BF16)
    xdte = P.sb("xdte", [128, 512], BF16)
    y1 = P.sb("y1", [128, 512], F32)
    y2 = P.sb("y2", [128, 512], F32)
    hgn = P.sb("hgn", [128, 512], BF16)
    ss = P.sb("ss", [128, 2], F32)
    hst = [P.sb(f"hst{i}", [128, 4, 128], BF16) for i in range(2)]
    cnt = {"h": 0, "l": 0}
    v3 = lambda ap: ap.rearrange("p (h d) -> p h d", h=8)
    bc = lambda ap: ap.unsqueeze(2).to_broadcast([ap.shape[0], 8, 64])

    def do_seg(si):
        s0 = si * W

        def fetch(sj, ti, tw):
            xin = xins[xcnt[0] % 2]
            xcnt[0] += 1
            for k in range(8):
                P.dma(("sp", "act")[k % 2], lambda e, k=k: e.dma_start(
                    out=xin.t[:, k, :tw], in_=xTd[sj, ti, k]), writes=[xin.b])
            return xin

        def ld(t0, tw):
            ti = t0 // tw
            xin = prefetched.pop((si, ti), None) or fetch(si, ti, tw)
            rmsnorm_fm(cx, xin, gin, uT, 0, tw, dst_c0=t0)
        for (t0, tw) in tiles(W + 3):
            ld(t0, tw)
        if si < 3:
            for ti in range(2):
                prefetched[(si + 1, ti)] = fetch(si + 1, ti, 411)

        def inproj(jc):
            for (t0, tw) in tiles(W + 3):
                ps = cx.psum()
                for k in range(8):
                    P.op("pe", lambda e, ps=ps, k=k, t0=t0, tw=tw: e.matmul(
                        ps.t[:, :tw], lhsT=wv[:, k, 512 + jc * 128:512 + (jc + 1) * 128], rhs=uT.t[:, k, t0:t0 + tw],
                        start=(k == 0), stop=(k == 7)), reads=[wb, uT.b], writes=[ps.b])
                P.op("act", lambda e, ps=ps, t0=t0, tw=tw: e.activation(out=pre.t[:, t0:t0 + tw], in_=ps.t[:, :tw],
                                                                       func=AF.Identity), reads=[ps.b], writes=[pre.b])
            conv_fm(cx, pre, cw, cb, jc, 4, W, acc)
            P.op("act", lambda e: e.activation(out=xsT.t[:, jc, :], in_=acc.t[:, :], func=AF.Silu),
                 reads=[acc.b], writes=[xsT.b])
        for jc in range(6):
            inproj(jc)

        def tr(ci, c0, csz):
            pt = ptr[ci % 2]
            for jc in range(5):
                P.op("pe", lambda e, jc=jc: e.transpose(pt.t[:csz, jc * 128:(jc + 1) * 128],
                                                        xsT.t[:, jc, c0:c0 + csz], ident.t[:]),
                     reads=[xsT.b, ident.b], writes=[pt.b])
            P.op("dve", lambda e: e.tensor_copy(out=xs_tm.t[:csz, ci, :], in_=pt.t[:csz, 0:512]),
                 reads=[pt.b], writes=[xs_tm.b])
            P.op("dve", lambda e: e.tensor_copy(out=B_tm.t[:csz, ci, :], in_=pt.t[:csz, 512:640]),
                 reads=[pt.b], writes=[B_tm.b])

        def zz(ci, c0, csz):
            ps = cx.psum()
            for k in range(8):
                P.op("pe", lambda e, k=k: e.matmul(ps.t[:csz, :512], lhsT=uT.t[:, k, 3 + c0:3 + c0 + csz],
                                                  rhs=wv[:, k, 0:512], start=(k == 0), stop=(k == 7)),
                     reads=[wb, uT.b], writes=[ps.b])
            P.op("act", lambda e: e.activation(out=zs_tm.t[:csz, ci, :], in_=ps.t[:csz, :512], func=AF.Silu),
                 reads=[ps.b], writes=[zs_tm.b])

        def dd(ci, c0, csz):
            ps = cx.psum()
            for k in range(8):
                P.op("pe", lambda e, k=k: e.matmul(ps.t[:csz, :8], lhsT=uT.t[:, k, 3 + c0:3 + c0 + csz],
                                                  rhs=wv[:, k, 1280:1288], start=(k == 0), stop=(k == 7)),
                     reads=[wb, uT.b], writes=[ps.b])
            P.op("dve", lambda e: e.tensor_tensor(out=dt.t[:csz, ci, :], in0=ps.t[:csz, :8], in1=dtb.t[:csz, :],
                                                  op=ALU.add), reads=[ps.b, dtb.b], writes=[dt.b])

        def da(ci, c0, csz):
            P.op("dve", lambda e: e.tensor_tensor(out=dta.t[:csz, ci, :], in0=dt.t[:csz, ci, :], in1=aneg.t[:csz, :],
                                                  op=ALU.mult), reads=[dt.b, aneg.b], writes=[dta.b])

        def cc(ci, c0, csz):
            ps = cx.psum()
            P.op("pe", lambda e: e.matmul(ps.t[:csz, 0:8], lhsT=triI.t[:csz, :csz], rhs=dta.t[:csz, ci, :],
                                          start=True, stop=True), reads=[triI.b, dta.b], writes=[ps.b])
            P.op("pe", lambda e: e.matmul(ps.t[:, 8:16], lhsT=cx.ones.t[:csz, :], rhs=dta.t[:csz, ci, :],
                                          start=True, stop=True), reads=[cx.ones.b, dta.b], writes=[ps.b])
            P.op("dve", lambda e: e.tensor_copy(out=cs.t[:csz, ci, :], in_=ps.t[:csz, 0:8]),
                 reads=[ps.b], writes=[cs.b])
            P.op("dve", lambda e: e.tensor_tensor(out=wst.t[:csz, ci, :], in0=ps.t[:csz, 8:16],
                                                  in1=cs.t[:csz, ci, :], op=ALU.subtract),
                 reads=[ps.b, cs.b], writes=[wst.b])
            P.op("act", lambda e: e.activation(out=cdec.t[:, ci, :], in_=ps.t[:, 8:16], func=AF.Exp),
                 reads=[ps.b], writes=[cdec.b])
        for ci, (c0, csz) in enumerate(CH):
            tr(ci, c0, csz)
        for ci, (c0, csz) in enumerate(CH):
            zz(ci, c0, csz)
        for ci, (c0, csz) in enumerate(CH):
            dd(ci, c0, csz)
        P.op("act", lambda e: e.activation(out=dt.t[:], in_=dt.t[:], func=AF.Exp), reads=[dt.b], writes=[dt.b])
        P.op("act", lambda e: e.activation(out=dt.t[:], in_=dt.t[:], func=AF.Ln, bias=onef1.t[:, :]),
             reads=[dt.b, onef1.b], writes=[dt.b])
        for ci, (c0, csz) in enumerate(CH):
            da(ci, c0, csz)
        for ci, (c0, csz) in enumerate(CH):
            cc(ci, c0, csz)
        P.op("act", lambda e: e.activation(out=ecs.t[:], in_=cs.t[:], func=AF.Exp), reads=[cs.b], writes=[ecs.b])
        P.op("act", lambda e: e.activation(out=wst.t[:], in_=wst.t[:], func=AF.Exp), reads=[wst.b], writes=[wst.b])
        P.op("dve", lambda e: e.tensor_tensor(out=wst.t[:], in0=wst.t[:], in1=dt.t[:], op=ALU.mult),
             reads=[wst.b, dt.b], writes=[wst.b])
        for ci, (c0, csz) in enumerate(CH):
            do_chunk(s0, ci, c0, csz)

    def do_chunk(s0, ci, c0, csz):
        P.op("dve", lambda e: e.tensor_tensor(out=v3(xdt.t[:csz, :]), in0=v3(xs_tm.t[:csz, ci, :]),
                                              in1=bc(dt.t[:csz, ci, :]), op=ALU.mult),
             reads=[xs_tm.b, dt.b], writes=[xdt.b])
        P.op("pool", lambda e: e.tensor_tensor(out=v3(xdte.t[:csz, :]), in0=v3(xs_tm.t[:csz, ci, :]),
                                               in1=bc(wst.t[:csz, ci, :]), op=ALU.mult),
             reads=[xs_tm.b, wst.b], writes=[xdte.b])
        ps = cx.psum()
        P.op("pe", lambda e: e.matmul(ps.t[:csz, :csz], lhsT=xsT.t[:, 4, c0:c0 + csz], rhs=xsT.t[:, 5, c0:c0 + csz],
                                      start=True, stop=True), reads=[xsT.b], writes=[ps.b])
        P.op("dve", lambda e: e.tensor_tensor(out=cbT.t[:csz, :csz], in0=ps.t[:csz, :csz], in1=triI.t[:csz, :csz],
                                              op=ALU.mult), reads=[ps.b, triI.b], writes=[cbT.b])
        yp = pacc[ci % 2]
        l8, d8, m8 = lh8[ci % 2], dec8[ci % 2], MT8[ci % 2]
        P.op("dve", lambda e: e.tensor_tensor(
            out=l8.t[:csz, :, :csz], in0=mstr.t[:csz, :csz].unsqueeze(1).to_broadcast([csz, 8, csz]),
            in1=dta.t[:csz, ci, :].unsqueeze(2).to_broadcast([csz, 8, csz]), op=ALU.mult),
            reads=[mstr.b, dta.b], writes=[l8.b])
        pgs = [cx.psum(), cx.psum()]
        for hh in range(8):
            pg = pgs[hh // 4]
            P.op("pe", lambda e, pg=pg, hh=hh: e.matmul(pg.t[:csz, (hh % 4) * 128:(hh % 4) * 128 + csz],
                                                       lhsT=l8.t[:csz, hh, :csz], rhs=triI.t[:csz, :csz],
                                                       start=True, stop=True), reads=[l8.b, triI.b], writes=[pg.b])
        for g4 in range(2):
            pg = pgs[g4]
            P.op("act", lambda e, pg=pg, g4=g4: e.activation(
                out=d8.t[:csz, 4 * g4:4 * g4 + 4, :csz],
                in_=pg.t[:csz, :].rearrange("p (h s) -> p h s", h=4)[:, :, :csz], func=AF.Exp),
                reads=[pg.b], writes=[d8.b])
        P.op("dve", lambda e: e.tensor_tensor(
            out=m8.t[:csz, :, :csz], in0=d8.t[:csz, :, :csz],
            in1=cbT.t[:csz, :csz].unsqueeze(1).to_broadcast([csz, 8, csz]), op=ALU.mult),
            reads=[d8.b, cbT.b], writes=[m8.b])
        for hh in range(8):
            P.op("pe", lambda e, hh=hh: e.matmul(yp.t[:csz, hh * 64:(hh + 1) * 64], lhsT=m8.t[:csz, hh, :csz],
                                                rhs=xdt.t[:csz, hh * 64:(hh + 1) * 64], start=True, stop=True),
                 reads=[m8.b, xdt.b], writes=[yp.b])
        po_ = cx.psum()
        P.op("pe", lambda e: e.matmul(po_.t[:csz, :512], lhsT=xsT.t[:, 5, c0:c0 + csz], rhs=Sb.t[:, :],
                                      start=True, stop=True), reads=[xsT.b, Sb.b], writes=[po_.b])
        P.op("dve", lambda e: e.tensor_tensor(out=v3(y1.t[:csz, :]), in0=v3(po_.t[:csz, :512]),
                                              in1=bc(ecs.t[:csz, ci, :]), op=ALU.mult),
             reads=[po_.b, ecs.b], writes=[y1.b])
        P.op("dve", lambda e: e.tensor_tensor(out=y1.t[:csz, :], in0=yp.t[:csz, :512], in1=y1.t[:csz, :], op=ALU.add),
             reads=[yp.b, y1.b], writes=[y1.b])
        P.op("pool", lambda e: e.tensor_tensor(out=v3(y2.t[:csz, :]), in0=v3(xs_tm.t[:csz, ci, :]),
                                               in1=bc(dsk.t[:csz, :]), op=ALU.mult),
             reads=[xs_tm.b, dsk.b], writes=[y2.b])
        P.op("dve", lambda e: e.tensor_tensor(out=y1.t[:csz, :], in0=y1.t[:csz, :], in1=y2.t[:csz, :], op=ALU.add),
             reads=[y1.b, y2.b], writes=[y1.b])
        P.op("dve", lambda e: e.tensor_tensor(out=y1.t[:csz, :], in0=y1.t[:csz, :], in1=zs_tm.t[:csz, ci, :],
                                              op=ALU.mult), reads=[y1.b, zs_tm.b], writes=[y1.b])
        P.op("act", lambda e: e.activation(out=y2.t[:csz, :], in_=y1.t[:csz, :], func=AF.Square,
                                           accum_out=ss.t[:csz, 0:1]), reads=[y1.b], writes=[y2.b, ss.b])
        P.op("act", lambda e: e.activation(out=ss.t[:csz, 1:2], in_=ss.t[:csz, 0:1], func=AF.Ln,
                                           bias=cx.epst.t[:csz, :], scale=1.0 / 512), reads=[ss.b, cx.epst.b],
             writes=[ss.b])
        P.op("act", lambda e: e.activation(out=ss.t[:csz, 1:2], in_=ss.t[:csz, 1:2], func=AF.Exp, scale=-0.5),
             reads=[ss.b], writes=[ss.b])
        P.op("dve", lambda e: e.scalar_tensor_tensor(out=hgn.t[:csz, :], in0=y1.t[:csz, :], scalar=ss.t[:csz, 1:2],
                                                     in1=gg.t[:csz, :], op0=ALU.mult, op1=ALU.mult),
             reads=[y1.b, ss.b, gg.b], writes=[hgn.b])
        pt = ptr[ci % 2]
        hs = hst[cnt["h"] % 2]
        cnt["h"] += 1
        for jc in range(4):
            P.op("pe", lambda e, jc=jc: e.transpose(pt.t[:, jc * 128:jc * 128 + csz], hgn.t[:csz, jc * 128:(jc + 1) * 128],
                                                    ident.t[:csz, :csz]), reads=[hgn.b, ident.b], writes=[pt.b])
        P.op("act", lambda e: e.activation(out=hs.t[:, :, :csz],
                                           in_=pt.t[:, 0:512].rearrange("p (j c) -> p j c", j=4)[:, :, :csz],
                                           func=AF.Identity), reads=[pt.b], writes=[hs.b])
        si = s0 // W
        P.dma("sp", lambda e: e.dma_start(
            out=hgo[si][:, :, 4 + c0:4 + c0 + csz].rearrange("j p c -> p j c"), in_=hs.t[:, :, :csz]),
            reads=[hs.b], sembuf=hs.b)
        if c0 + csz == W and si < 3:
            P.dma("sp", lambda e: e.dma_start(
                out=hgo[si + 1][:, :, 0:4].rearrange("j p c -> p j c"), in_=hs.t[:, :, csz - 4:csz]),
                reads=[hs.b], sembuf=hs.b)
        pn = cx.psum()
        P.op("pe", lambda e: e.matmul(pn.t[:, :512], lhsT=B_tm.t[:csz, ci, :], rhs=xdte.t[:csz, :], start=True,
                                      stop=True), reads=[B_tm.b, xdte.b], writes=[pn.b])
        P.op("dve", lambda e: e.tensor_tensor(out=v3(S.t[:, :]), in0=v3(S.t[:, :]), in1=bc(cdec.t[:, ci, :]),
                                              op=ALU.mult), reads=[S.b, cdec.b], writes=[S.b])
        P.op("dve", lambda e: e.tensor_tensor(out=S.t[:, :], in0=pn.t[:, :512], in1=S.t[:, :], op=ALU.add),
             reads=[pn.b, S.b], writes=[S.b])
        P.op("pool", lambda e: e.tensor_copy(out=Sb.t[:, :], in_=S.t[:, :]), reads=[S.b], writes=[Sb.b])

    zt = P.sb("zt", [128, 4, 4], BF16)
    P.op("pool", lambda e: e.memset(zt.t[:], 0.0), writes=[zt.b])
    P.dma("sp", lambda e: e.dma_start(out=hgo[0][:, :, 0:4].rearrange("j p c -> p j c"), in_=zt.t[:]),
          reads=[zt.b], sembuf=zt.b)
    for si in range(4):
        do_seg(si)
        io["after_seg"](si, [h.b for h in hst] + [zt.b])
    return [h.b for h in hst] + [zt.b]


GROUPS = [[0, 1, 2, 3], [4, 5, 6, 7]]


def build_fused():
    nc = bass.Bass("TRN2", target_bir_lowering=False)

    def dr(n, s, dt=F32, k="ExternalInput"):
        return nc.dram_tensor(n, list(s), dt, kind=k)
    ioA = {"xT": dr("A_xT", [4, 5, 8, 128, 411]).ap(), "wsel": dr("A_wsel", [D, 1288]).ap(), "g_in": dr("A_g_in", [128, 8]).ap(),
           "cw": dr("A_cw", [128, 24]).ap(), "cb": dr("A_cb", [128, 6]).ap(), "dtb": dr("A_dtb", [128, 8]).ap(),
           "alog": dr("A_alog", [128, 8]).ap(), "dsk": dr("A_dsk", [128, 8]).ap(), "gg": dr("A_gg", [128, 512]).ap()}

    def tok_io(pfx, kcm):
        return {"w_mix": dr(pfx + "w_mix", [kcm * 128, D]).ap(), "w_up": dr(pfx + "w_up", [D, 2 * DFF]).ap(),
                "w_down": dr(pfx + "w_down", [DFF, D]).ap(), "g_ffn": dr(pfx + "g_ffn", [128, 8]).ap(),
                "cw": dr(pfx + "cw", [128, 132]).ap(), "cb": dr(pfx + "cb", [128, 44]).ap()}
    ioB = tok_io("B_", 16)
    ioB.update({"resid": dr("B_resid", [D, TQ + 4]).ap(), "w_kv": dr("B_w_kv", [D, 2 * D]).ap(),
                "w_q": dr("B_w_q", [D, D]).ap(), "g_kv": dr("B_g_kv", [128, 8]).ap(), "g_q": dr("B_g_q", [128, 8]).ap(),
                "hmask": dr("B_hmask", [128, 1]).ap()})
    ioD = tok_io("D_", 8)
    ioD.update({"g_fin": dr("D_g_fin", [128, 8]).ap(), "outo": dr("out", [D, TQ], F32, "ExternalOutput").ap()})
    idxd = dr("idx", [1, 1], I32).ap()

    C1, C3, CH2 = TQ + 4, TQ + 2, 128 * TQ
    b1 = nc.dram_tensor("b1", [4, 4, 128, C1], BF16)
    g1 = nc.dram_tensor("g1", [4, 4, 4, 128, C1], BF16)
    b2 = nc.dram_tensor("b2", [4, 6, 128, TQ], BF16)
    g2 = nc.dram_tensor("g2", [4, 6, 4, 128, TQ], BF16)
    b3 = nc.dram_tensor("b3", [4, 2, 128, C3], BF16)
    g3 = nc.dram_tensor("g3", [4, 2, 4, 128, C3], BF16)
    h1scr = nc.dram_tensor("h1scr", [D, TQ + 2], F32)
    dtap = nc.dram_tensor("dtap", [D, 1028], F32)
    sc1 = nc.dram_tensor("sc1", [4, 4, 128, C1], BF16)
    sc2 = nc.dram_tensor("sc2", [6, 4, 128, TQ], BF16)
    sc3 = nc.dram_tensor("sc3", [2, 4, 128, C3], BF16)

    P = Prog(nc)
    regs = {n: P.stack.enter_context(nc.gpsimd.register(n)) for n in ("ridx", "r1", "r2q", "r3", "rtmp")}
    it = P.sb("idxt", [1, 2], I32)
    scr = P.sb("scr", [1, 16], BF16)
    P.persist = P.off
    P.dma("pool", lambda e: e.dma_start(out=it.t[0:1, 0:1], in_=idxd), writes=[it.b])

    def setup(e):
        e.reg_load(regs["ridx"], it.t[0:1, 0:1])
        e.reg_mul(regs["r1"], regs["ridx"], 16 * 128 * C1)
        e.reg_mul(regs["r2q"], regs["ridx"], 24 * CH2)
        e.reg_mul(regs["r3"], regs["ridx"], 8 * 128 * C3)
        return e.memset(scr.t[:], 0.0)
    P.op("pool", setup, reads=[it.b], writes=[scr.b])

    def pull(sct, gt, reg, nrows, ncols):
        b = P.buf("sc")
        P.dma("pool", lambda e: e.dma_start(out=sct.ap().rearrange("a b c d -> (a b c) d"),
                                            in_=bass.AP(gt, reg, [[ncols, nrows], [1, ncols]])), writes=[b])
        return b

    def gather_dest(bt, gt, d, n1, ob):
        for c in range(n1):
            P.collective("AllGather", bt.ap()[d, c].opt(), gt.ap()[d, c].opt(), GROUPS, ob if c == 0 else [])

    def gather_all(bt, gt, n0, n1, ob):
        for d in range(n0):
            gather_dest(bt, gt, d, n1, ob if d == 0 else [])
        P.collective_wait()

    import os
    upto = int(os.environ.get("FUSE_UPTO", "4"))
    nocc = os.environ.get("FUSE_NOCC", "0") == "1"
    if nocc:
        P.collective = lambda *a, **k: None

    def finish():
        dbg = os.environ.get("FUSE_DEBUG", "")
        if dbg:
            src = {"h1scr": h1scr, "sc1": sc1, "sc2": sc2, "sc3": sc3, "b1": b1, "b2": b2, "b3": b3, "dtap": dtap}[dbg]
            shp = list(src.ap().shape)
            n = 1
            for d_ in shp[:-1]:
                n *= d_
            dt_ = F32 if dbg in ("h1scr", "dtap") else BF16
            dbo = nc.dram_tensor("dbg", [n, shp[-1]], dt_, kind="ExternalOutput")
            P.barrier()
            bb = P.buf("dbg")
            names = "abcdefg"[:len(shp) - 1]
            view = src.ap() if len(shp) == 2 else src.ap().rearrange(" ".join(names) + " z -> (" + " ".join(names) + ") z")
            P.dma("sp", lambda e: e.dma_start(out=dbo.ap(), in_=view), writes=[bb])
            P.wait_all("sp", [bb])
        P.barrier()
        print("sems", len(P.sems), "instr", {e: len(q) for e, q in P.q.items()})
        P.emit()
        P.close()
        return nc
    P.phase_start()
    ioA["b1"] = b1.ap()
    ioA["after_seg"] = lambda si, ob: gather_dest(b1, g1, si, 4, ob)
    phase_ssd(P, ioA)
    P.collective_wait()
    if upto == 1:
        return finish()
    P.phase_start()
    ioB.update({"sc": sc1.ap(), "scb": pull(sc1, g1, regs["r1"], 16 * 128, C1), "h1scr": h1scr.ap(),
                "b2": b2.ap(), "b2h": b2})
    ob = phase_token(P, "B", ioB)
    gather_all(b2, g2, 4, 6, ob)
    if upto == 2:
        return finish()
    P.phase_start()
    ob = phase_attn(P, {"b3": b3.ap(), "sc": sc2.ap(), "sch": sc2, "scb": pull(sc2, g2, regs["r2q"], 24 * 128, TQ)})
    gather_all(b3, g3, 4, 2, ob)
    if upto == 3:
        return finish()
    P.phase_start()
    ioD.update({"dtap": dtap.ap(), "sc": sc3.ap(), "scb": pull(sc3, g3, regs["r3"], 8 * 128, C3), "resid": h1scr.ap()})
    ob = phase_token(P, "D", ioD)
    P.wait_all("sp", ob)
    return finish()


_NC_CACHE = {}


def get_nc(key, fn, *a):
    if key not in _NC_CACHE:
        _NC_CACHE[key] = fn(*a)
    return _NC_CACHE[key]


def fm(v, n):
    return np.ascontiguousarray(np.asarray(v, np.float32).reshape(n, 128).T)


def _halo_cols(full, s, halo, W):
    out = np.zeros((full.shape[0], halo + W), full.dtype)
    lo = max(0, s - halo)
    out[:, lo - (s - halo):] = full[:, lo:s + W]
    return out


def _ffn_params(inp, layer, pfx):
    cwT = np.ascontiguousarray(np.asarray(inp["ffn_conv_w"][layer], np.float32).T.reshape(44, 128, 3)
                               .transpose(1, 0, 2).reshape(128, 132))
    return {pfx + "w_up": np.asarray(inp["ffn_w_up"][layer], np.float32),
            pfx + "w_down": np.asarray(inp["ffn_w_down"][layer], np.float32),
            pfx + "g_ffn": fm(inp["ffn_norm"][layer], 8), pfx + "cw": cwT, pfx + "cb": fm(inp["ffn_conv_b"][layer], 44)}


def kernel(**inp):
    inp = {k: np.asarray(v) for k, v in inp.items()}
    x = inp["x"].astype(np.float32)
    nb = x.shape[0]
    h0 = np.concatenate([np.broadcast_to(inp["meta_tokens"][None].astype(np.float32), (nb, 16, D)), x], axis=1)
    h0T = [np.ascontiguousarray(h0[b].T) for b in range(nb)]
    cores = list(range(8))
    nc = get_nc("F", build_fused)
    mA = ssd_maps(inp, h0)
    fB = _ffn_params(inp, 0, "B_")
    fD = _ffn_params(inp, 1, "D_")
    maps = []
    for c in cores:
        b, i = divmod(c, 4)
        m = {"A_" + k: v for k, v in mA[c].items()}
        m.update(fB)
        m.update(fD)
        m.update({"B_resid": _halo_cols(h0T[b], i * TQ, 4, TQ), "B_w_mix": np.ascontiguousarray(np.asarray(inp["ssd_w_out"][0], np.float32)
                                                   .reshape(4, 4, 128, D).transpose(1, 0, 2, 3).reshape(DI, D)),
                  "B_w_kv": np.asarray(inp["w_kv"], np.float32), "B_w_q": np.asarray(inp["sb_w_q"][0], np.float32),
                  "B_g_kv": fm(inp["kv_norm"], 8), "B_g_q": fm(inp["sb_norm"][0], 8),
                  "B_hmask": np.full((128, 1), 0.0 if i == 0 else 1.0, np.float32),
                  "D_w_mix": np.ascontiguousarray(np.asarray(inp["sb_w_o"][0], np.float32)
                                                   .reshape(4, 2, 128, D).transpose(1, 0, 2, 3).reshape(D, D)), "D_g_fin": fm(inp["final_norm"], 8),
                  "idx": np.array([[i]], np.int32)})
        maps.append(m)
    res = run_bass_kernel_spmd(nc, maps, core_ids=cores).results
    out = np.empty((nb, LB - 16, D), np.float32)
    for b in range(nb):
        full = np.concatenate([res[b * 4 + t]["out"] for t in range(4)], axis=1)
        out[b] = full[:, 16:].T
    return out


def ssd_maps(inp, h0):
    w_in = inp["ssd_w_in"][0]
    cwf = inp["ssd_conv_w"][0]
    cbf = inp["ssd_conv_b"][0]
    maps = []
    for c in range(8):
        b, g = divmod(c, 4)
        cols = np.concatenate([np.arange(512 * g, 512 * g + 512), 2048 + np.arange(512 * g, 512 * g + 512),
                               4096 + np.arange(128 * g, 128 * g + 128), 4608 + np.arange(128 * g, 128 * g + 128),
                               5120 + np.arange(8 * g, 8 * g + 8)])
        cch = np.concatenate([np.arange(512 * g, 512 * g + 512), 2048 + np.arange(128 * g, 128 * g + 128),
                              2560 + np.arange(128 * g, 128 * g + 128)])
        xpad = np.zeros((D, 3 + LB), np.float32)
        xpad[:, 3:] = h0[b].T
        xT = np.empty((4, 5, 8, 128, 411), np.float32)
        for si_ in range(4):
            for ti_ in range(5):
                c0_ = si_ * TQ + ti_ * 411
                xT[si_, ti_] = xpad[:, c0_:c0_ + 411].reshape(8, 128, 411)
        rep = lambda v: np.ascontiguousarray(np.broadcast_to(np.asarray(v, np.float32)[None, :], (128, len(v))))
        maps.append({
            "xT": xT, "wsel": np.ascontiguousarray(w_in[:, cols]), "g_in": fm(inp["ssd_norm"][0], 8),
            "cw": np.ascontiguousarray(cwf[:, cch].T.reshape(6, 128, 4).transpose(1, 0, 2).reshape(128, 24)),
            "cb": fm(cbf[cch], 6), "dtb": rep(inp["ssd_dt_bias"][0][8 * g:8 * g + 8]),
            "alog": rep(inp["ssd_a_log"][0][8 * g:8 * g + 8]), "dsk": rep(inp["ssd_d_skip"][0][8 * g:8 * g + 8]),
            "gg": rep(inp["ssd_gate_norm"][0][512 * g:512 * g + 512]),
        })
    return maps
```

```python
import numpy as np
import ml_dtypes
from contextlib import ExitStack
import concourse.bass as bass
import concourse.mybir as mybir
from concourse.bass_utils import run_bass_kernel_spmd

F32 = mybir.dt.float32
BF16 = mybir.dt.bfloat16
AF = mybir.ActivationFunctionType
ALU = mybir.AluOpType
AX = mybir.AxisListType
NPBF = ml_dtypes.bfloat16

D = 1024
LB = 8208
TQ = 2052
DI = 2048
DFF = 2816
EPS = 1e-6
EPOCH = 30000


I32 = mybir.dt.int32
ARENA = 106400
ISZ = {F32: 4, BF16: 2, I32: 4}


class Buf:
    __slots__ = ("name", "w", "r", "dsem", "dcnt")

    def __init__(self, name):
        self.name = name
        self.w = None
        self.r = {}
        self.dsem = None
        self.dcnt = 0


class T:
    __slots__ = ("t", "b")

    def __init__(self, t, b):
        self.t = t
        self.b = b


class Prog:
    ENGS = ("pe", "act", "dve", "pool", "sp")
    EMAP = {"pe": "tensor", "act": "scalar", "dve": "vector", "pool": "gpsimd", "sp": "sync"}

    def __init__(self, nc):
        self.nc = nc
        self.q = {e: [] for e in self.ENGS}
        self.cnt = {e: 0 for e in self.ENGS}
        self.seen = {e: {} for e in self.ENGS}
        self.sems = {}
        self.latest = {}
        self.stack = ExitStack()
        self.nbuf = 0
        self.arena = self.stack.enter_context(nc.sbuf_tensor("arena", [128, ARENA], BF16))
        self.banks = [self.stack.enter_context(nc.psum_tensor(f"bank{i}", [128, 512], F32)) for i in range(8)]
        self.off = 0
        self.persist = 0
        self.nbank = 0
        self.ncc = 0
        self.ccpend = []
        self.dfree = {"sw": [], "hw": []}
        self.dlive = []

    def _sem(self, key):
        if key not in self.sems:
            self.sems[key] = self.stack.enter_context(self.nc.semaphore("s_" + key.replace("#", "_")))
        return self.sems[key]

    def buf(self, name=None):
        self.nbuf += 1
        return Buf(f"{name or 'b'}{self.nbuf}")

    def sb(self, name, shape, dtype, stack=None):
        shape = list(shape)
        n = 1
        for d in shape[1:]:
            n *= d
        nel = (n * ISZ[dtype] + 1) // 2
        nel = (nel + 15) // 16 * 16
        assert self.off + nel <= ARENA, f"SBUF arena overflow at {name}: {self.off}+{nel}"
        v = self.arena[0:shape[0], self.off:self.off + n * ISZ[dtype] // 2]
        self.off += nel
        if dtype != BF16:
            v = v.bitcast(dtype)
        if len(shape) == 3:
            v = v.rearrange("p (a b) -> p a b", a=shape[1])
        return T(v, self.buf(name))

    def psum(self, name, shape, dtype=F32, stack=None):
        assert self.nbank < 8, "out of PSUM banks"
        bk = self.banks[self.nbank]
        self.nbank += 1
        v = bk[:, :]
        if dtype == BF16:
            v = v.bitcast(BF16)
        return T(v, self.buf(name))

    def _waits(self, eng, reads, writes):
        need = {}

        def add(k, v):
            if need.get(k, 0) < v:
                need[k] = v
        for b in reads:
            if b.w:
                add(*b.w)
        for b in writes:
            if b.w:
                add(*b.w)
            for k, v in b.r.items():
                add(k, v)
        if eng == "pe":
            for k in [k for k in need if k.startswith("pe#")]:
                del need[k]
        out = []
        seen = self.seen[eng]
        for k, v in need.items():
            if seen.get(k, 0) < v:
                seen[k] = v
                out.append((k, v))
        return out

    def _mark(self, ev, reads, writes):
        k, v = ev
        if self.latest.get(k, 0) < v:
            self.latest[k] = v
        for b in reads:
            if b.r.get(k, 0) < v:
                b.r[k] = v
        for b in writes:
            b.w = ev
            b.r = {}

    def op(self, eng, fn, reads=(), writes=(), sig=True):
        waits = self._waits(eng, reads, writes)
        c = self.cnt[eng]
        key = f"{eng}#{c // EPOCH}"
        self._sem(key)
        ev = (key, c % EPOCH + 1)
        if sig:
            self.cnt[eng] = c + 1
            self.q[eng].append((waits, fn, (key, 1)))
        else:
            assert eng == "pe"
            self.q[eng].append((waits, fn, None))
        self._mark(ev, reads, writes)
        return ev

    def dma(self, eng, fn, reads=(), writes=(), sembuf=None):
        waits = self._waits(eng, reads, writes)
        sb = sembuf or (writes[0] if writes else reads[0])
        if sb.dsem is None or sb.dcnt >= EPOCH:
            cls = "sw" if eng == "pool" else "hw"
            fl = self.dfree[cls]
            while fl and self.latest.get(fl[-1], 0) >= EPOCH - 4096:
                fl.pop()
            if fl:
                sb.dsem = fl.pop()
                sb.dcnt = self.latest.get(sb.dsem, 0)
            else:
                sb.dsem = f"d{cls}{len(self.sems)}"
                sb.dcnt = 0
                self._sem(sb.dsem)
            self.dlive.append((cls, sb.dsem))
        sb.dcnt += 16
        ev = (sb.dsem, sb.dcnt)
        self.q[eng].append((waits, fn, (sb.dsem, 16)))
        self._mark(ev, reads, writes)
        return ev

    def wait_all(self, eng, bufs):
        waits = self._waits(eng, bufs, bufs)
        self.q[eng].append((waits, None, None))

    def barrier(self):
        for e in self.ENGS:
            seen = self.seen[e]
            waits = []
            for k, v in self.latest.items():
                if seen.get(k, 0) < v:
                    seen[k] = v
                    waits.append((k, v))
            self.q[e].append((waits, None, None))

    def phase_start(self):
        self.barrier()
        self.off = self.persist
        self.nbank = 0
        for cls, k in self.dlive:
            self.dfree[cls].append(k)
        self.dlive = []

    def collective(self, kind, in_ap, out_ap, groups, wait_bufs):
        waits = self._waits("pool", wait_bufs, wait_bufs)
        key = f"cc{self.ncc}"
        self.ncc += 1
        self._sem(key)
        self.q["pool"].append((waits, lambda e: e.collective_compute(kind, ALU.bypass, replica_groups=groups,
                                                                    ins=[in_ap], outs=[out_ap]), (key, 1)))
        self.latest[key] = 1
        self.ccpend.append(key)

    def collective_wait(self):
        waits = [(k, 1) for k in self.ccpend if self.seen["pool"].get(k, 0) < 1]
        for k, _ in waits:
            self.seen["pool"][k] = 1
        self.ccpend = []
        self.q["pool"].append((waits, None, None))

    def emit(self):
        nc = self.nc
        sems = self.sems
        with nc.Block() as block:
            for e in self.ENGS:
                items = self.q[e]

                def body(engine, items=items):
                    for waits, fn, inc in items:
                        for k, v in waits:
                            engine.wait_ge(sems[k], v)
                        if fn is not None:
                            ins = fn(engine)
                            if inc is not None:
                                ins.then_inc(sems[inc[0]], inc[1])
                getattr(block, self.EMAP[e])(body)

    def close(self):
        self.stack.close()


def tiles(width, maxw=512):
    n = -(-width // maxw)
    base, rem = divmod(width, n)
    out, o = [], 0
    for i in range(n):
        w = base + (1 if i < rem else 0)
        out.append((o, w))
        o += w
    return out


def chunks128(width):
    out, o = [], 0
    while o < width:
        w = min(128, width - o)
        out.append((o, w))
        o += w
    return out


class Ctx:
    def __init__(self, P, wslot_elems=4096, nwslots=3, nps=8):
        self.nc = P.nc
        self.P = P
        self.ps = [P.psum(f"ps{i}", [128, 512]) for i in range(nps)]
        self.psi = 0
        self.wslots = [P.sb(f"wsl{i}", [128, wslot_elems], BF16) for i in range(nwslots)]
        self.wsi = 0
        self.wslot_elems = wslot_elems
        self.ones = P.sb("ones_f", [128, 128], F32)
        P.op("pool", lambda e: e.memset(self.ones.t[:], 1.0), writes=[self.ones.b])
        self.epst = P.sb("epst", [128, 1], F32)
        P.op("pool", lambda e: e.memset(self.epst.t[:], EPS), writes=[self.epst.b])
        self.onesb = P.sb("ones_b", [128, 128], BF16)
        P.op("pool", lambda e: e.memset(self.onesb.t[:], 1.0), writes=[self.onesb.b])
        self.sq = [P.sb(f"sq{i}", [128, 512], BF16) for i in range(4)]
        self.rs = [P.sb(f"rs{i}", [128, 512], F32) for i in range(2)]
        self.sqi = 0
        self.rsi = 0

    def psum(self):
        p = self.ps[self.psi % len(self.ps)]
        self.psi += 1
        return p

    def wslot(self):
        w = self.wslots[self.wsi % len(self.wslots)]
        self.wsi += 1
        return w

    def load_w(self, w_ap, k0, kc, n0, ncols):
        assert kc * ncols <= self.wslot_elems, (kc, ncols)
        sl = self.wslot()
        view = sl.t[:, 0:kc * ncols].rearrange("p (k n) -> p k n", k=kc)
        src = w_ap[k0:k0 + kc * 128, n0:n0 + ncols].rearrange("(k p) n -> p k n", p=128)
        self.P.dma("pool", lambda e: e.dma_start(out=view, in_=src), writes=[sl.b])
        return view, sl.b


def load_small(cx, name, dram_ap, shape, dtype=F32):
    t = cx.P.sb(name, shape, dtype)
    cx.P.dma("sp", lambda e: e.dma_start(out=t.t[:], in_=dram_ap), writes=[t.b])
    return t


def rmsnorm_fm(cx, src, g, dst, c0, width, dst_c0=0, kc=8):
    P = cx.P
    for (t0, tw) in tiles(width):
        ps = cx.psum()
        for k in range(kc):
            sq = cx.sq[cx.sqi % 4]
            cx.sqi += 1
            sl = src.t[:, k, c0 + t0:c0 + t0 + tw]
            P.op("pool", lambda e, sq=sq, sl=sl, tw=tw: e.tensor_tensor(out=sq.t[:, :tw], in0=sl, in1=sl, op=ALU.mult),
                 reads=[src.b], writes=[sq.b])
            P.op("pe", lambda e, ps=ps, sq=sq, tw=tw, k=k: e.matmul(ps.t[:, :tw], lhsT=cx.onesb.t[:], rhs=sq.t[:, :tw],
                                                              start=(k == 0), stop=(k == kc - 1)),
                 reads=[cx.onesb.b, sq.b], writes=[ps.b])
        rs = cx.rs[cx.rsi % 2]
        cx.rsi += 1
        P.op("act", lambda e, rs=rs, ps=ps, tw=tw: e.activation(out=rs.t[:, :tw], in_=ps.t[:, :tw], func=AF.Ln,
                                                             bias=cx.epst.t[:], scale=1.0 / (128 * kc)),
             reads=[ps.b, cx.epst.b], writes=[rs.b])
        P.op("act", lambda e, rs=rs, tw=tw: e.activation(out=rs.t[:, :tw], in_=rs.t[:, :tw], func=AF.Exp, scale=-0.5),
             reads=[rs.b], writes=[rs.b])
        for k in range(kc):
            P.op("dve", lambda e, k=k, rs=rs, t0=t0, tw=tw: e.scalar_tensor_tensor(
                out=dst.t[:, k, dst_c0 + t0:dst_c0 + t0 + tw], in0=src.t[:, k, c0 + t0:c0 + t0 + tw],
                scalar=g.t[:, k:k + 1], in1=rs.t[:, :tw], op0=ALU.mult, op1=ALU.mult),
                reads=[src.b, g.b, rs.b], writes=[dst.b])


def proj_fm(cx, w_ap, kc, n0, ncols, uT, c0, width, evac, ngroup=None):
    P = cx.P
    gcols = ngroup or max(128, (cx.wslot_elems // kc) // 128 * 128)
    gcols = min(gcols, 512)
    tl = tiles(width)
    for g0 in range(0, ncols, gcols):
        gc = min(gcols, ncols - g0)
        wv, wb = cx.load_w(w_ap, 0, kc, n0 + g0, gc)
        for jj, (j0, nsz) in enumerate(chunks128(gc)):
            j = (g0 + j0) // 128
            for (t0, tw) in tl:
                ps = cx.psum()
                for k in range(kc):
                    P.op("pe", lambda e, ps=ps, k=k, j0=j0, nsz=nsz, t0=t0, tw=tw, wv=wv: e.matmul(
                        ps.t[:nsz, :tw], lhsT=wv[:, k, j0:j0 + nsz], rhs=uT.t[:, k, c0 + t0:c0 + t0 + tw],
                        start=(k == 0), stop=(k == kc - 1)), reads=[wb, uT.b], writes=[ps.b], sig=(k == kc - 1))
                evac(ps, j, nsz, t0, tw)


def proj_tm(cx, w_ap, kc, n0, ncols, uT, c0, width, evac):
    P = cx.P
    gcols = min(512, max(1, (cx.wslot_elems // kc)))
    ch = chunks128(width)
    for g0 in range(0, ncols, gcols):
        gc = min(gcols, ncols - g0)
        wv, wb = cx.load_w(w_ap, 0, kc, n0 + g0, gc)
        for ci, (t0, csz) in enumerate(ch):
            ps = cx.psum()
            for k in range(kc):
                P.op("pe", lambda e, ps=ps, k=k, t0=t0, csz=csz, gc=gc, wv=wv: e.matmul(
                    ps.t[:csz, :gc], lhsT=uT.t[:, k, c0 + t0:c0 + t0 + csz], rhs=wv[:, k, 0:gc],
                    start=(k == 0), stop=(k == kc - 1)), reads=[wb, uT.b], writes=[ps.b], sig=(k == kc - 1))
            evac(ps, ci, t0, csz, g0, gc)


def conv_fm(cx, pre, w_t, b_t, j, taps, wo, acc):
    P = cx.P
    kl = taps - 1
    P.op("dve", lambda e: e.tensor_scalar(out=acc.t[:, :wo], in0=pre.t[:, kl:kl + wo], scalar1=w_t.t[:, j, kl:kl + 1],
                                          scalar2=b_t.t[:, j:j + 1], op0=ALU.mult, op1=ALU.add),
         reads=[pre.b, w_t.b, b_t.b], writes=[acc.b])
    for k in range(taps - 1):
        P.op("dve", lambda e, k=k: e.scalar_tensor_tensor(out=acc.t[:, :wo], in0=pre.t[:, k:k + wo],
                                                        scalar=w_t.t[:, j, k:k + 1], in1=acc.t[:, :wo],
                                                        op0=ALU.mult, op1=ALU.add),
             reads=[pre.b, w_t.b, acc.b], writes=[acc.b])


def ffn_fm(cx, hm, uT, w_up, w_down, cw, cb, wh, halo, actT, pre, acc):
    P = cx.P
    for j in range(22):
        pg, pv = pre[(2 * j) % 4], pre[(2 * j + 1) % 4]
        ag, av = acc[(2 * j) % 4], acc[(2 * j + 1) % 4]

        def ev_g(ps, jj, nsz, t0, tw, pg=pg):
            P.op("act", lambda e: e.activation(out=pg.t[:, t0:t0 + tw], in_=ps.t[:, :tw], func=AF.Identity),
                 reads=[ps.b], writes=[pg.b])

        def ev_v(ps, jj, nsz, t0, tw, pv=pv):
            P.op("act", lambda e: e.activation(out=pv.t[:, t0:t0 + tw], in_=ps.t[:, :tw], func=AF.Identity),
                 reads=[ps.b], writes=[pv.b])
        proj_fm(cx, w_up, 8, j * 128, 128, uT, 0, wh + 2, ev_g)
        proj_fm(cx, w_up, 8, DFF + j * 128, 128, uT, 0, wh + 2, ev_v)
        conv_fm(cx, pg, cw, cb, j, 3, wh, ag)
        conv_fm(cx, pv, cw, cb, 22 + j, 3, wh, av)
        P.op("act", lambda e, ag=ag: e.activation(out=ag.t[:, :wh], in_=ag.t[:, :wh], func=AF.Silu),
             reads=[ag.b], writes=[ag.b])
        P.op("dve", lambda e, ag=ag, av=av, j=j: e.tensor_tensor(out=actT.t[:, j, :wh], in0=ag.t[:, :wh],
                                                                in1=av.t[:, :wh], op=ALU.mult),
             reads=[ag.b, av.b], writes=[actT.b])

    def ev_d(ps, j, nsz, t0, tw):
        sl = hm.t[:, j, halo + t0:halo + t0 + tw]
        P.op("dve", lambda e: e.tensor_tensor(out=sl, in0=ps.t[:, :tw], in1=sl, op=ALU.add),
             reads=[ps.b, hm.b], writes=[hm.b])
    proj_fm(cx, w_down, 22, 0, D, actT, 0, wh, ev_d, ngroup=128)


def phase_token(P, kind, io):
    kcm = 16 if kind == "B" else 8
    HIN = 4 if kind == "B" else 2
    WIN = TQ + HIN
    WOUT = WIN - 2
    HW = WOUT // 2
    HWH = HW + 2
    cx = Ctx(P)
    outbufs = []
    hm = P.sb("hm", [128, 8, HWH], F32)
    uT = P.sb("uT", [128, 8, HWH], BF16)
    arena = P.sb("tkar", [128, max(kcm * HWH, 22 * HW)], BF16)
    opT = T(arena.t[:, 0:kcm * HWH].rearrange("p (k w) -> p k w", k=kcm), arena.b)
    actT = T(arena.t[:, 0:22 * HW].rearrange("p (k w) -> p k w", k=22), arena.b)
    pre = [P.sb(f"pre{i}", [128, HWH], F32) for i in range(4)]
    acc = [P.sb(f"acc{i}", [128, HW], F32) for i in range(4)]
    gf = load_small(cx, "gf", io["g_ffn"], [128, 8])
    cw = P.sb("cw", [128, 44, 3], F32)
    P.dma("sp", lambda e: e.dma_start(out=cw.t[:].rearrange("p a b -> p (a b)"), in_=io["cw"]), writes=[cw.b])
    cb = load_small(cx, "cb", io["cb"], [128, 44])
    stg_i = [0]
    resid, w_mix, w_up, w_down = io["resid"], io["w_mix"], io["w_up"], io["w_down"]
    sc, scb = io["sc"], io["scb"]
    if kind == "B":
        gkv = load_small(cx, "gkv", io["g_kv"], [128, 8])
        gq = load_small(cx, "gq", io["g_q"], [128, 8])
        hmask = load_small(cx, "hmask", io["hmask"], [128, 1])
        stg = [P.sb(f"stg{i}", [128, 512], BF16) for i in range(4)]
        outbufs += [s.b for s in stg] + [hm.b]
        w_kv, w_q, h1scr, b2, b2h = io["w_kv"], io["w_q"], io["h1scr"], io["b2"], io["b2h"]
    else:
        gfin = load_small(cx, "gfin", io["g_fin"], [128, 8])
        fo = P.sb("fo", [128, 8, 512], F32)
        outbufs.append(fo.b)
        outo = io["outo"]

    def do_half(a, first):
        for k in range(8):
            P.dma("sp", lambda e, k=k: e.dma_start(out=hm.t[:, k, :], in_=resid[k * 128:(k + 1) * 128, a:a + HWH]),
                  writes=[hm.b])

        for pt in range(kcm // 4):
            P.dma("sp", lambda e, pt=pt: e.dma_start(
                out=opT.t[:, pt * 4:(pt + 1) * 4, :], in_=sc[pt][:, :, a:a + HWH].rearrange("r p c -> p r c")),
                reads=[scb], writes=[opT.b])

        def tap(n):
            import os
            if kind == "D" and first and os.environ.get("FUSE_DTAP", "") == str(n):
                for k in range(8):
                    P.dma("sp", lambda e, k=k: e.dma_start(out=io["dtap"][k * 128:(k + 1) * 128, 0:HWH], in_=hm.t[:, k, :]),
                          reads=[hm.b], sembuf=hm.b)
        tap(1)

        def ev_mix(ps, j, nsz, t0, tw):
            sl = hm.t[:, j, t0:t0 + tw]
            P.op("dve", lambda e: e.tensor_tensor(out=sl, in0=ps.t[:, :tw], in1=sl, op=ALU.add),
                 reads=[ps.b, hm.b], writes=[hm.b])
        proj_fm(cx, w_mix, kcm, 0, D, opT, 0, HWH, ev_mix, ngroup=256 if kcm == 16 else 512)
        tap(2)
        rmsnorm_fm(cx, hm, gf, uT, 0, HWH)
        ffn_fm(cx, hm, uT, w_up, w_down, cw, cb, HW, 2, actT, pre, acc)
        tap(3)

        if kind == "B":
            if first:
                P.op("dve", lambda e: e.tensor_scalar(out=hm.t[:, :, 2:4], in0=hm.t[:, :, 2:4],
                                                      scalar1=hmask.t[:, 0:1], scalar2=None, op0=ALU.mult),
                     reads=[hm.b, hmask.b], writes=[hm.b])
            for k in range(8):
                P.dma("sp", lambda e, k=k: e.dma_start(out=h1scr[k * 128:(k + 1) * 128, a:a + HW],
                                                       in_=hm.t[:, k, 2:2 + HW]), reads=[hm.b], sembuf=hm.b)
            skip = 2 if first else 0
            c0 = 2 + skip
            wk = HW - skip
            rel0 = a + c0 - 4
            rmsnorm_fm(cx, hm, gkv, uT, c0, wk)

            def mk_ev(row0, scale):
                def ev(ps, j, nsz, t0, tw):
                    s = stg[stg_i[0] % 4]
                    stg_i[0] += 1
                    P.op("act", lambda e: e.activation(out=s.t[:, :tw], in_=ps.t[:, :tw], func=AF.Copy, scale=scale),
                         reads=[ps.b], writes=[s.b])
                    P.dma("sp", lambda e: e.dma_start(
                        out=b2[j // 2, row0 + j % 2][:, rel0 + t0:rel0 + t0 + tw], in_=s.t[:, :tw]),
                        reads=[s.b], sembuf=s.b)
                return ev
            proj_fm(cx, w_kv, 8, 0, D, uT, 0, wk, mk_ev(2, 1.0))

            def ev_v(ps, ci, t0, csz, n_off, nw):
                s = stg[stg_i[0] % 4]
                stg_i[0] += 1
                P.op("dve", lambda e: e.tensor_copy(out=s.t[:csz, :nw], in_=ps.t[:csz, :nw]),
                     reads=[ps.b], writes=[s.b])
                for u in range(nw // 128):
                    f0 = n_off + u * 128
                    off = ((f0 // 256) * 6 + 4 + (f0 % 256) // 128) * 128 * TQ + (rel0 + t0) * 128
                    P.dma("sp", lambda e, u=u, off=off: e.dma_start(
                        out=bass.AP(b2h, off, [[128, csz], [1, 128]]), in_=s.t[:csz, u * 128:(u + 1) * 128]),
                        reads=[s.b], sembuf=s.b)
            proj_tm(cx, w_kv, 8, D, D, uT, 0, wk, ev_v)
            rmsnorm_fm(cx, hm, gq, uT, c0, wk)
            proj_fm(cx, w_q, 8, 0, D, uT, 0, wk, mk_ev(0, 0.125))
        else:
            for (t0, tw) in tiles(HW):
                fin_tile(a, t0, tw)

    def fin_tile(a, t0, tw):
        rmsnorm_fm(cx, hm, gfin, fo, 2 + t0, tw)
        for k in range(8):
            P.dma("sp", lambda e, k=k: e.dma_start(out=outo[k * 128:(k + 1) * 128, a + t0:a + t0 + tw],
                                                   in_=fo.t[:, k, :tw]), reads=[fo.b], sembuf=fo.b)

    do_half(0, True)
    do_half(HW, False)
    return outbufs


def phase_attn(P, io):
    oTd = io["b3"]
    sc, sch, scb = io["sc"], io["sch"], io["scb"]
    NB = 65
    kT = P.sb("kT", [128, 4, LB], BF16)
    vS = P.sb("vS", [128, NB, 320], BF16)
    qt = [P.sb(f"qt{i}", [128, 4, 512], BF16) for i in range(2)]
    spA = P.sb("spA", [128, NB, 512], BF16)
    wt = [P.sb(f"wt{i}", [128, 1024], BF16) for i in range(2)]
    tmp = [P.sb(f"tmp{i}", [128, 1024], F32) for i in range(2)]
    carry = P.sb("carry", [128, 512], F32)
    ost = [P.sb(f"ost{i}", [64, 512], BF16) for i in range(2)]
    negU = P.sb("negU", [128, 128], BF16)
    onesb = P.sb("onesb", [128, 128], BF16)
    masks = [P.sb(f"mask{i}", [128, 512], BF16) for i in range(4)]
    ps1 = [P.psum(f"ps1_{i}", [128, 512]) for i in range(2)]
    ps2 = [P.psum(f"ps2_{i}", [128, 512]) for i in range(2)]
    pcb = [P.psum(f"pcb{i}", [128, 512]) for i in range(2)]
    po = [P.psum(f"po{i}", [128, 512]) for i in range(2)]

    P.op("pool", lambda e: e.memset(onesb.t[:], 1.0), writes=[onesb.b])
    P.op("pool", lambda e: e.memset(negU.t[:], -1.0), writes=[negU.b])
    P.op("pool", lambda e: e.memset(kT.t[64:128, :, :], 0.0), writes=[kT.b])
    P.op("pool", lambda e: e.memset(vS.t[:, :, 256:320], 0.0), writes=[vS.b])
    for qq in qt:
        P.op("pool", lambda e, qq=qq: e.memset(qq.t[64:128, :, :], 0.0), writes=[qq.b])
    P.op("pool", lambda e: e.affine_select(out=negU.t[:], in_=negU.t[:], pattern=[[-1, 128]], compare_op=ALU.is_ge,
                                           fill=0.0, base=0, channel_multiplier=1), reads=[negU.b], writes=[negU.b])
    for i in range(4):
        P.op("pool", lambda e, i=i: e.memset(masks[i].t[:], 1.0), writes=[masks[i].b])
        P.op("pool", lambda e, i=i: e.affine_select(out=masks[i].t[:], in_=masks[i].t[:], pattern=[[1, 512]],
                                                    compare_op=ALU.is_gt, fill=0.0, base=-128 * i,
                                                    channel_multiplier=-1), reads=[masks[i].b], writes=[masks[i].b])
    CH = 128 * TQ
    for X in range(2):
        for r in range(4):
            P.dma("sp", lambda e, X=X, r=r: e.dma_start(
                out=kT.t[0:64, 2 * X:2 * X + 2, r * TQ:(r + 1) * TQ],
                in_=sc[2 + X, r].rearrange("(two p) c -> p two c", p=64)), reads=[scb], writes=[kT.b])
    zt = P.sb("zt", [128, 2, 2], BF16)
    P.op("pool", lambda e: e.memset(zt.t[:], 0.0), writes=[zt.b])
    P.dma("sp", lambda e: e.dma_start(out=oTd[0][:, :, 0:2].rearrange("j p c -> p j c"), in_=zt.t[:]),
          reads=[zt.b], sembuf=zt.b)
    def ldv_piece(X, r, lrow, nrow, blk, p0, nblk):
        off = ((4 + X) * 4 + r) * CH + lrow * 128
        if nblk:
            P.dma("sp", lambda e: e.dma_start(out=vS.t[:, blk:blk + nblk, X * 128:(X + 1) * 128],
                                              in_=bass.AP(sch, off, [[128, 128], [128 * 128, nblk], [1, 128]])),
                  reads=[scb], writes=[vS.b])
        else:
            P.dma("sp", lambda e: e.dma_start(out=vS.t[p0:p0 + nrow, blk, X * 128:(X + 1) * 128],
                                              in_=bass.AP(sch, off, [[128, nrow], [1, 128]])),
                  reads=[scb], writes=[vS.b])
    for X in range(2):
        for r in range(4):
            lo, hi = TQ * r, TQ * r + TQ
            pos = lo
            if pos % 128:
                n = 128 - pos % 128
                ldv_piece(X, r, pos - lo, n, pos // 128, pos % 128, 0)
                pos += n
            nfull = (hi - pos) // 128
            if nfull:
                ldv_piece(X, r, pos - lo, 128 * nfull, pos // 128, 0, nfull)
                pos += 128 * nfull
            if pos < hi:
                ldv_piece(X, r, pos - lo, hi - pos, pos // 128, 0, 0)

    spb = [P.buf(f"sp{b_}") for b_ in range(NB)]
    pex = [P.sb(f"pex{i}", [128, 1024], F32) for i in range(2)]
    onef = P.sb("onef", [128, 1], F32)
    P.op("pool", lambda e: e.memset(onef.t[:], 1.0), writes=[onef.b])
    cnt = {"a": 0, "b": 0, "o": 0, "ap": 0, "bp": 0}

    def tile_geom(j):
        t0 = 512 * j
        tw = 512 if j < 16 else 16
        nblk = 4 * j + 4 if j < 16 else 65
        return t0, tw, nblk

    def load_q(j):
        t0, tw, nblk = tile_geom(j)
        q = qt[j % 2]
        for X in range(2):
            for r in range(4):
                lo, hi = max(t0, TQ * r), min(t0 + tw, TQ * r + TQ)
                if lo < hi:
                    P.dma("sp", lambda e, X=X, r=r, lo=lo, hi=hi: e.dma_start(
                        out=q.t[0:64, 2 * X:2 * X + 2, lo - t0:hi - t0],
                        in_=sc[X, r].rearrange("(two p) c -> p two c", p=64)[:, :, lo - TQ * r:hi - TQ * r]),
                        reads=[scb], writes=[q.b])

    def blkinfo(j, blk):
        ksz = 128 if blk < 64 else 16
        diag = None
        if j < 16 and blk >= 4 * j:
            diag = blk - 4 * j
        if j == 16 and blk == 64:
            diag = 0
        return ksz, diag

    def a_step(j, h, blk):
        t0, tw, nblk = tile_geom(j)
        q = qt[j % 2]
        ksz, diag = blkinfo(j, blk)
        i1 = cnt["a"] % 2
        cnt["a"] += 1
        p1, px = ps1[i1], pex[cnt["ap"] % 2]
        cnt["ap"] += 1
        P.op("pe", lambda e: e.matmul(p1.t[:ksz, :tw], lhsT=kT.t[:, h, blk * 128:blk * 128 + ksz], rhs=q.t[:, h, :tw],
                                      start=True, stop=True), reads=[kT.b, q.b], writes=[p1.b])
        P.op("act", lambda e: e.activation(out=px.t[:ksz, :tw], in_=p1.t[:ksz, :tw], func=AF.Exp),
             reads=[p1.b], writes=[px.b])
        P.op("act", lambda e: e.activation(out=spA.t[:ksz, blk, :tw], in_=px.t[:ksz, :tw], func=AF.Ln,
                                           bias=onef.t[:ksz, :]), reads=[px.b, onef.b], writes=[spb[blk]])
        if diag is not None:
            P.op("dve", lambda e: e.tensor_tensor(out=spA.t[:ksz, blk, :tw], in0=spA.t[:ksz, blk, :tw],
                                                   in1=masks[diag].t[:ksz, :tw], op=ALU.mult),
                 reads=[spb[blk], masks[diag].b], writes=[spb[blk]])

    class BState:
        pass

    def b_begin(j, h):
        st = BState()
        st.j, st.h = j, h
        st.t0, st.tw, st.nblk = tile_geom(j)
        st.pacc = po[cnt["o"] % 2]
        st.first = True
        st.pend = None
        return st

    def b_pv(st, blk, ksz, w, start):
        h, tw, pacc = st.h, st.tw, st.pacc
        P.op("pe", lambda e: e.matmul(pacc.t[:128, :tw], lhsT=vS.t[:ksz, blk, h * 64:h * 64 + 128],
                                      rhs=w.t[:ksz, :tw], start=start, stop=(blk == 0)),
             reads=[vS.b, w.b], writes=[pacc.b])

    def b_step(st, blk):
        j, h, tw = st.j, st.h, st.tw
        q = qt[j % 2]
        ksz, diag = blkinfo(j, blk)
        i2 = cnt["b"] % 2
        cnt["b"] += 1
        ip = cnt["bp"] % 2
        cnt["bp"] += 1
        p2, pc, w, tm = ps2[i2], pcb[i2], wt[ip], tmp[ip]
        first = st.first
        P.op("pe", lambda e: e.matmul(p2.t[:ksz, :tw], lhsT=negU.t[:ksz, :ksz], rhs=spA.t[:ksz, blk, :tw],
                                      start=True, stop=False), reads=[negU.b, spb[blk]], writes=[p2.b], sig=False)
        P.op("pe", lambda e: e.matmul(p2.t[:ksz, :tw], lhsT=kT.t[:, h, blk * 128:blk * 128 + ksz], rhs=q.t[:, h, :tw],
                                      start=False, stop=True), reads=[kT.b, q.b], writes=[p2.b])
        if blk > 0:
            P.op("pe", lambda e: e.matmul(pc.t[:, :tw], lhsT=onesb.t[:ksz, :], rhs=spA.t[:ksz, blk, :tw],
                                          start=True, stop=True), reads=[onesb.b, spb[blk]], writes=[pc.b])
        if st.pend is not None:
            for pe_ in st.pend:
                b_pv(st, *pe_)
        if first:
            P.op("act", lambda e: e.activation(out=w.t[:ksz, :tw], in_=p2.t[:ksz, :tw], func=AF.Exp),
                 reads=[p2.b], writes=[w.b])
        else:
            P.op("dve", lambda e: e.tensor_tensor(out=tm.t[:ksz, :tw], in0=p2.t[:ksz, :tw], in1=carry.t[:ksz, :tw],
                                                  op=ALU.subtract), reads=[p2.b, carry.b], writes=[tm.b])
            P.op("act", lambda e: e.activation(out=w.t[:ksz, :tw], in_=tm.t[:ksz, :tw], func=AF.Exp),
                 reads=[tm.b], writes=[w.b])
        if diag is not None:
            P.op("dve", lambda e: e.tensor_tensor(out=w.t[:ksz, :tw], in0=w.t[:ksz, :tw],
                                                   in1=masks[diag].t[:ksz, :tw], op=ALU.mult),
                 reads=[w.b, masks[diag].b], writes=[w.b])
        if blk > 0:
            if first:
                P.op("dve", lambda e: e.tensor_copy(out=carry.t[:, :tw], in_=pc.t[:, :tw]),
                     reads=[pc.b], writes=[carry.b])
            else:
                P.op("dve", lambda e: e.tensor_tensor(out=carry.t[:, :tw], in0=pc.t[:, :tw], in1=carry.t[:, :tw],
                                                      op=ALU.add), reads=[pc.b, carry.b], writes=[carry.b])
        st.pend = [(blk, ksz, w, first)]
        st.first = False

    def b_end(st):
        h, t0, tw, pacc = st.h, st.t0, st.tw, st.pacc
        for pe_ in st.pend:
            b_pv(st, *pe_)
        cnt["o"] += 1
        o = ost[cnt["o"] % 2]
        P.op("dve", lambda e: e.tensor_copy(out=o.t[:, :tw], in_=pacc.t[:64, :tw]), reads=[pacc.b], writes=[o.b])
        for d in range(4):
            lo, hi = max(t0, TQ * d - 2), min(t0 + tw, TQ * d + TQ)
            if lo < hi:
                c0d = TQ * d - 2
                P.dma("sp", lambda e, d=d, lo=lo, hi=hi, c0d=c0d: e.dma_start(
                    out=oTd[d, h // 2][(h % 2) * 64:(h % 2) * 64 + 64, lo - c0d:hi - c0d], in_=o.t[:, lo - t0:hi - t0]),
                    reads=[o.b], sembuf=o.b)

    def a_pair(j, h, blk):
        t0, tw, nblk = tile_geom(j)
        q = qt[j % 2]
        px = pex[cnt["ap"] % 2]
        cnt["ap"] += 1
        for u, bb in ((1, blk), (0, blk - 1)):
            p1 = ps1[cnt["a"] % 2]
            cnt["a"] += 1
            P.op("pe", lambda e, p1=p1, bb=bb: e.matmul(p1.t[:, :tw], lhsT=kT.t[:, h, bb * 128:bb * 128 + 128],
                                                       rhs=q.t[:, h, :tw], start=True, stop=True),
                 reads=[kT.b, q.b], writes=[p1.b])
            P.op("act", lambda e, p1=p1, u=u: e.activation(out=px.t[:, u * 512:u * 512 + tw], in_=p1.t[:, :tw],
                                                         func=AF.Exp), reads=[p1.b], writes=[px.b])
        P.op("act", lambda e: e.activation(out=spA.t[:, blk - 1:blk + 1, :tw],
                                           in_=px.t[:, :].rearrange("p (u c) -> p u c", u=2)[:, :, :tw],
                                           func=AF.Ln, bias=onef.t[:, :]),
             reads=[px.b, onef.b], writes=[spb[blk - 1], spb[blk]])
        for bb in (blk, blk - 1):
            ksz, diag = blkinfo(j, bb)
            if diag is not None:
                P.op("dve", lambda e, bb=bb, diag=diag: e.tensor_tensor(
                    out=spA.t[:, bb, :tw], in0=spA.t[:, bb, :tw], in1=masks[diag].t[:, :tw], op=ALU.mult),
                    reads=[spb[bb], masks[diag].b], writes=[spb[bb]])

    def b_pair(st, blk):
        j, h, tw = st.j, st.h, st.tw
        q = qt[j % 2]
        ip = cnt["bp"] % 2
        cnt["bp"] += 1
        w, tm = wt[ip], tmp[ip]
        first = st.first
        for u, bb in ((1, blk), (0, blk - 1)):
            i2 = cnt["b"] % 2
            cnt["b"] += 1
            p2, pc = ps2[i2], pcb[i2]
            P.op("pe", lambda e, p2=p2, bb=bb: e.matmul(p2.t[:, :tw], lhsT=negU.t[:, :], rhs=spA.t[:, bb, :tw],
                                                       start=True, stop=False),
                 reads=[negU.b, spb[bb]], writes=[p2.b], sig=False)
            P.op("pe", lambda e, p2=p2, bb=bb: e.matmul(p2.t[:, :tw], lhsT=kT.t[:, h, bb * 128:bb * 128 + 128],
                                                       rhs=q.t[:, h, :tw], start=False, stop=True),
                 reads=[kT.b, q.b], writes=[p2.b])
            if bb > 0:
                P.op("pe", lambda e, pc=pc, bb=bb: e.matmul(pc.t[:, :tw], lhsT=onesb.t[:, :], rhs=spA.t[:, bb, :tw],
                                                           start=True, stop=True),
                     reads=[onesb.b, spb[bb]], writes=[pc.b])
            if u == 1 and st.pend is not None:
                for pe_ in st.pend:
                    b_pv(st, *pe_)
                st.pend = None
            if first and u == 1:
                P.op("dve", lambda e, p2=p2, u=u: e.tensor_copy(out=tm.t[:, u * 512:u * 512 + tw], in_=p2.t[:, :tw]),
                     reads=[p2.b], writes=[tm.b])
            else:
                P.op("dve", lambda e, p2=p2, u=u: e.tensor_tensor(out=tm.t[:, u * 512:u * 512 + tw], in0=p2.t[:, :tw],
                                                                 in1=carry.t[:, :tw], op=ALU.subtract),
                     reads=[p2.b, carry.b], writes=[tm.b])
            if bb > 0:
                if first and u == 1:
                    P.op("dve", lambda e, pc=pc: e.tensor_copy(out=carry.t[:, :tw], in_=pc.t[:, :tw]),
                         reads=[pc.b], writes=[carry.b])
                else:
                    P.op("dve", lambda e, pc=pc: e.tensor_tensor(out=carry.t[:, :tw], in0=pc.t[:, :tw],
                                                                in1=carry.t[:, :tw], op=ALU.add),
                         reads=[pc.b, carry.b], writes=[carry.b])
        P.op("act", lambda e: e.activation(out=w.t[:, :].rearrange("p (u c) -> p u c", u=2)[:, :, :tw],
                                           in_=tm.t[:, :].rearrange("p (u c) -> p u c", u=2)[:, :, :tw], func=AF.Exp),
             reads=[tm.b], writes=[w.b])
        pend = []
        for u, bb in ((1, blk), (0, blk - 1)):
            ksz, diag = blkinfo(j, bb)
            wv = T(w.t[:, u * 512:(u + 1) * 512], w.b)
            if diag is not None:
                P.op("dve", lambda e, wv=wv, diag=diag: e.tensor_tensor(
                    out=wv.t[:, :tw], in0=wv.t[:, :tw], in1=masks[diag].t[:, :tw], op=ALU.mult),
                    reads=[w.b, masks[diag].b], writes=[w.b])
            pend.append((bb, 128, wv, first and u == 1))
        st.pend = pend
        st.first = False

    streams = [(j, h) for j in range(17) for h in range(4)]
    def a_range(j, h, hi, lo):
        blk = hi - 1
        while blk >= lo:
            if j < 16 and blk - 1 >= lo and blk < 64:
                a_pair(j, h, blk)
                blk -= 2
            else:
                a_step(j, h, blk)
                blk -= 1

    load_q(0)
    a_range(0, 0, tile_geom(0)[2], 0)
    for k, (j, h) in enumerate(streams):
        nb_cur = tile_geom(j)[2]
        nxt = streams[k + 1] if k + 1 < len(streams) else None
        if nxt is not None:
            if nxt[1] == 0:
                load_q(nxt[0])
            nb_nxt = tile_geom(nxt[0])[2]
            a_range(nxt[0], nxt[1], nb_nxt, nb_cur)
        st = b_begin(j, h)
        blk = nb_cur - 1
        while blk >= 0:
            if j < 16 and blk >= 1:
                b_pair(st, blk)
                if nxt is not None:
                    a_range(nxt[0], nxt[1], blk + 1, blk - 1)
                blk -= 2
            else:
                b_step(st, blk)
                if nxt is not None:
                    a_range(nxt[0], nxt[1], blk + 1, blk)
                blk -= 1
        b_end(st)
        if h == 3 and j in (4, 8, 12):
            io["after_dest"](j // 4 - 1, [o.b for o in ost] + [zt.b])
    return [o.b for o in ost] + [zt.b]


def phase_ssd(P, io):
    xTd, wsd, g_in, cwd, cbd = io["xT"], io["wsel"], io["g_in"], io["cw"], io["cb"]
    dtbd, alogd, dskd, ggd, hgo = io["dtb"], io["alog"], io["dsk"], io["gg"], io["b1"]
    cx = Ctx(P, wslot_elems=8 * 1288, nwslots=1, nps=4)
    W = TQ
    CH = chunks128(W)
    NCH = len(CH)
    ptr = [P.psum(f"ptr{i}", [128, 1024], BF16) for i in range(2)]
    pacc = [P.psum(f"pacc{i}", [128, 512]) for i in range(2)]
    gin = load_small(cx, "gin", g_in, [128, 8])
    cw = P.sb("cw", [128, 6, 4], F32)
    P.dma("sp", lambda e: e.dma_start(out=cw.t[:].rearrange("p a b -> p (a b)"), in_=cwd), writes=[cw.b])
    cb = load_small(cx, "cb", cbd, [128, 6])
    dtb = load_small(cx, "dtb", dtbd, [128, 8])
    aneg = load_small(cx, "aneg", alogd, [128, 8])
    dsk = load_small(cx, "dsk", dskd, [128, 8])
    gg = load_small(cx, "gg", ggd, [128, 512])
    P.op("act", lambda e: e.activation(out=aneg.t[:], in_=aneg.t[:], func=AF.Exp), reads=[aneg.b], writes=[aneg.b])
    P.op("dve", lambda e: e.tensor_scalar(out=aneg.t[:], in0=aneg.t[:], scalar1=-1.0, scalar2=None, op0=ALU.mult),
         reads=[aneg.b], writes=[aneg.b])
    wv, wb = cx.load_w(wsd, 0, 8, 0, 1288)

    ident = P.sb("ident", [128, 128], BF16)
    P.op("pool", lambda e: e.memset(ident.t[:], 1.0), writes=[ident.b])
    P.op("pool", lambda e: e.affine_select(out=ident.t[:], in_=ident.t[:], pattern=[[-1, 128]], compare_op=ALU.is_equal,
                                           fill=0.0, base=0, channel_multiplier=1), reads=[ident.b], writes=[ident.b])
    triI = P.sb("triI", [128, 128], F32)
    P.op("pool", lambda e: e.memset(triI.t[:], 1.0), writes=[triI.b])
    P.op("pool", lambda e: e.affine_select(out=triI.t[:], in_=triI.t[:], pattern=[[1, 128]], compare_op=ALU.is_ge,
                                           fill=0.0, base=0, channel_multiplier=-1), reads=[triI.b], writes=[triI.b])
    mstr = P.sb("mstr", [128, 128], F32)
    P.op("pool", lambda e: e.memset(mstr.t[:], 1.0), writes=[mstr.b])
    P.op("pool", lambda e: e.affine_select(out=mstr.t[:], in_=mstr.t[:], pattern=[[-1, 128]], compare_op=ALU.is_gt,
                                           fill=0.0, base=0, channel_multiplier=1), reads=[mstr.b], writes=[mstr.b])

    xins = [P.sb(f"xin{i}", [128, 8, 416], F32) for i in range(2)]
    xcnt = [0]
    prefetched = {}
    uT = P.sb("uT", [128, 8, W + 3], BF16)
    pre = P.sb("pre", [128, W + 3], F32)
    acc = P.sb("acc", [128, W], F32)
    xsT = P.sb("xsT", [128, 6, W], BF16)
    xs_tm = P.sb("xs_tm", [128, NCH, 512], BF16)
    B_tm = P.sb("B_tm", [128, NCH, 128], BF16)
    zs_tm = P.sb("zs_tm", [128, NCH, 512], BF16)
    dt = P.sb("dt", [128, NCH, 8], F32)
    dta = P.sb("dta", [128, NCH, 8], F32)
    cs = P.sb("cs", [128, NCH, 8], F32)
    ecs = P.sb("ecs", [128, NCH, 8], F32)
    wst = P.sb("wst", [128, NCH, 8], F32)
    cdec = P.sb("cdec", [128, NCH, 8], F32)
    onef1 = P.sb("onef1", [128, 1], F32)
    P.op("pool", lambda e: e.memset(onef1.t[:], 1.0), writes=[onef1.b])
    for t_ in (dt, dta, cs, ecs, wst, cdec):
        P.op("pool", lambda e, t_=t_: e.memset(t_.t[:], 0.0), writes=[t_.b])
    S = P.sb("S", [128, 512], F32)
    Sb = P.sb("Sb", [128, 512], BF16)
    P.op("pool", lambda e: e.memset(S.t[:], 0.0), writes=[S.b])
    P.op("pool", lambda e: e.memset(Sb.t[:], 0.0), writes=[Sb.b])
    cbT = P.sb("cbT", [128, 128], F32)
    lh8 = [P.sb(f"lh8_{i}", [128, 8, 128], F32) for i in range(2)]
    dec8 = [P.sb(f"dec8_{i}", [128, 8, 128], F32) for i in range(2)]
    MT8 = [P.sb(f"MT8_{i}", [128, 8, 128], BF16) for i in range(2)]
    xdt = P.sb("xdt", [128, 512], BF16)
    xdte = P.sb("xdte", [128, 512], BF16)
    y1 = P.sb("y1", [128, 512], F32)
    y2 = P.sb("y2", [128, 512], F32)
    hgn = P.sb("hgn", [128, 512], BF16)
    ss = P.sb("ss", [128, 2], F32)
    hst = [P.sb(f"hst{i}", [128, 4, 128], BF16) for i in range(2)]
    cnt = {"h": 0, "l": 0}
    v3 = lambda ap: ap.rearrange("p (h d) -> p h d", h=8)
    bc = lambda ap: ap.unsqueeze(2).to_broadcast([ap.shape[0], 8, 64])

    def do_seg(si):
        s0 = si * W

        def fetch(sj, ti, tw):
            xin = xins[xcnt[0] % 2]
            xcnt[0] += 1
            for k in range(8):
                P.dma(("sp", "act")[k % 2], lambda e, k=k: e.dma_start(
                    out=xin.t[:, k, :tw], in_=xTd[sj, ti, k]), writes=[xin.b])
            return xin

        def ld(t0, tw):
            ti = t0 // tw
            xin = prefetched.pop((si, ti), None) or fetch(si, ti, tw)
            rmsnorm_fm(cx, xin, gin, uT, 0, tw, dst_c0=t0)
        for (t0, tw) in tiles(W + 3):
            ld(t0, tw)
        if si < 3:
            for ti in range(2):
                prefetched[(si + 1, ti)] = fetch(si + 1, ti, 411)

        def inproj(jc):
            for (t0, tw) in tiles(W + 3):
                ps = cx.psum()
                for k in range(8):
                    P.op("pe", lambda e, ps=ps, k=k, t0=t0, tw=tw: e.matmul(
                        ps.t[:, :tw], lhsT=wv[:, k, 512 + jc * 128:512 + (jc + 1) * 128], rhs=uT.t[:, k, t0:t0 + tw],
                        start=(k == 0), stop=(k == 7)), reads=[wb, uT.b], writes=[ps.b])
                P.op("act", lambda e, ps=ps, t0=t0, tw=tw: e.activation(out=pre.t[:, t0:t0 + tw], in_=ps.t[:, :tw],
                                                                       func=AF.Identity), reads=[ps.b], writes=[pre.b])
            conv_fm(cx, pre, cw, cb, jc, 4, W, acc)
            P.op("act", lambda e: e.activation(out=xsT.t[:, jc, :], in_=acc.t[:, :], func=AF.Silu),
                 reads=[acc.b], writes=[xsT.b])
        for jc in range(6):
            inproj(jc)

        def tr(ci, c0, csz):
            pt = ptr[ci % 2]
            for jc in range(5):
                P.op("pe", lambda e, jc=jc: e.transpose(pt.t[:csz, jc * 128:(jc + 1) * 128],
                                                        xsT.t[:, jc, c0:c0 + csz], ident.t[:]),
                     reads=[xsT.b, ident.b], writes=[pt.b])
            P.op("dve", lambda e: e.tensor_copy(out=xs_tm.t[:csz, ci, :], in_=pt.t[:csz, 0:512]),
                 reads=[pt.b], writes=[xs_tm.b])
            P.op("dve", lambda e: e.tensor_copy(out=B_tm.t[:csz, ci, :], in_=pt.t[:csz, 512:640]),
                 reads=[pt.b], writes=[B_tm.b])

        def zz(ci, c0, csz):
            ps = cx.psum()
            for k in range(8):
                P.op("pe", lambda e, k=k: e.matmul(ps.t[:csz, :512], lhsT=uT.t[:, k, 3 + c0:3 + c0 + csz],
                                                  rhs=wv[:, k, 0:512], start=(k == 0), stop=(k == 7)),
                     reads=[wb, uT.b], writes=[ps.b])
            P.op("act", lambda e: e.activation(out=zs_tm.t[:csz, ci, :], in_=ps.t[:csz, :512], func=AF.Silu),
                 reads=[ps.b], writes=[zs_tm.b])

        def dd(ci, c0, csz):
            ps = cx.psum()
            for k in range(8):
                P.op("pe", lambda e, k=k: e.matmul(ps.t[:csz, :8], lhsT=uT.t[:, k, 3 + c0:3 + c0 + csz],
                                                  rhs=wv[:, k, 1280:1288], start=(k == 0), stop=(k == 7)),
                     reads=[wb, uT.b], writes=[ps.b])
            P.op("dve", lambda e: e.tensor_tensor(out=dt.t[:csz, ci, :], in0=ps.t[:csz, :8], in1=dtb.t[:csz, :],
                                                  op=ALU.add), reads=[ps.b, dtb.b], writes=[dt.b])

        def da(ci, c0, csz):
            P.op("dve", lambda e: e.tensor_tensor(out=dta.t[:csz, ci, :], in0=dt.t[:csz, ci, :], in1=aneg.t[:csz, :],
                                                  op=ALU.mult), reads=[dt.b, aneg.b], writes=[dta.b])

        def cc(ci, c0, csz):
            ps = cx.psum()
            P.op("pe", lambda e: e.matmul(ps.t[:csz, 0:8], lhsT=triI.t[:csz, :csz], rhs=dta.t[:csz, ci, :],
                                          start=True, stop=True), reads=[triI.b, dta.b], writes=[ps.b])
            P.op("pe", lambda e: e.matmul(ps.t[:, 8:16], lhsT=cx.ones.t[:csz, :], rhs=dta.t[:csz, ci, :],
                                          start=True, stop=True), reads=[cx.ones.b, dta.b], writes=[ps.b])
            P.op("dve", lambda e: e.tensor_copy(out=cs.t[:csz, ci, :], in_=ps.t[:csz, 0:8]),
                 reads=[ps.b], writes=[cs.b])
            P.op("dve", lambda e: e.tensor_tensor(out=wst.t[:csz, ci, :], in0=ps.t[:csz, 8:16],
                                                  in1=cs.t[:csz, ci, :], op=ALU.subtract),
                 reads=[ps.b, cs.b], writes=[wst.b])
            P.op("act", lambda e: e.activation(out=cdec.t[:, ci, :], in_=ps.t[:, 8:16], func=AF.Exp),
                 reads=[ps.b], writes=[cdec.b])
        for ci, (c0, csz) in enumerate(CH):
            tr(ci, c0, csz)
        for ci, (c0, csz) in enumerate(CH):
            zz(ci, c0, csz)
        for ci, (c0, csz) in enumerate(CH):
            dd(ci, c0, csz)
        P.op("act", lambda e: e.activation(out=dt.t[:], in_=dt.t[:], func=AF.Exp), reads=[dt.b], writes=[dt.b])
        P.op("act", lambda e: e.activation(out=dt.t[:], in_=dt.t[:], func=AF.Ln, bias=onef1.t[:, :]),
             reads=[dt.b, onef1.b], writes=[dt.b])
        for ci, (c0, csz) in enumerate(CH):
            da(ci, c0, csz)
        for ci, (c0, csz) in enumerate(CH):
            cc(ci, c0, csz)
        P.op("act", lambda e: e.activation(out=ecs.t[:], in_=cs.t[:], func=AF.Exp), reads=[cs.b], writes=[ecs.b])
        P.op("act", lambda e: e.activation(out=wst.t[:], in_=wst.t[:], func=AF.Exp), reads=[wst.b], writes=[wst.b])
        P.op("dve", lambda e: e.tensor_tensor(out=wst.t[:], in0=wst.t[:], in1=dt.t[:], op=ALU.mult),
             reads=[wst.b, dt.b], writes=[wst.b])
        for ci, (c0, csz) in enumerate(CH):
            do_chunk(s0, ci, c0, csz)

    def do_chunk(s0, ci, c0, csz):
        P.op("dve", lambda e: e.tensor_tensor(out=v3(xdt.t[:csz, :]), in0=v3(xs_tm.t[:csz, ci, :]),
                                              in1=bc(dt.t[:csz, ci, :]), op=ALU.mult),
             reads=[xs_tm.b, dt.b], writes=[xdt.b])
        P.op("pool", lambda e: e.tensor_tensor(out=v3(xdte.t[:csz, :]), in0=v3(xs_tm.t[:csz, ci, :]),
                                               in1=bc(wst.t[:csz, ci, :]), op=ALU.mult),
             reads=[xs_tm.b, wst.b], writes=[xdte.b])
        ps = cx.psum()
        P.op("pe", lambda e: e.matmul(ps.t[:csz, :csz], lhsT=xsT.t[:, 4, c0:c0 + csz], rhs=xsT.t[:, 5, c0:c0 + csz],
                                      start=True, stop=True), reads=[xsT.b], writes=[ps.b])
        P.op("dve", lambda e: e.tensor_tensor(out=cbT.t[:csz, :csz], in0=ps.t[:csz, :csz], in1=triI.t[:csz, :csz],
                                              op=ALU.mult), reads=[ps.b, triI.b], writes=[cbT.b])
        yp = pacc[ci % 2]
        l8, d8, m8 = lh8[ci % 2], dec8[ci % 2], MT8[ci % 2]
        P.op("dve", lambda e: e.tensor_tensor(
            out=l8.t[:csz, :, :csz], in0=mstr.t[:csz, :csz].unsqueeze(1).to_broadcast([csz, 8, csz]),
            in1=dta.t[:csz, ci, :].unsqueeze(2).to_broadcast([csz, 8, csz]), op=ALU.mult),
            reads=[mstr.b, dta.b], writes=[l8.b])
        pgs = [cx.psum(), cx.psum()]
        for hh in range(8):
            pg = pgs[hh // 4]
            P.op("pe", lambda e, pg=pg, hh=hh: e.matmul(pg.t[:csz, (hh % 4) * 128:(hh % 4) * 128 + csz],
                                                       lhsT=l8.t[:csz, hh, :csz], rhs=triI.t[:csz, :csz],
                                                       start=True, stop=True), reads=[l8.b, triI.b], writes=[pg.b])
        for g4 in range(2):
            pg = pgs[g4]
            P.op("act", lambda e, pg=pg, g4=g4: e.activation(
                out=d8.t[:csz, 4 * g4:4 * g4 + 4, :csz],
                in_=pg.t[:csz, :].rearrange("p (h s) -> p h s", h=4)[:, :, :csz], func=AF.Exp),
                reads=[pg.b], writes=[d8.b])
        P.op("dve", lambda e: e.tensor_tensor(
            out=m8.t[:csz, :, :csz], in0=d8.t[:csz, :, :csz],
            in1=cbT.t[:csz, :csz].unsqueeze(1).to_broadcast([csz, 8, csz]), op=ALU.mult),
            reads=[d8.b, cbT.b], writes=[m8.b])
        for hh in range(8):
            P.op("pe", lambda e, hh=hh: e.matmul(yp.t[:csz, hh * 64:(hh + 1) * 64], lhsT=m8.t[:csz, hh, :csz],
                                                rhs=xdt.t[:csz, hh * 64:(hh + 1) * 64], start=True, stop=True),
                 reads=[m8.b, xdt.b], writes=[yp.b])
        po_ = cx.psum()
        P.op("pe", lambda e: e.matmul(po_.t[:csz, :512], lhsT=xsT.t[:, 5, c0:c0 + csz], rhs=Sb.t[:, :],
                                      start=True, stop=True), reads=[xsT.b, Sb.b], writes=[po_.b])
        P.op("dve", lambda e: e.tensor_tensor(out=v3(y1.t[:csz, :]), in0=v3(po_.t[:csz, :512]),
                                              in1=bc(ecs.t[:csz, ci, :]), op=ALU.mult),
             reads=[po_.b, ecs.b], writes=[y1.b])
        P.op("dve", lambda e: e.tensor_tensor(out=y1.t[:csz, :], in0=yp.t[:csz, :512], in1=y1.t[:csz, :], op=ALU.add),
             reads=[yp.b, y1.b], writes=[y1.b])
        P.op("pool", lambda e: e.tensor_tensor(out=v3(y2.t[:csz, :]), in0=v3(xs_tm.t[:csz, ci, :]),
                                               in1=bc(dsk.t[:csz, :]), op=ALU.mult),
             reads=[xs_tm.b, dsk.b], writes=[y2.b])
        P.op("dve", lambda e: e.tensor_tensor(out=y1.t[:csz, :], in0=y1.t[:csz, :], in1=y2.t[:csz, :], op=ALU.add),
             reads=[y1.b, y2.b], writes=[y1.b])
        P.op("dve", lambda e: e.tensor_tensor(out=y1.t[:csz, :], in0=y1.t[:csz, :], in1=zs_tm.t[:csz, ci, :],
                                              op=ALU.mult), reads=[y1.b, zs_tm.b], writes=[y1.b])
        P.op("act", lambda e: e.activation(out=y2.t[:csz, :], in_=y1.t[:csz, :], func=AF.Square,
                                           accum_out=ss.t[:csz, 0:1]), reads=[y1.b], writes=[y2.b, ss.b])
        P.op("act", lambda e: e.activation(out=ss.t[:csz, 1:2], in_=ss.t[:csz, 0:1], func=AF.Ln,
                                           bias=cx.epst.t[:csz, :], scale=1.0 / 512), reads=[ss.b, cx.epst.b],
             writes=[ss.b])
        P.op("act", lambda e: e.activation(out=ss.t[:csz, 1:2], in_=ss.t[:csz, 1:2], func=AF.Exp, scale=-0.5),
             reads=[ss.b], writes=[ss.b])
        P.op("dve", lambda e: e.scalar_tensor_tensor(out=hgn.t[:csz, :], in0=y1.t[:csz, :], scalar=ss.t[:csz, 1:2],
                                                     in1=gg.t[:csz, :], op0=ALU.mult, op1=ALU.mult),
             reads=[y1.b, ss.b, gg.b], writes=[hgn.b])
        pt = ptr[ci % 2]
        hs = hst[cnt["h"] % 2]
        cnt["h"] += 1
        for jc in range(4):
            P.op("pe", lambda e, jc=jc: e.transpose(pt.t[:, jc * 128:jc * 128 + csz], hgn.t[:csz, jc * 128:(jc + 1) * 128],
                                                    ident.t[:csz, :csz]), reads=[hgn.b, ident.b], writes=[pt.b])
        P.op("act", lambda e: e.activation(out=hs.t[:, :, :csz],
                                           in_=pt.t[:, 0:512].rearrange("p (j c) -> p j c", j=4)[:, :, :csz],
                                           func=AF.Identity), reads=[pt.b], writes=[hs.b])
        si = s0 // W
        P.dma("sp", lambda e: e.dma_start(
            out=hgo[si][:, :, 4 + c0:4 + c0 + csz].rearrange("j p c -> p j c"), in_=hs.t[:, :, :csz]),
            reads=[hs.b], sembuf=hs.b)
        if c0 + csz == W and si < 3:
            P.dma("sp", lambda e: e.dma_start(
                out=hgo[si + 1][:, :, 0:4].rearrange("j p c -> p j c"), in_=hs.t[:, :, csz - 4:csz]),
                reads=[hs.b], sembuf=hs.b)
        pn = cx.psum()
        P.op("pe", lambda e: e.matmul(pn.t[:, :512], lhsT=B_tm.t[:csz, ci, :], rhs=xdte.t[:csz, :], start=True,
                                      stop=True), reads=[B_tm.b, xdte.b], writes=[pn.b])
        P.op("dve", lambda e: e.tensor_tensor(out=v3(S.t[:, :]), in0=v3(S.t[:, :]), in1=bc(cdec.t[:, ci, :]),
                                              op=ALU.mult), reads=[S.b, cdec.b], writes=[S.b])
        P.op("dve", lambda e: e.tensor_tensor(out=S.t[:, :], in0=pn.t[:, :512], in1=S.t[:, :], op=ALU.add),
             reads=[pn.b, S.b], writes=[S.b])
        P.op("pool", lambda e: e.tensor_copy(out=Sb.t[:, :], in_=S.t[:, :]), reads=[S.b], writes=[Sb.b])

    zt = P.sb("zt", [128, 4, 4], BF16)
    P.op("pool", lambda e: e.memset(zt.t[:], 0.0), writes=[zt.b])
    P.dma("sp", lambda e: e.dma_start(out=hgo[0][:, :, 0:4].rearrange("j p c -> p j c"), in_=zt.t[:]),
          reads=[zt.b], sembuf=zt.b)
    for si in range(4):
        do_seg(si)
        io["after_seg"](si, [h.b for h in hst] + [zt.b])
    return [h.b for h in hst] + [zt.b]


GROUPS = [[0, 1, 2, 3], [4, 5, 6, 7]]


def build_fused():
    nc = bass.Bass("TRN2", target_bir_lowering=False)

    def dr(n, s, dt=F32, k="ExternalInput"):
        return nc.dram_tensor(n, list(s), dt, kind=k)
    ioA = {"xT": dr("A_xT", [4, 5, 8, 128, 411]).ap(), "wsel": dr("A_wsel", [D, 1288]).ap(), "g_in": dr("A_g_in", [128, 8]).ap(),
           "cw": dr("A_cw", [128, 24]).ap(), "cb": dr("A_cb", [128, 6]).ap(), "dtb": dr("A_dtb", [128, 8]).ap(),
           "alog": dr("A_alog", [128, 8]).ap(), "dsk": dr("A_dsk", [128, 8]).ap(), "gg": dr("A_gg", [128, 512]).ap()}

    def tok_io(pfx, kcm):
        return {"w_mix": dr(pfx + "w_mix", [kcm * 128, D]).ap(), "w_up": dr(pfx + "w_up", [D, 2 * DFF]).ap(),
                "w_down": dr(pfx + "w_down", [DFF, D]).ap(), "g_ffn": dr(pfx + "g_ffn", [128, 8]).ap(),
                "cw": dr(pfx + "cw", [128, 132]).ap(), "cb": dr(pfx + "cb", [128, 44]).ap()}
    ioB = tok_io("B_", 16)
    ioB.update({"resid": dr("B_resid", [D, TQ + 4]).ap(), "w_kv": dr("B_w_kv", [D, 2 * D]).ap(),
                "w_q": dr("B_w_q", [D, D]).ap(), "g_kv": dr("B_g_kv", [128, 8]).ap(), "g_q": dr("B_g_q", [128, 8]).ap(),
                "hmask": dr("B_hmask", [128, 1]).ap()})
    ioD = tok_io("D_", 8)
    ioD.update({"g_fin": dr("D_g_fin", [128, 8]).ap(), "outo": dr("out", [D, TQ], F32, "ExternalOutput").ap()})
    idxd = dr("idx", [1, 1], I32).ap()

    C1, C3, CH2 = TQ + 4, TQ + 2, 128 * TQ
    b1 = nc.dram_tensor("b1", [4, 4, 128, C1], BF16)
    g1 = nc.dram_tensor("g1", [4, 4, 4, 128, C1], BF16)
    b2 = nc.dram_tensor("b2", [4, 6, 128, TQ], BF16)
    g2 = nc.dram_tensor("g2", [4, 6, 4, 128, TQ], BF16)
    b3 = nc.dram_tensor("b3", [4, 2, 128, C3], BF16)
    g3 = nc.dram_tensor("g3", [4, 2, 4, 128, C3], BF16)
    h1scr = nc.dram_tensor("h1scr", [D, TQ + 2], F32)
    dtap = nc.dram_tensor("dtap", [D, 1028], F32)
    sc1 = nc.dram_tensor("sc1", [4, 4, 128, C1], BF16)
    sc2 = nc.dram_tensor("sc2", [6, 4, 128, TQ], BF16)
    sc3 = nc.dram_tensor("sc3", [2, 4, 128, C3], BF16)

    P = Prog(nc)
    regs = {n: P.stack.enter_context(nc.gpsimd.register(n)) for n in ("ridx", "r1", "r2q", "r3", "rtmp")}
    it = P.sb("idxt", [1, 2], I32)
    scr = P.sb("scr", [1, 16], BF16)
    P.persist = P.off
    P.dma("pool", lambda e: e.dma_start(out=it.t[0:1, 0:1], in_=idxd), writes=[it.b])

    def setup(e):
        e.reg_load(regs["ridx"], it.t[0:1, 0:1])
        e.reg_mul(regs["r1"], regs["ridx"], 16 * 128 * C1)
        e.reg_mul(regs["r2q"], regs["ridx"], 24 * CH2)
        e.reg_mul(regs["r3"], regs["ridx"], 8 * 128 * C3)
        return e.memset(scr.t[:], 0.0)
    P.op("pool", setup, reads=[it.b], writes=[scr.b])

    def pull(sct, gt, reg, nrows, ncols):
        b = P.buf("sc")
        P.dma("pool", lambda e: e.dma_start(out=sct.ap().rearrange("a b c d -> (a b c) d"),
                                            in_=bass.AP(gt, reg, [[ncols, nrows], [1, ncols]])), writes=[b])
        return b

    def gather_dest(bt, gt, d, n1, ob):
        for c in range(n1):
            P.collective("AllGather", bt.ap()[d, c].opt(), gt.ap()[d, c].opt(), GROUPS, ob if c == 0 else [])

    def gather_all(bt, gt, n0, n1, ob):
        for d in range(n0):
            gather_dest(bt, gt, d, n1, ob if d == 0 else [])
        P.collective_wait()

    import os
    upto = int(os.environ.get("FUSE_UPTO", "4"))
    nocc = os.environ.get("FUSE_NOCC", "0") == "1"
    if nocc:
        P.collective = lambda *a, **k: None

    def finish():
        dbg = os.environ.get("FUSE_DEBUG", "")
        if dbg:
            src = {"h1scr": h1scr, "sc1": sc1, "sc2": sc2, "sc3": sc3, "b1": b1, "b2": b2, "b3": b3, "dtap": dtap}[dbg]
            shp = list(src.ap().shape)
            n = 1
            for d_ in shp[:-1]:
                n *= d_
            dt_ = F32 if dbg in ("h1scr", "dtap") else BF16
            dbo = nc.dram_tensor("dbg", [n, shp[-1]], dt_, kind="ExternalOutput")
            P.barrier()
            bb = P.buf("dbg")
            names = "abcdefg"[:len(shp) - 1]
            view = src.ap() if len(shp) == 2 else src.ap().rearrange(" ".join(names) + " z -> (" + " ".join(names) + ") z")
            P.dma("sp", lambda e: e.dma_start(out=dbo.ap(), in_=view), writes=[bb])
            P.wait_all("sp", [bb])
        P.barrier()
        print("sems", len(P.sems), "instr", {e: len(q) for e, q in P.q.items()})
        P.emit()
        P.close()
        return nc
    P.phase_start()
    ioA["b1"] = b1.ap()
    ioA["after_seg"] = lambda si, ob: gather_dest(b1, g1, si, 4, ob)
    phase_ssd(P, ioA)
    P.collective_wait()
    if upto == 1:
        return finish()
    P.phase_start()
    ioB.update({"sc": sc1.ap(), "scb": pull(sc1, g1, regs["r1"], 16 * 128, C1), "h1scr": h1scr.ap(),
                "b2": b2.ap(), "b2h": b2})
    ob = phase_token(P, "B", ioB)
    gather_all(b2, g2, 4, 6, ob)
    if upto == 2:
        return finish()
    P.phase_start()
    ob = phase_attn(P, {"after_dest": lambda d_, ob_: gather_dest(b3, g3, d_, 2, ob_), "b3": b3.ap(), "sc": sc2.ap(), "sch": sc2, "scb": pull(sc2, g2, regs["r2q"], 24 * 128, TQ)})
    gather_dest(b3, g3, 3, 2, ob)
    P.collective_wait()
    if upto == 3:
        return finish()
    P.phase_start()
    ioD.update({"dtap": dtap.ap(), "sc": sc3.ap(), "scb": pull(sc3, g3, regs["r3"], 8 * 128, C3), "resid": h1scr.ap()})
    ob = phase_token(P, "D", ioD)
    P.wait_all("sp", ob)
    return finish()


_NC_CACHE = {}


def get_nc(key, fn, *a):
    if key not in _NC_CACHE:
        _NC_CACHE[key] = fn(*a)
    return _NC_CACHE[key]


def fm(v, n):
    return np.ascontiguousarray(np.asarray(v, np.float32).reshape(n, 128).T)


def _halo_cols(full, s, halo, W):
    out = np.zeros((full.shape[0], halo + W), full.dtype)
    lo = max(0, s - halo)
    out[:, lo - (s - halo):] = full[:, lo:s + W]
    return out


def _ffn_params(inp, layer, pfx):
    cwT = np.ascontiguousarray(np.asarray(inp["ffn_conv_w"][layer], np.float32).T.reshape(44, 128, 3)
                               .transpose(1, 0, 2).reshape(128, 132))
    return {pfx + "w_up": np.asarray(inp["ffn_w_up"][layer], np.float32),
            pfx + "w_down": np.asarray(inp["ffn_w_down"][layer], np.float32),
            pfx + "g_ffn": fm(inp["ffn_norm"][layer], 8), pfx + "cw": cwT, pfx + "cb": fm(inp["ffn_conv_b"][layer], 44)}


def kernel(**inp):
    inp = {k: np.asarray(v) for k, v in inp.items()}
    x = inp["x"].astype(np.float32)
    nb = x.shape[0]
    h0 = np.concatenate([np.broadcast_to(inp["meta_tokens"][None].astype(np.float32), (nb, 16, D)), x], axis=1)
    h0T = [np.ascontiguousarray(h0[b].T) for b in range(nb)]
    cores = list(range(8))
    nc = get_nc("F", build_fused)
    mA = ssd_maps(inp, h0)
    fB = _ffn_params(inp, 0, "B_")
    fD = _ffn_params(inp, 1, "D_")
    maps = []
    for c in cores:
        b, i = divmod(c, 4)
        m = {"A_" + k: v for k, v in mA[c].items()}
        m.update(fB)
        m.update(fD)
        m.update({"B_resid": _halo_cols(h0T[b], i * TQ, 4, TQ), "B_w_mix": np.ascontiguousarray(np.asarray(inp["ssd_w_out"][0], np.float32)
                                                   .reshape(4, 4, 128, D).transpose(1, 0, 2, 3).reshape(DI, D)),
                  "B_w_kv": np.asarray(inp["w_kv"], np.float32), "B_w_q": np.asarray(inp["sb_w_q"][0], np.float32),
                  "B_g_kv": fm(inp["kv_norm"], 8), "B_g_q": fm(inp["sb_norm"][0], 8),
                  "B_hmask": np.full((128, 1), 0.0 if i == 0 else 1.0, np.float32),
                  "D_w_mix": np.ascontiguousarray(np.asarray(inp["sb_w_o"][0], np.float32)
                                                   .reshape(4, 2, 128, D).transpose(1, 0, 2, 3).reshape(D, D)), "D_g_fin": fm(inp["final_norm"], 8),
                  "idx": np.array([[i]], np.int32)})
        maps.append(m)
    res = run_bass_kernel_spmd(nc, maps, core_ids=cores).results
    out = np.empty((nb, LB - 16, D), np.float32)
    for b in range(nb):
        full = np.concatenate([res[b * 4 + t]["out"] for t in range(4)], axis=1)
        out[b] = full[:, 16:].T
    return out


def ssd_maps(inp, h0):
    w_in = inp["ssd_w_in"][0]
    cwf = inp["ssd_conv_w"][0]
    cbf = inp["ssd_conv_b"][0]
    maps = []
    for c in range(8):
        b, g = divmod(c, 4)
        cols = np.concatenate([np.arange(512 * g, 512 * g + 512), 2048 + np.arange(512 * g, 512 * g + 512),
                               4096 + np.arange(128 * g, 128 * g + 128), 4608 + np.arange(128 * g, 128 * g + 128),
                               5120 + np.arange(8 * g, 8 * g + 8)])
        cch = np.concatenate([np.arange(512 * g, 512 * g + 512), 2048 + np.arange(128 * g, 128 * g + 128),
                              2560 + np.arange(128 * g, 128 * g + 128)])
        xpad = np.zeros((D, 3 + LB), np.float32)
        xpad[:, 3:] = h0[b].T
        xT = np.empty((4, 5, 8, 128, 411), np.float32)
        for si_ in range(4):
            for ti_ in range(5):
                c0_ = si_ * TQ + ti_ * 411
                xT[si_, ti_] = xpad[:, c0_:c0_ + 411].reshape(8, 128, 411)
        rep = lambda v: np.ascontiguousarray(np.broadcast_to(np.asarray(v, np.float32)[None, :], (128, len(v))))
        maps.append({
            "xT": xT, "wsel": np.ascontiguousarray(w_in[:, cols]), "g_in": fm(inp["ssd_norm"][0], 8),
            "cw": np.ascontiguousarray(cwf[:, cch].T.reshape(6, 128, 4).transpose(1, 0, 2).reshape(128, 24)),
            "cb": fm(cbf[cch], 6), "dtb": rep(inp["ssd_dt_bias"][0][8 * g:8 * g + 8]),
            "alog": rep(inp["ssd_a_log"][0][8 * g:8 * g + 8]), "dsk": rep(inp["ssd_d_skip"][0][8 * g:8 * g + 8]),
            "gg": rep(inp["ssd_gate_norm"][0][512 * g:512 * g + 512]),
        })
    return maps
```
